# Optimizing a Trainium2 kernel written in Bass

```python
import jax, jax.numpy as jnp
from jax import lax
import numpy as np

D_MODEL = 1024
BATCH = 32
SEQ = 256
DEPTH = 2
DEC_BATCH = 4
DEC_SEQ = 2048
PAST_LEN = 256

GRID_W = 64
MLA_HEADS = 8
Q_LORA = 256
KV_LORA = 128
QK_NOPE = 64
QK_ROPE = 32
V_HEAD = 64
QK_HEAD = QK_NOPE + QK_ROPE
MLA_W = MLA_HEADS * V_HEAD
CONV_W = 256
CONV_GROUPS = 4
RET_HEADS = 4
RET_DK = 64
RET_DV = 64
RET_W = RET_HEADS * RET_DV
RET_CHUNK = 128
MIX_W = MLA_W + CONV_W + RET_W
IN_COLS = Q_LORA + KV_LORA + QK_ROPE + 3 * CONV_W + 2 * RET_HEADS * RET_DK + 2 * RET_W
FFN_HIDDEN = ((8 * D_MODEL // 3 + 255) // 256) * 256
Q_BLOCK = 128
ROPE_THETA = 10000.0
EPS = 1e-6

kernel_name = "hybrid_mla_conv_retention_dit_step"


def _rmsnorm(x, g=None):
    x32 = x.astype(jnp.float32)
    y = (x32 * lax.rsqrt(jnp.mean(x32 * x32, axis=-1, keepdims=True) + EPS)).astype(x.dtype)
    return y if g is None else y * g


def _modulation(cond, ada_w, ada_b):
    m = jax.nn.silu(cond) @ ada_w + ada_b
    return [p[:, None, :] for p in jnp.split(m, 6, axis=-1)]


def _modulate(h, shift, scale):
    return h * (1.0 + scale) + shift


def _rope_2d(x):
    n = x.shape[1]
    rows = n // GRID_W
    row = jnp.repeat(jnp.arange(rows), GRID_W).astype(jnp.float32)
    col = jnp.tile(jnp.arange(GRID_W), rows).astype(jnp.float32)
    half = QK_ROPE // 2
    freqs = jnp.power(ROPE_THETA, -jnp.arange(0, half, 2, dtype=jnp.float32) / half)

    def rot(xa, pos):
        ang = pos[:, None] * freqs[None, :]
        cos = jnp.cos(ang)[None, :, None, :]
        sin = jnp.sin(ang)[None, :, None, :]
        x1, x2 = jnp.split(xa.astype(jnp.float32), 2, axis=-1)
        return jnp.concatenate([x1 * cos - x2 * sin, x1 * sin + x2 * cos], axis=-1)

    xr, xc = jnp.split(x, 2, axis=-1)
    return jnp.concatenate([rot(xr, row), rot(xc, col)], axis=-1).astype(x.dtype)


def _attention(q, k, v):
    b, h, tq, dq = q.shape
    nblk = tq // Q_BLOCK
    qb = jnp.moveaxis(q.reshape(b, h, nblk, Q_BLOCK, dq), 2, 0)
    scale = QK_HEAD ** -0.5

    def block(qi):
        s = jnp.einsum("bhqd,bhkd->bhqk", qi, k).astype(jnp.float32) * scale
        p = jax.nn.softmax(s, axis=-1).astype(v.dtype)
        return jnp.einsum("bhqk,bhkd->bhqd", p, v)

    o = lax.map(block, qb)
    return jnp.moveaxis(o, 0, 2).reshape(b, h, tq, V_HEAD)


def _mla_kv(c_kv, k_pe, w_kv_up, k_head_g):
    b, t, _ = c_kv.shape
    kv = (c_kv @ w_kv_up).reshape(b, t, MLA_HEADS, QK_NOPE + V_HEAD)
    k_nope, v = jnp.split(kv, [QK_NOPE], axis=-1)
    k = jnp.concatenate([k_nope, jnp.broadcast_to(k_pe[:, :, None, :], (b, t, MLA_HEADS, QK_ROPE))], axis=-1)
    return _rmsnorm(k, k_head_g), v


def _short_conv(gb, gc, xin, conv_w):
    u = gc * xin
    up = jnp.pad(u, ((0, 0), (1, 1), (0, 0)))
    y = up[:, :-2] * conv_w[0] + up[:, 1:-1] * conv_w[1] + up[:, 2:] * conv_w[2]
    return gb * y


def _retention_scan(q, k, v, log_gamma, s0):
    b, t, hh, _ = q.shape
    n = t // RET_CHUNK

    def chunks(a):
        a = a.astype(jnp.float32).reshape(b, n, RET_CHUNK, hh, a.shape[-1])
        return jnp.moveaxis(a, 1, 0).transpose(0, 1, 3, 2, 4)

    idx = jnp.arange(RET_CHUNK, dtype=jnp.float32)
    lg = log_gamma[:, None]
    diff = idx[:, None] - idx[None, :]
    decay = jnp.exp(jnp.where(diff >= 0, lg[:, :, None] * diff, -jnp.inf))
    q_dec = jnp.exp(lg * (idx + 1.0))[:, :, None]
    k_dec = jnp.exp(lg * (RET_CHUNK - 1.0 - idx))[:, :, None]
    c_dec = jnp.exp(lg * RET_CHUNK)[:, :, None]

    def step(s, inp):
        qi, ki, vi = inp
        scores = jnp.einsum("bhqd,bhkd->bhqk", qi, ki) * decay
        o = jnp.einsum("bhqk,bhkv->bhqv", scores, vi) + jnp.einsum("bhqd,bhdv->bhqv", qi * q_dec, s)
        s = s * c_dec + jnp.einsum("bhkd,bhkv->bhdv", ki * k_dec, vi)
        return s, o

    s_fin, o = lax.scan(step, s0.astype(jnp.float32), (chunks(q), chunks(k), chunks(v)))
    o = jnp.moveaxis(o.transpose(0, 1, 3, 2, 4), 0, 1).reshape(b, t, hh, RET_DV)
    return o, s_fin


def _retention(rq, rk, rv, rg, lg_f, lg_b, s0_f, s0_b):
    b, t, _ = rq.shape
    q = rq.reshape(b, t, RET_HEADS, RET_DK)
    k = rk.reshape(b, t, RET_HEADS, RET_DK) * (RET_DK ** -0.5)
    v = rv.reshape(b, t, RET_HEADS, RET_DV)
    o_f, s_f = _retention_scan(q, k, v, lg_f, s0_f)
    o_b, s_b = _retention_scan(q[:, ::-1], k[:, ::-1], v[:, ::-1], lg_b, s0_b)
    o = _rmsnorm(o_f + o_b[:, ::-1])
    return o.reshape(b, t, RET_W).astype(rg.dtype) * jax.nn.silu(rg), s_f, s_b


def _token_mixer(h, w_in, q_norm_g, kv_norm_g, w_q_up, w_kv_up, q_head_g, k_head_g,
                 conv_w, lg_f, lg_b, w_o, ctx_ckv=None, ctx_kpe=None, s0_f=None, s0_b=None):
    latent = ctx_ckv is not None
    b, t, _ = h.shape
    widths = [Q_LORA, KV_LORA, QK_ROPE, CONV_W, CONV_W, CONV_W,
              RET_HEADS * RET_DK, RET_HEADS * RET_DK, RET_W]
    splits = [int(s) for s in np.cumsum(widths)]
    q_lat, kv_lat, k_pe, gb, gc, xin, rq, rk, rv, rg = jnp.split(h @ w_in, splits, axis=-1)

    c_kv = _rmsnorm(kv_lat, kv_norm_g)
    q = (_rmsnorm(q_lat, q_norm_g) @ w_q_up).reshape(b, t, MLA_HEADS, QK_HEAD)
    q = _rmsnorm(q, q_head_g)
    k, v = _mla_kv(c_kv, k_pe, w_kv_up, k_head_g)
    if latent:
        q = jnp.concatenate([q[..., :QK_NOPE], _rope_2d(q[..., QK_NOPE:])], axis=-1)
        k = jnp.concatenate([k[..., :QK_NOPE], _rope_2d(k[..., QK_NOPE:])], axis=-1)
        k_c, v_c = _mla_kv(ctx_ckv, ctx_kpe, w_kv_up, k_head_g)
        k = jnp.concatenate([k, k_c], axis=1)
        v = jnp.concatenate([v, v_c], axis=1)
    else:
        s0_f = jnp.zeros((b, RET_HEADS, RET_DK, RET_DV), jnp.float32)
        s0_b = s0_f
    attn = _attention(q.transpose(0, 2, 1, 3), k.transpose(0, 2, 1, 3), v.transpose(0, 2, 1, 3))
    attn = attn.transpose(0, 2, 1, 3).reshape(b, t, MLA_W)

    conv = _short_conv(gb, gc, xin, conv_w)

    ret, s_f, s_b = _retention(rq, rk, rv, rg, lg_f, lg_b, s0_f, s0_b)

    out = jnp.concatenate([attn, conv, ret], axis=-1) @ w_o
    return out, c_kv, k_pe, s_f, s_b


def _swiglu(h, w_gate, w_up, w_down):
    return (jax.nn.silu(h @ w_gate) * (h @ w_up)) @ w_down


def setup_inputs(seed: int = 0) -> dict:
    key = jax.random.key(seed)
    ks = jax.random.split(key, 25)
    f32 = jnp.float32

    def nrm(k, shape, scale=1.0):
        return jax.random.normal(k, shape, f32) * scale

    def gain(k, shape):
        return 1.0 + 0.05 * jax.random.normal(k, shape, f32)

    p = 1.0 - jnp.power(2.0, -5.0 - jnp.arange(RET_HEADS, dtype=f32))
    decay_logit = jnp.log(p) - jnp.log1p(-p)
    return {
        "x_prompt": nrm(ks[0], (BATCH, SEQ, D_MODEL)),
        "x_sample": nrm(ks[1], (DEC_BATCH, DEC_SEQ, D_MODEL)),
        "cache_ckv": nrm(ks[2], (DEC_BATCH, DEPTH, PAST_LEN, KV_LORA)),
        "cache_kpe": nrm(ks[3], (DEC_BATCH, DEPTH, PAST_LEN, QK_ROPE)),
        "state_ret": nrm(ks[4], (DEC_BATCH, DEPTH, 2, RET_HEADS, RET_DK, RET_DV), 0.1),
        "c": nrm(ks[5], (DEC_BATCH, D_MODEL)),
        "c_ctx": nrm(ks[6], (D_MODEL,)),
        "ada_w": nrm(ks[7], (DEPTH, D_MODEL, 6 * D_MODEL), 0.3 * D_MODEL ** -0.5),
        "ada_b": nrm(ks[8], (DEPTH, 6 * D_MODEL), 0.01),
        "norm1_g": gain(ks[9], (DEPTH, D_MODEL)),
        "norm2_g": gain(ks[10], (DEPTH, D_MODEL)),
        "w_in": nrm(ks[11], (DEPTH, D_MODEL, IN_COLS), D_MODEL ** -0.5),
        "q_norm_g": gain(ks[12], (DEPTH, Q_LORA)),
        "kv_norm_g": gain(ks[13], (DEPTH, KV_LORA)),
        "w_q_up": nrm(ks[14], (DEPTH, Q_LORA, MLA_HEADS * QK_HEAD), Q_LORA ** -0.5),
        "w_kv_up": nrm(ks[15], (DEPTH, KV_LORA, MLA_HEADS * (QK_NOPE + V_HEAD)), KV_LORA ** -0.5),
        "q_head_norm_g": gain(ks[16], (DEPTH, QK_HEAD)),
        "k_head_norm_g": gain(ks[17], (DEPTH, QK_HEAD)),
        "conv_w": nrm(ks[18], (DEPTH, 3, CONV_W), 3 ** -0.5),
        "ret_decay_fwd": decay_logit[None, :] + nrm(ks[19], (DEPTH, RET_HEADS), 0.1),
        "ret_decay_bwd": decay_logit[None, :] + nrm(ks[20], (DEPTH, RET_HEADS), 0.1),
        "w_o": nrm(ks[21], (DEPTH, MIX_W, D_MODEL), MIX_W ** -0.5),
        "w_ffn_gate": nrm(ks[22], (DEPTH, D_MODEL, FFN_HIDDEN), D_MODEL ** -0.5),
        "w_ffn_up": nrm(ks[23], (DEPTH, D_MODEL, FFN_HIDDEN), D_MODEL ** -0.5),
        "w_ffn_down": nrm(ks[24], (DEPTH, FFN_HIDDEN, D_MODEL), FFN_HIDDEN ** -0.5),
    }


def reference(x_prompt, x_sample, cache_ckv, cache_kpe, state_ret, c, c_ctx,
              ada_w, ada_b, norm1_g, norm2_g, w_in, q_norm_g, kv_norm_g, w_q_up, w_kv_up,
              q_head_norm_g, k_head_norm_g, conv_w, ret_decay_fwd, ret_decay_bwd, w_o,
              w_ffn_gate, w_ffn_up, w_ffn_down):
    xp, xs = x_prompt, x_sample
    ckv_list, kpe_list, st_list = [], [], []
    for l in range(DEPTH):
        lw = (w_in[l], q_norm_g[l], kv_norm_g[l], w_q_up[l], w_kv_up[l],
              q_head_norm_g[l], k_head_norm_g[l], conv_w[l],
              jax.nn.log_sigmoid(ret_decay_fwd[l].astype(jnp.float32)),
              jax.nn.log_sigmoid(ret_decay_bwd[l].astype(jnp.float32)), w_o[l])
        ffn = (w_ffn_gate[l], w_ffn_up[l], w_ffn_down[l])

        sh1, sc1, g1, sh2, sc2, g2 = _modulation(c_ctx[None, :], ada_w[l], ada_b[l])
        mix, ckv, kpe, s_f, s_b = _token_mixer(_modulate(_rmsnorm(xp, norm1_g[l]), sh1, sc1), *lw)
        xp = xp + g1 * mix
        xp = xp + g2 * _swiglu(_modulate(_rmsnorm(xp, norm2_g[l]), sh2, sc2), *ffn)
        ckv_list.append(ckv)
        kpe_list.append(kpe)
        st_list.append(jnp.stack([s_f, s_b], axis=1).astype(xp.dtype))

        sh1, sc1, g1, sh2, sc2, g2 = _modulation(c, ada_w[l], ada_b[l])
        mix, _, _, _, _ = _token_mixer(_modulate(_rmsnorm(xs, norm1_g[l]), sh1, sc1), *lw,
                                       cache_ckv[:, l], cache_kpe[:, l],
                                       state_ret[:, l, 0], state_ret[:, l, 1])
        xs = xs + g1 * mix
        xs = xs + g2 * _swiglu(_modulate(_rmsnorm(xs, norm2_g[l]), sh2, sc2), *ffn)

    new_ckv = jnp.stack(ckv_list, axis=1)
    new_kpe = jnp.stack(kpe_list, axis=1)
    new_state_ret = jnp.stack(st_list, axis=1)
    return (xp, xs, new_ckv, new_kpe, new_state_ret)
```

```python
import os
import numpy as np
import concourse.bass as bass
import concourse.mybir as mybir
from concourse.bass_utils import run_bass_kernel_spmd

F32 = mybir.dt.float32
BF16 = mybir.dt.bfloat16
AF = mybir.ActivationFunctionType
ALU = mybir.AluOpType
AX = mybir.AxisListType

L = 2; D = 1024; T = 2048; NT = 16; NCTX = 256; NK = T + NCTX; NKT = 18
QL = 256; KVL = 128; ROPE = 32; NOPE = 64; QKH = 96; VH = 64; H = 8
CW = 256; RH = 4; CH = 128
FFN = 2816; NHC = 22; EPS = 1e-6; BIG = 30000.0
GRID_W = 64
NCORES = 8
C_QLAT = 0; C_KV = 256; C_KPE = 384; C_GB = 416; C_GC = 672; C_XIN = 928
C_RQ = 1184; C_RK = 1440; C_RV = 1696; C_RG = 1952


class _Stop(Exception):
    pass


class Region:
    __slots__ = ("name", "w", "r", "excl")

    def __init__(self, name, excl=False):
        self.name = name
        self.w = []
        self.r = []
        self.excl = excl


ATTACH_ENGINES = ("pe", "act", "dve", "pool")


class Sched:
    ENG = ("pe", "act", "dve", "pool", "sp")

    def __init__(self, nc, n_dma_sems=32):
        self.nc = nc
        self.e = {"pe": nc.tensor, "act": nc.scalar, "dve": nc.vector,
                  "pool": nc.gpsimd, "sp": nc.sync}
        self.sem = {k: nc.alloc_semaphore("sem_" + k) for k in self.ENG}
        self.cnt = {k: 0 for k in self.ENG}
        self.seen = {k: {} for k in self.ENG}
        self.hist = {}
        self.dma_sems = [nc.alloc_semaphore("dsem%d" % i) for i in range(n_dma_sems)]
        self.dma_cnt = [0] * n_dma_sems
        self.dma_last = [None] * n_dma_sems
        self.dma_rr = 0
        self.nwaits = 0
        self.ninstr = 0
        self.defer = None
        self.limit = int(os.environ.get("KSTOPN", "0"))

    def _need(self, eng, tok):
        key, sem, val = tok
        if key == eng and eng in ("pe", "sp", "pool"):
            return False
        return self.seen[eng].get(key, 0) < val

    def _wait(self, eng, tok):
        key, sem, val = tok
        if not self._need(eng, tok):
            return
        if self.defer is not None:
            self.defer.append((sem, val))
        else:
            self.e[eng].wait_ge(sem, val)
        self.nwaits += 1
        s = self.seen[eng]
        s[key] = val
        h = self.hist.get((key, val))
        if h:
            for k2, v2 in h.items():
                if s.get(k2, 0) < v2:
                    s[k2] = v2

    def _deps(self, eng, reads, writes):
        best = {}
        self_raw = None

        def upd(t):
            if t[0] not in best or best[t[0]][2] < t[2]:
                best[t[0]] = t

        for r in reads:
            for t in r.w:
                if t[0] == eng:
                    if self_raw is None or self_raw[2] < t[2]:
                        self_raw = t
                else:
                    upd(t)
            if r.excl:
                for t in r.r:
                    if t[0] != eng:
                        upd(t)
        for w in writes:
            for t in w.w + w.r:
                if t[0] != eng:
                    upd(t)
        if self_raw is not None:
            best[eng] = self_raw
        for t in best.values():
            self._wait(eng, t)

    def _record(self, tok, reads, writes):
        for r in reads:
            if r.excl:
                r.w = [tok]
                r.r = []
            else:
                r.r = [t for t in r.r if t[0] != tok[0]] + [tok]
        for w in writes:
            w.w = [tok]
            w.r = []

    def op(self, eng, fn, reads=(), writes=(), sig=True):
        reads = list(reads)
        writes = list(writes)
        if self.limit and self.ninstr >= self.limit:
            raise _Stop()
        self.defer = []
        self._deps(eng, reads, writes)
        pend = self.defer
        self.defer = None
        attach = None
        if pend and eng in ATTACH_ENGINES:
            attach = pend.pop()
        for (sem_, val_) in pend:
            self.e[eng].wait_ge(sem_, val_)
        ins = fn(self.e[eng])
        if attach is not None:
            ins._wait_ge(attach[0], attach[1])
        self.ninstr += 1
        if sig:
            self.cnt[eng] += 1
            ins.then_inc(self.sem[eng], 1)
            tok = (eng, self.sem[eng], self.cnt[eng])
            self.hist[(eng, self.cnt[eng])] = dict(self.seen[eng])
        else:
            tok = (eng, self.sem[eng], self.cnt[eng] + 1)
        self._record(tok, reads, writes)
        return ins

    def dma(self, eng, out, in_, reads=(), writes=(), **kw):
        reads = list(reads)
        writes = list(writes)
        if self.limit and self.ninstr >= self.limit:
            raise _Stop()
        slot = self.dma_rr
        self.dma_rr = (self.dma_rr + 1) % len(self.dma_sems)
        prev = self.dma_last[slot]
        self.defer = []
        self._deps(eng, reads, writes)
        if prev is not None:
            self._wait(eng, prev)
        pend = self.defer
        self.defer = None
        attach = pend.pop() if pend else None
        for (sem_, val_) in pend:
            self.e[eng].wait_ge(sem_, val_)
        ins = self.e[eng].dma_start(out=out, in_=in_, **kw)
        if attach is not None:
            ins._wait_ge(attach[0], attach[1])
        self.ninstr += 1
        self.dma_cnt[slot] += 16
        ins.then_inc(self.dma_sems[slot], 16)
        key = "d%d" % slot
        tok = (key, self.dma_sems[slot], self.dma_cnt[slot])
        self.hist[(key, self.dma_cnt[slot])] = dict(self.seen[eng])
        self.dma_last[slot] = tok
        self._record(tok, reads, writes)
        return tok

    def all_tokens(self):
        toks = []
        for k in self.ENG:
            if self.cnt[k] > 0:
                toks.append((k, self.sem[k], self.cnt[k]))
        for t in self.dma_last:
            if t is not None:
                toks.append(t)
        return toks

    def barrier(self, engines=None):
        toks = self.all_tokens()
        for e in (engines or self.ENG):
            for t in toks:
                if t[0] != e:
                    self._wait(e, t)

    def finish(self, eng="sp"):
        for t in self.all_tokens():
            if t[0] != eng:
                self._wait(eng, t)


def _prod(s):
    n = 1
    for v in s:
        n *= v
    return n


def build_program(dbg=None, stop=None):
    nc = bass.Bass("TRN2", target_bir_lowering=False)
    S = Sched(nc)
    dbg = dbg or set()
    dbg_outs = {}

    def din(name, shape):
        return nc.dram_tensor(name, list(shape), F32, kind="ExternalInput").ap()

    def dout(name, shape, dt=F32):
        return nc.dram_tensor(name, list(shape), dt, kind="ExternalOutput").ap()

    x_d = din("x", [T, D]); cond_d = din("cond_pc", [128, 8])
    cckv_d = din("cckv", [L, NCTX, KVL]); ckpe_d = din("ckpe", [L, NCTX, ROPE])
    s0_d = din("s0", [L, 2, RH, 64, 64])
    cos_d = din("cos", [NK, 16]); sin_d = din("sin", [NK, 16])
    qa_d = din("qa", [32, T]); ka_d = din("ka", [32, NK])
    nbL_d = din("nbL", [8]); nbR_d = din("nbR", [8])
    keepf_d = din("keepf", [NT]); keepb_d = din("keepb", [NT])
    cst_d = din("cst", [128, 3, 128])
    ada_w_d = din("ada_w", [L, D, 6 * D]); ada_b_d = din("ada_b", [L, 6 * D]); adab_pc_d = din("ada_b_pc", [L, 128, 48])
    n1g_d = din("n1g_pc", [L, 128, 8]); n2g_d = din("n2g_pc", [L, 128, 8])
    w_in_d = din("w_in", [L, D, 2208])
    qng_d = din("q_norm_g", [L, QL]); kvng_d = din("kv_norm_g", [L, KVL])
    wq_d = din("w_q_up", [L, QL, H * QKH]); wkv_d = din("w_kv_up", [L, KVL, H * 128])
    qhg_d = din("q_head_norm_g", [L, QKH]); khg_d = din("k_head_norm_g", [L, QKH])
    cw_d = din("cw_pc", [L, 128, 2, 3])
    rdf_d = din("ret_decay_fwd", [L, RH]); rdb_d = din("ret_decay_bwd", [L, RH])
    wo_d = din("w_o", [L, D, D])
    wg_d = din("w_ffn_gate", [L, D, FFN]); wu_d = din("w_ffn_up", [L, D, FFN])
    wd_d = din("w_ffn_down", [L, FFN, D])

    y_d = dout("y", [T, D]); ockv_d = dout("ockv", [L, T, KVL]); okpe_d = dout("okpe", [L, T, ROPE])
    ost_d = dout("ost", [L, 2, 8, RH, 64, 64])

    BIGW = 51200
    big = nc.alloc_sbuf_tensor("big", [128, BIGW], F32)
    ps_all = nc.alloc_psum_tensor("ps_all", [128, 4096], F32)
    PB = [ps_all[:, i * 512:(i + 1) * 512] for i in range(8)]
    PBb = [ps_all[:, i * 512:(i + 1) * 512].bitcast(BF16) for i in range(8)]
    RB = [Region("bank%d" % i, excl=True) for i in range(8)]
    bank_rr = [0]

    def nb():
        b = bank_rr[0]
        bank_rr[0] = (b + 1) % 8
        return b

    def view(off, shape, dt=F32):
        assert off % 4 == 0 and shape[0] == 128
        n = _prod(shape[1:])
        esz = 4 if dt == F32 else 2
        words = (n * esz + 3) // 4
        assert off // 4 + words <= BIGW, (off, shape)
        ap = big[:, off // 4: off // 4 + words]
        if dt != F32:
            ap = ap.bitcast(dt)
            if ap.shape[1] != n:
                ap = ap[:, 0:n]
        if len(shape) == 3:
            ap = ap.rearrange("p (a b) -> p a b", a=shape[1])
        elif len(shape) == 4:
            ap = ap.rearrange("p (a b c) -> p a b c", a=shape[1], b=shape[2])
        elif len(shape) == 5:
            ap = ap.rearrange("p (a b c d) -> p a b c d", a=shape[1], b=shape[2], c=shape[3])
        return ap

    KB = 1024
    X_OFF = 0
    C_OFF = 64 * KB
    P_OFF = 88 * KB

    class Alloc:
        def __init__(self, base, limit):
            self.base = base; self.cur = base; self.limit = limit

        def __call__(self, shape, dt=F32):
            n = _prod(shape[1:]) * (4 if dt == F32 else 2)
            n = (n + 31) // 32 * 32
            off = self.cur
            self.cur += n
            assert self.cur <= self.limit, ("arena overflow", self.cur - self.limit)
            return view(off, shape, dt)

        def at(self, off_kb):
            self.cur = self.base + int(off_kb * KB)

    CA = Alloc(C_OFF, P_OFF)
    PA = Alloc(P_OFF, BIGW * 4)

    def act(out, in_, func, reads, writes, **kw):
        return S.op("act", lambda e: e.activation(out=out, in_=in_, func=func, **kw), reads, writes)

    def tt(eng, out, in0, in1, op, reads, writes):
        return S.op(eng, lambda e: e.tensor_tensor(out=out, in0=in0, in1=in1, op=op), reads, writes)

    def ts(eng, out, in0, s1, op0, reads, writes, s2=None, op1=None):
        if op1 is None:
            return S.op(eng, lambda e: e.tensor_scalar(out=out, in0=in0, scalar1=s1, scalar2=None, op0=op0), reads, writes)
        return S.op(eng, lambda e: e.tensor_scalar(out=out, in0=in0, scalar1=s1, scalar2=s2, op0=op0, op1=op1), reads, writes)

    def stt(eng, out, in0, scalar, in1, op0, op1, reads, writes):
        return S.op(eng, lambda e: e.scalar_tensor_tensor(out=out, in0=in0, scalar=scalar, in1=in1, op0=op0, op1=op1), reads, writes)

    def cp(eng, out, in_, reads, writes):
        if eng == "act":
            return S.op("act", lambda e: e.copy(out=out, in_=in_), reads, writes)
        return S.op(eng, lambda e: e.tensor_copy(out=out, in_=in_), reads, writes)

    def mm(out, lhsT, rhs, start, stop, reads, writes, sig=True):
        return S.op("pe", lambda e: e.matmul(out, lhsT=lhsT, rhs=rhs, start=start, stop=stop), reads, writes, sig=sig)

    def chk(tag):
        if stop == tag:
            raise _Stop()

    def dump(name, ap, reads, dt=F32):
        if name not in dbg:
            return
        d = dout("dbg_" + name, list(ap.shape), dt)
        dbg_outs[name] = d
        S.dma("sp", d, ap, reads=reads)

    def rstd_of(ss_ap, n, r_ss, eps_ap):
        act(ss_ap, ss_ap, AF.Ln, [r_ss], [r_ss], scale=1.0 / n, bias=eps_ap)
        act(ss_ap, ss_ap, AF.Exp, [r_ss], [r_ss], scale=-0.5)

    X = view(X_OFF, [128, NT, D]); RX = [Region("x%d" % t) for t in range(NT)]
    ident = CA([128, 128], BF16); r_ident = Region("ident")
    cst = CA([128, 3, 128]); r_cst = Region("cst")
    epsc = CA([128, 1]); r_eps = Region("eps")
    gates = CA([128, 2, D]); r_gates = [Region("g1"), Region("g2")]
    colmod = CA([128, 48]); r_colmod = Region("colmod")
    adab_col = CA([128, 48]); r_adabc = Region("adabc")
    ngc = CA([128, 2, 8]); r_ngc = Region("ngc")
    GS = CA([128, 4, 8]); r_GS = Region("GS")
    cos_sb = CA([128, NKT, 16]); sin_sb = CA([128, NKT, 16]); r_trig = Region("trig")
    qgb = CA([128, QL]); kvgb = CA([128, KVL]); qhgb = CA([128, QKH]); khgb = CA([128, QKH]); r_gb = Region("gb")
    cwc = CA([128, 2, 3]); r_cwc = Region("cwc")
    condc = CA([128, 8]); r_cond = Region("cond")
    sc_col = CA([128, 8], BF16); r_sccol = Region("sccol")
    lgall = CA([128, 8]); r_lg = Region("lg")
    lgcol = CA([128, 2, 2]); r_lgcol = Region("lgcol")
    Mtab = CA([128, RH, 128]); r_M = Region("M")
    qdtab = CA([128, 2, 2, 128]); r_qd = Region("qdtab")
    kdtab = CA([128, 2, RH]); r_kd = Region("kdtab")
    cdcol = CA([128, 2, 2]); r_cd = Region("cdcol")
    keepfb = CA([128, 2, NT]); r_keep = Region("keep")
    aktab = CA([128, 2, 2, NT]); r_ak = Region("ak")
    smallscr = CA([128, 64]); r_scr = Region("scr")
    tri = CA([128, 2, 128]); r_tri = Region("tri")
    nbt = CA([128, 2, 8]); r_nbt = Region("nbt")
    print("const arena used", CA.cur - C_OFF, "of", P_OFF - C_OFF)

    xv = x_d.rearrange("(t p) d -> p t d", p=128)
    S.op("pool", lambda e: e.memset(ident, 0.0), writes=[r_ident])
    S.op("pool", lambda e: e.affine_select(out=ident, in_=ident, pattern=[[-1, 128]], compare_op=ALU.not_equal,
                                           fill=1.0, base=0, channel_multiplier=1), reads=[r_ident], writes=[r_ident])
    S.op("dve", lambda e: e.memset(epsc, EPS), writes=[r_eps])
    S.dma("act", cst, cst_d, writes=[r_cst])
    S.dma("act", condc, cond_d, writes=[r_cond])
    S.dma("act", cos_sb, cos_d.rearrange("(t p) f -> p t f", p=128), writes=[r_trig])
    S.dma("act", sin_sb, sin_d.rearrange("(t p) f -> p t f", p=128), writes=[r_trig])
    S.dma("act", nbt[:, 0, :], nbL_d.partition_broadcast(128), writes=[r_nbt])
    S.dma("act", nbt[:, 1, :], nbR_d.partition_broadcast(128), writes=[r_nbt])
    S.dma("act", keepfb[:, 0, :], keepf_d.partition_broadcast(128), writes=[r_keep])
    S.dma("act", keepfb[:, 1, :], keepb_d.partition_broadcast(128), writes=[r_keep])
    act(sc_col, condc, AF.Silu, [r_cond], [r_sccol])
    ts("dve", tri[:, 0, :], cst[:, 0, :], 0.0, ALU.is_ge, [r_cst], [r_tri])
    ts("dve", tri[:, 1, :], cst[:, 0, :], 0.0, ALU.is_le, [r_cst], [r_tri])

    try:
      for l in range(L):
          S.barrier()
          PA.at(0)
          sc_b = PA([128, 8, 128], BF16); r_scb = Region("scb")
          NAP = 12
          apiece = [PA([128, 8, 512], BF16) for _ in range(NAP)]; r_ap = [Region("ap%d" % i) for i in range(NAP)]
          abp = [PA([128, 512]) for _ in range(2)]; r_abp = [Region("abp0"), Region("abp1")]
          cp("dve", sc_b, sc_col.unsqueeze(2).to_broadcast([128, 8, 128]), [r_sccol], [r_scb])
          S.dma("act", adab_col, adab_pc_d[l], writes=[r_adabc])
          S.dma("act", ngc[:, 0, :], n1g_d[l], writes=[r_ngc])
          S.dma("act", ngc[:, 1, :], n2g_d[l], writes=[r_ngc])
          S.dma("act", qgb, qng_d[l].partition_broadcast(128), writes=[r_gb])
          S.dma("act", kvgb, kvng_d[l].partition_broadcast(128), writes=[r_gb])
          S.dma("act", qhgb, qhg_d[l].partition_broadcast(128), writes=[r_gb])
          S.dma("act", khgb, khg_d[l].partition_broadcast(128), writes=[r_gb])
          S.dma("act", cwc, cw_d[l], writes=[r_cwc])
          S.dma("act", lgall[:, 0:4], rdf_d[l].partition_broadcast(128), writes=[r_lg])
          S.dma("act", lgall[:, 4:8], rdb_d[l].partition_broadcast(128), writes=[r_lg])
          adav = ada_w_d[l].rearrange("(c p) n -> p c n", p=128)
          colbank = nb()
          gi = 0
          for pc in range(NAP):
              S.dma("pool", apiece[pc], adav[:, :, pc * 512:(pc + 1) * 512], writes=[r_ap[pc]])
          if l == 0:
              for t in range(NT):
                  S.dma("sp", X[:, t, :], xv[:, t, :], reads=[r_ap[5]], writes=[RX[t]])
          for pc in range(12):
              bi = pc % NAP
              if pc >= NAP:
                  S.dma("pool", apiece[bi], adav[:, :, pc * 512:(pc + 1) * 512], writes=[r_ap[bi]])
              if pc in (4, 5, 10, 11):
                  g = 0 if pc < 6 else 1
                  half = pc % 2 if pc < 6 else (pc - 10)
                  S.dma("act", abp[gi % 2], ada_b_d[l, pc * 512:(pc + 1) * 512].partition_broadcast(128), writes=[r_abp[gi % 2]])
                  b = nb()
                  if b == colbank:
                      b = nb()
                  for k in range(8):
                      mm(PB[b], sc_b[:, k, :], apiece[bi][:, k, :], k == 0, k == 7, [r_scb, r_ap[bi]], [RB[b]], sig=(k == 7))
                  tt("dve", gates[:, g, half * 512:(half + 1) * 512], PB[b], abp[gi % 2], ALU.add,
                     [RB[b], r_abp[gi % 2]], [r_gates[g]])
                  gi += 1
              else:
                  for j in range(4):
                      col = pc * 4 + j
                      for k in range(8):
                          mm(PB[colbank][:, col:col + 1], apiece[bi][:, k, j * 128:(j + 1) * 128], sc_col[:, k:k + 1],
                             k == 0, k == 7, [r_sccol, r_ap[bi]], [RB[colbank]], sig=(k == 7))
          tt("dve", colmod, PB[colbank][:, 0:48], adab_col, ALU.add, [RB[colbank], r_adabc], [r_colmod])
          stt("dve", GS[:, 0, :], colmod[:, 8:16], 1.0, ngc[:, 0, :], ALU.add, ALU.mult, [r_colmod, r_ngc], [r_GS])
          cp("dve", GS[:, 1, :], colmod[:, 0:8], [r_colmod], [r_GS])
          stt("dve", GS[:, 2, :], colmod[:, 32:40], 1.0, ngc[:, 1, :], ALU.add, ALU.mult, [r_colmod, r_ngc], [r_GS])
          cp("dve", GS[:, 3, :], colmod[:, 24:32], [r_colmod], [r_GS])
          dump("GS%d" % l, GS, [r_GS]); dump("gates%d" % l, gates, r_gates)

          act(lgall, lgall, AF.Exp, [r_lg], [r_lg], scale=-1.0)
          act(lgall, lgall, AF.Ln, [r_lg], [r_lg], bias=1.0)
          ts("dve", lgall, lgall, -1.0, ALU.mult, [r_lg], [r_lg])
          for d_ in range(2):
              for pr in range(2):
                  for hh in range(2):
                      h_ = 2 * pr + hh
                      cp("dve", lgcol[hh * 64:(hh + 1) * 64, d_, pr:pr + 1], lgall[hh * 64:(hh + 1) * 64, d_ * 4 + h_:d_ * 4 + h_ + 1],
                         [r_lg], [r_lgcol])
          dpos = smallscr
          tmpA = PA([128, 128]); tmpB = PA([128, 128]); tmpC = PA([128, 128]); r_tmp = Region("tmpM")
          ts("dve", tmpA, cst[:, 0, :], 0.0, ALU.max, [r_cst], [r_tmp])
          ts("dve", tmpB, cst[:, 0, :], -1.0, ALU.mult, [r_cst], [r_tmp], s2=0.0, op1=ALU.max)
          for h_ in range(RH):
              act(tmpC, tmpA, AF.Exp, [r_tmp, r_lg], [r_tmp], scale=lgall[:, h_:h_ + 1])
              tt("dve", Mtab[:, h_, :], tmpC, tri[:, 0, :], ALU.mult, [r_tmp, r_tri], [r_M])
              act(tmpC, tmpB, AF.Exp, [r_tmp, r_lg], [r_tmp], scale=lgall[:, 4 + h_:5 + h_])
              tt("dve", tmpC, tmpC, tri[:, 1, :], ALU.mult, [r_tmp, r_tri], [r_tmp])
              tt("dve", Mtab[:, h_, :], Mtab[:, h_, :], tmpC, ALU.add, [r_tmp, r_M], [r_M])
          for pr in range(2):
              act(qdtab[:, 0, pr, :], cst[:, 1, :], AF.Exp, [r_cst, r_lgcol], [r_qd], scale=lgcol[:, 0, pr:pr + 1])
              act(qdtab[:, 1, pr, :], cst[:, 2, :], AF.Exp, [r_cst, r_lgcol], [r_qd], scale=lgcol[:, 1, pr:pr + 1])
          ts("dve", smallscr[:, 0:1], cst[:, 0, 0:1], -1.0, ALU.mult, [r_cst], [r_scr])
          act(kdtab[:, 0, :], lgall[:, 0:4], AF.Exp, [r_lg, r_cst], [r_kd], scale=cst[:, 0, 127:128])
          act(kdtab[:, 1, :], lgall[:, 4:8], AF.Exp, [r_lg, r_scr], [r_kd], scale=smallscr[:, 0:1])
          act(cdcol.rearrange("p a b -> p (a b)"), lgcol.rearrange("p a b -> p (a b)"), AF.Exp, [r_lgcol], [r_cd], scale=128.0)
          for d_ in range(2):
              for pr in range(2):
                  ts("dve", aktab[:, d_, pr, :], keepfb[:, d_, :], cdcol[:, d_, pr:pr + 1], ALU.mult, [r_keep, r_cd], [r_ak])
          dump("Mtab%d" % l, Mtab, [r_M]); dump("qdtab%d" % l, qdtab, [r_qd]); dump("kdtab%d" % l, kdtab, [r_kd])
          chk("mod%d" % l)

          norm_ctr = [0]

          def norm_phase(gidx, tiles, hT, r_hT, xn, r_xn, junk, r_junk, ssb, r_ss):
              ng = len(tiles) // 4
              for g in range(ng):
                  tl = tiles[g * 4:(g + 1) * 4]
                  for i, t in enumerate(tl):
                      act(junk, X[:, t, :], AF.Square, [RX[t]], [r_junk, r_ss], accum_out=ssb[:, g * 4 + i:g * 4 + i + 1])
                  rstd_of(ssb[:, g * 4:(g + 1) * 4], D, r_ss, epsc)
                  b0 = 4 * (norm_ctr[0] % 2)
                  norm_ctr[0] += 1
                  for i, t in enumerate(tl):
                      ts("dve", xn[i], X[:, t, :], ssb[:, g * 4 + i:g * 4 + i + 1], ALU.mult, [RX[t], r_ss], [r_xn[i]])
                      for c in range(8):
                          bk = b0 + c // 2
                          o = PBb[bk][:, (c % 2) * 512 + i * 128:(c % 2) * 512 + (i + 1) * 128]
                          S.op("pe", lambda e, o=o, c=c, i=i: e.transpose(o, xn[i][:, c * 128:(c + 1) * 128], ident),
                               [r_xn[i], r_ident], [RB[bk]], sig=(c % 2 == 1))
                  for c in range(8):
                      bk = b0 + c // 2
                      src = PBb[bk][:, (c % 2) * 512:(c % 2 + 1) * 512]
                      dst = hT[:, c, g * 512:(g + 1) * 512]
                      if c % 2 == 0:
                          act(dst, src, AF.Identity, [RB[bk], r_GS], [r_hT[g]], scale=GS[:, gidx, c:c + 1], bias=GS[:, gidx + 1, c:c + 1])
                      else:
                          ts("dve", dst, src, GS[:, gidx, c:c + 1], ALU.mult, [RB[bk], r_GS], [r_hT[g]],
                             s2=GS[:, gidx + 1, c:c + 1], op1=ALU.add)

          S.barrier()
          PA.at(0)
          hT = PA([128, 8, T], BF16); r_hT = [Region("hT%d" % g) for g in range(4)]
          qlnT = PA([128, 2, T], BF16); r_qlnT = [Region("qlnT%d" % t) for t in range(NT)]
          ckvT = PA([128, NK], BF16); r_ckvT = [Region("ckvT%d" % t) for t in range(NKT)]
          kpe_sb = PA([128, NKT, ROPE]); r_kpe = [Region("kpe%d" % t) for t in range(NKT)]
          assert PA.cur - P_OFF <= 48 * KB, PA.cur - P_OFF
          PA.at(48)
          mixT = PA([128, 8, T], BF16)
          r_attnT = [[Region("attnT%d_%d" % (c, b)) for b in range(4)] for c in range(4)]
          r_convT = [Region("convT0"), Region("convT1")]
          r_retT = [[Region("retT%d_%d" % (c, t)) for t in range(NT)] for c in range(2)]
          PA.at(80)
          xn = [PA([128, D], BF16) for _ in range(4)]; r_xn = [Region("xn%d" % i) for i in range(4)]
          junk = PA([128, D], BF16); r_junk = Region("junk")
          ssb = PA([128, NT]); r_ss = Region("ss")
          wA = PA([128, 8, 416], BF16); r_wA = Region("wA")
          w_inv = w_in_d[l].rearrange("(c p) n -> p c n", p=128)
          S.dma("pool", wA, w_inv[:, :, 0:416], writes=[r_wA])
          norm_phase(0, list(range(NT)), hT, r_hT, xn, r_xn, junk, r_junk, ssb, r_ss)
          dump("hT%d" % l, hT, r_hT, BF16)
          chk("norm1%d" % l)

          qn = [PA([128, QL], BF16) for _ in range(2)]; r_qn = [Region("qn0"), Region("qn1")]
          ckvf = [PA([128, KVL]) for _ in range(2)]; r_ckvf = [Region("ckvf0"), Region("ckvf1")]
          ckvb = [PA([128, KVL], BF16) for _ in range(2)]; r_ckvb = [Region("ckvb0"), Region("ckvb1")]
          ss2 = PA([128, NT, 2]); r_ss2 = Region("ss2")
          cctx = PA([128, 2, KVL]); r_cctx = Region("cctx")
          S.dma("act", cctx, cckv_d[l].rearrange("(t p) f -> p t f", p=128), writes=[r_cctx])
          S.dma("act", kpe_sb[:, NT:NKT, :], ckpe_d[l].rearrange("(t p) f -> p t f", p=128), writes=[r_kpe[16], r_kpe[17]])
          ockv_v = ockv_d[l].rearrange("(t p) f -> p t f", p=128)
          for t in range(NT):
              b = nb()
              for k in range(8):
                  mm(PB[b][:, 0:416], hT[:, k, t * 128:(t + 1) * 128], wA[:, k, :], k == 0, k == 7,
                     [r_hT[t // 4], r_wA], [RB[b]], sig=(k == 7))
              i2 = t % 2
              act(junk[:, 0:QL], PB[b][:, 0:QL], AF.Square, [RB[b]], [r_junk, r_ss2], accum_out=ss2[:, t, 0:1])
              act(junk[:, 0:KVL], PB[b][:, C_KV:C_KV + KVL], AF.Square, [RB[b]], [r_junk, r_ss2], accum_out=ss2[:, t, 1:2])
              act(ss2[:, t, 0:1], ss2[:, t, 0:1], AF.Ln, [r_ss2], [r_ss2], scale=1.0 / QL, bias=epsc)
              act(ss2[:, t, 1:2], ss2[:, t, 1:2], AF.Ln, [r_ss2], [r_ss2], scale=1.0 / KVL, bias=epsc)
              act(ss2[:, t, :], ss2[:, t, :], AF.Exp, [r_ss2], [r_ss2], scale=-0.5)
              stt("dve", qn[i2], PB[b][:, 0:QL], ss2[:, t, 0:1], qgb, ALU.mult, ALU.mult, [RB[b], r_ss2, r_gb], [r_qn[i2]])
              stt("dve", ckvf[i2], PB[b][:, C_KV:C_KV + KVL], ss2[:, t, 1:2], kvgb, ALU.mult, ALU.mult, [RB[b], r_ss2, r_gb], [r_ckvf[i2]])
              cp("act", kpe_sb[:, t, :], PB[b][:, C_KPE:C_KPE + ROPE], [RB[b]], [r_kpe[t]])
              cp("dve", ckvb[i2], ckvf[i2], [r_ckvf[i2]], [r_ckvb[i2]])
              S.dma("sp", ockv_v[:, t, :], ckvf[i2], reads=[r_ckvf[i2]])
              b2 = nb()
              for c in range(2):
                  S.op("pe", lambda e, c=c: e.transpose(PBb[b2][:, c * 128:(c + 1) * 128], qn[i2][:, c * 128:(c + 1) * 128], ident),
                       [r_qn[i2], r_ident], [RB[b2]], sig=False)
              S.op("pe", lambda e: e.transpose(PBb[b2][:, 256:384], ckvb[i2], ident), [r_ckvb[i2], r_ident], [RB[b2]])
              cp("act", qlnT[:, :, t * 128:(t + 1) * 128], PBb[b2][:, 0:256].rearrange("p (c n) -> p c n", c=2), [RB[b2]], [r_qlnT[t]])
              cp("dve", ckvT[:, t * 128:(t + 1) * 128], PBb[b2][:, 256:384], [RB[b2]], [r_ckvT[t]])
          S.dma("sp", okpe_d[l].rearrange("(t p) f -> p t f", p=128), kpe_sb[:, 0:NT, :], reads=r_kpe[0:NT])
          for j in range(2):
              i2 = j
              cp("dve", ckvb[i2], cctx[:, j, :], [r_cctx], [r_ckvb[i2]])
              b2 = nb()
              S.op("pe", lambda e: e.transpose(PBb[b2][:, 0:128], ckvb[i2], ident), [r_ckvb[i2], r_ident], [RB[b2]])
              cp("dve", ckvT[:, (NT + j) * 128:(NT + j + 1) * 128], PBb[b2][:, 0:128], [RB[b2]], [r_ckvT[NT + j]])
          dump("qlnT%d" % l, qlnT, r_qlnT, BF16); dump("ckvT%d" % l, ckvT, r_ckvT, BF16)
          chk("projA%d" % l)

          S.barrier()
          PA.at(48)
          gbs = PA([128, T]); r_gbs = Region("gbs")
          acc = PA([128, T]); r_acc = Region("acc")
          PA.at(80)
          wC = PA([128, 8, 3, 128], BF16); r_wC = Region("wC")
          u_h = [PA([128, T + 2]) for _ in range(2)]; r_u = [Region("u0"), Region("u1")]
          gcs = [PA([128, 512]) for _ in range(2)]; r_gcs = [Region("gcs0"), Region("gcs1")]
          ctmp = PA([128, 2, 8]); r_ctmp = Region("ctmp")
          for c in range(2):
              S.op("pool", lambda e, c=c: e.memset(u_h[c][:, 0:1], 0.0), writes=[r_u[c]])
              S.op("pool", lambda e, c=c: e.memset(u_h[c][:, T + 1:T + 2], 0.0), writes=[r_u[c]])
          cb = 0
          for c in range(2):
              uh = u_h[c]
              for a in range(3):
                  S.dma("pool", wC[:, :, a, :], w_inv[:, :, C_GB + a * 256 + c * 128:C_GB + a * 256 + (c + 1) * 128], writes=[r_wC])
              for tb in range(4):
                  bks = [(cb * 3 + a) % 6 for a in range(3)]
                  cb += 1
                  for a in (1, 2, 0):
                      bk = bks[a]
                      for k in range(8):
                          mm(PB[bk], wC[:, k, a, :], hT[:, k, tb * 512:(tb + 1) * 512], k == 0, k == 7,
                             [r_wC, r_hT[tb]], [RB[bk]], sig=(k == 7))
                  bg, bc, bx = bks
                  cp("act", gcs[tb % 2], PB[bc], [RB[bc]], [r_gcs[tb % 2]])
                  tt("dve", uh[:, 1 + tb * 512:1 + (tb + 1) * 512], PB[bx], gcs[tb % 2], ALU.mult, [RB[bx], r_gcs[tb % 2]], [r_u[c]])
                  cp("act", gbs[:, tb * 512:(tb + 1) * 512], PB[bg], [RB[bg]], [r_gbs])
              ts("dve", acc, uh[:, 0:T], cwc[:, c, 0:1], ALU.mult, [r_u[c], r_cwc], [r_acc])
              stt("dve", acc, uh[:, 1:T + 1], cwc[:, c, 1:2], acc, ALU.mult, ALU.add, [r_u[c], r_cwc, r_acc], [r_acc])
              stt("dve", acc, uh[:, 2:T + 2], cwc[:, c, 2:3], acc, ALU.mult, ALU.add, [r_u[c], r_cwc, r_acc], [r_acc])
              accv = acc.rearrange("p (k c) -> p k c", c=256)
              uL = uh[:, 0:T].rearrange("p (k c) -> p k c", c=256)[:, :, 0]
              uR = uh[:, 2:T + 2].rearrange("p (k c) -> p k c", c=256)[:, :, 255]
              tt("dve", ctmp[:, 0, :], uL, nbt[:, 0, :], ALU.mult, [r_u[c], r_nbt], [r_ctmp])
              tt("dve", ctmp[:, 1, :], uR, nbt[:, 1, :], ALU.mult, [r_u[c], r_nbt], [r_ctmp])
              stt("dve", accv[:, :, 0], ctmp[:, 0, :], cwc[:, c, 0:1], accv[:, :, 0], ALU.mult, ALU.add, [r_ctmp, r_cwc, r_acc], [r_acc])
              stt("dve", accv[:, :, 255], ctmp[:, 1, :], cwc[:, c, 2:3], accv[:, :, 255], ALU.mult, ALU.add, [r_ctmp, r_cwc, r_acc], [r_acc])
              tt("dve", mixT[:, 4 + c, :], acc, gbs, ALU.mult, [r_acc, r_gbs], [r_convT[c]])
          dump("convT%d" % l, mixT[:, 4:6, :], r_convT, BF16)
          chk("conv%d" % l)

          S.barrier()
          PA.at(48)
          rqT = PA([128, 2, T], BF16); rkT = PA([128, 2, T], BF16)
          r_rqT = [[Region("rqT%d_%d" % (c, b)) for b in range(4)] for c in range(2)]
          r_rkT = [[Region("rkT%d_%d" % (c, b)) for b in range(4)] for c in range(2)]
          PA.at(80)
          rk_tok = PA([128, NT, 256], BF16); rv_tok = PA([128, NT, 256], BF16)
          r_rktok = [Region("rktok%d" % t) for t in range(NT)]; r_rvtok = [Region("rvtok%d" % t) for t in range(NT)]
          wR = [PA([128, 8, 512], BF16) for _ in range(2)]; r_wR = [Region("wR0"), Region("wR1")]
          S.dma("pool", wR[0], w_inv[:, :, C_RQ:C_RQ + 512], writes=[r_wR[0]])
          S.dma("pool", wR[1], w_inv[:, :, C_RK:C_RK + 512], writes=[r_wR[1]])
          bi = 0
          for cc in range(4):
              for tb in range(4):
                  bk = bi % 8; bi += 1
                  for k in range(8):
                      mm(PB[bk], wR[0][:, k, cc * 128:(cc + 1) * 128], hT[:, k, tb * 512:(tb + 1) * 512], k == 0, k == 7,
                         [r_wR[0], r_hT[tb]], [RB[bk]], sig=(k == 7))
                  if cc < 2:
                      cp("act" if tb % 2 == 0 else "dve", rqT[:, cc, tb * 512:(tb + 1) * 512], PB[bk], [RB[bk]], [r_rqT[cc][tb]])
                  else:
                      if tb % 2 == 0:
                          act(rkT[:, cc - 2, tb * 512:(tb + 1) * 512], PB[bk], AF.Copy, [RB[bk]], [r_rkT[cc - 2][tb]], scale=0.125)
                      else:
                          ts("dve", rkT[:, cc - 2, tb * 512:(tb + 1) * 512], PB[bk], 0.125, ALU.mult, [RB[bk]], [r_rkT[cc - 2][tb]])
          for t in range(NT):
              bk = bi % 8; bi += 1
              for k in range(8):
                  mm(PB[bk], hT[:, k, t * 128:(t + 1) * 128], wR[1][:, k, :], k == 0, k == 7,
                     [r_wR[1], r_hT[t // 4]], [RB[bk]], sig=(k == 7))
              act(rk_tok[:, t, :], PB[bk][:, 0:256], AF.Copy, [RB[bk]], [r_rktok[t]], scale=0.125)
              cp("dve", rv_tok[:, t, :], PB[bk][:, 256:512], [RB[bk]], [r_rvtok[t]])
          S.dma("pool", wR[0][:, :, 0:256], w_inv[:, :, C_RG:C_RG + 256], writes=[r_wR[0]])
          for cc in range(2):
              for tb in range(4):
                  bk = bi % 8; bi += 1
                  for k in range(8):
                      mm(PB[bk], wR[0][:, k, cc * 128:(cc + 1) * 128], hT[:, k, tb * 512:(tb + 1) * 512], k == 0, k == 7,
                         [r_wR[0], r_hT[tb]], [RB[bk]], sig=(k == 7))
                  act(mixT[:, 6 + cc, tb * 512:(tb + 1) * 512], PB[bk], AF.Silu, [RB[bk]], r_retT[cc][tb * 4:(tb + 1) * 4])
          dump("rqT%d" % l, rqT, r_rqT[0] + r_rqT[1], BF16); dump("rkT%d" % l, rkT, r_rkT[0] + r_rkT[1], BF16)
          dump("rv_tok%d" % l, rv_tok, r_rvtok, BF16); dump("rk_tok%d" % l, rk_tok, r_rktok, BF16)
          chk("retproj%d" % l)

          S.barrier()
          PA.at(0)
          U_sb = PA([128, NT, 2, 2, 64]); r_U = [[Region("U%d_%d" % (d_, j)) for j in range(NT)] for d_ in range(2)]
          Sin = PA([128, 2, NT, 2, 64], BF16); r_Sin = [[Region("Sin%d_%d" % (d_, j)) for j in range(NT)] for d_ in range(2)]
          stout = PA([128, 2, 8, 2, 64]); r_stout = [Region("stout0"), Region("stout1")]
          PA.at(96)
          ND = 3
          kd = [PA([128, 2, 256], BF16) for _ in range(2)]; r_kdb = [Region("kd0"), Region("kd1")]
          Pm = [PA([128, 4, 128], BF16) for _ in range(ND)]; r_Pm = [Region("Pm%d" % i) for i in range(ND)]
          qd = [PA([128, 2, 2, 128], BF16) for _ in range(ND)]; r_qdb = [Region("qd%d" % i) for i in range(ND)]
          Sst = PA([128, 2, 2, 64]); r_Sst = [Region("Sst0"), Region("Sst1")]
          o_sb = [PA([128, 256]) for _ in range(ND)]; r_osb = [Region("osb%d" % i) for i in range(ND)]
          sqb = [PA([128, 256]) for _ in range(2)]; r_sqb = [Region("sqb0"), Region("sqb1")]
          rss = PA([128, NT, 4]); r_rss = [Region("rss%d" % j) for j in range(NT)]
          onb = [PA([128, 256], BF16) for _ in range(ND)]; r_onb = [Region("on%d" % i) for i in range(ND)]
          assert PA.cur - P_OFF <= 112 * KB, PA.cur - P_OFF
          for hh in range(2):
              for a in range(2):
                  S.dma("act", Sst[hh * 64:(hh + 1) * 64, a], s0_d[l, a].rearrange("(r hh) d v -> hh d r v", hh=2)[hh], writes=r_Sst)
          def u_step(d_, j, n):
              i2 = n % 2
              tt("pool" if d_ == 0 else "dve", kd[i2][:, d_, :].rearrange("p (h n) -> p h n", h=4), rk_tok[:, j, :].rearrange("p (h n) -> p h n", h=4),
                 kdtab[:, d_, :].unsqueeze(2).to_broadcast([128, 4, 64]), ALU.mult, [r_rktok[j], r_kd], [r_kdb[i2]])
              bk = n % 2
              for pr in range(2):
                  mm(PB[bk][:, pr * 128:(pr + 1) * 128], kd[i2][:, d_, pr * 128:(pr + 1) * 128], rv_tok[:, j, pr * 128:(pr + 1) * 128],
                     True, True, [r_kdb[i2], r_rvtok[j]], [RB[bk]], sig=(pr == 1))
              bv = PB[bk][:, 0:256].rearrange("p (a n) -> p a n", a=2)
              cp("act", U_sb[0:64, j, d_], bv[0:64, :, 0:64], [RB[bk]], [r_U[d_][j]])
              cp("act", U_sb[64:128, j, d_], bv[64:128, :, 64:128], [RB[bk]], [r_U[d_][j]])

          def chain_step(d_, j, eng):
              tt(eng, Sin[:, d_, j], Sst[:, d_], keepfb[:, d_, j:j + 1].unsqueeze(2).to_broadcast([128, 2, 64]), ALU.mult,
                 [r_Sst[d_], r_keep], [r_Sin[d_][j]])
              tt(eng, Sst[:, d_], Sst[:, d_], aktab[:, d_, :, j:j + 1].to_broadcast([128, 2, 64]), ALU.mult,
                 [r_Sst[d_], r_ak], [r_Sst[d_]])
              tt(eng, Sst[:, d_], Sst[:, d_], U_sb[:, j, d_], ALU.add, [r_Sst[d_], r_U[d_][j]], [r_Sst[d_]])
              if (d_ == 0 and j % 2 == 1) or (d_ == 1 and j % 2 == 0):
                  cp(eng, stout[:, d_, j // 2], Sst[:, d_], [r_Sst[d_]], [r_stout[d_]])

          n_ = 0
          for i in range(NT + 1):
              if i < NT:
                  u_step(0, i, n_); n_ += 1
                  u_step(1, NT - 1 - i, n_); n_ += 1
              if i >= 1:
                  chain_step(0, i - 1, "pool")
                  chain_step(1, NT - i, "dve")
          for d_ in range(2):
              for hh in range(2):
                  for pr in range(2):
                      S.dma("sp", ost_d[l, d_, :, 2 * pr + hh].rearrange("s d v -> d s v"), stout[hh * 64:(hh + 1) * 64, d_, :, pr, :],
                            reads=[r_stout[d_]])
          def stA(j):
              i3 = j % ND; tb = j // 4
              for par in range(2):
                  bs = 2 * (j % 2) + par
                  for h_ in (par, par + 2):
                      rows = slice((h_ % 2) * 64, (h_ % 2 + 1) * 64)
                      mm(PB[bs][:, (h_ // 2) * 128:(h_ // 2 + 1) * 128], rkT[rows, h_ // 2, j * 128:(j + 1) * 128],
                         rqT[rows, h_ // 2, j * 128:(j + 1) * 128],
                         True, True, [r_rkT[h_ // 2][tb], r_rqT[h_ // 2][tb]], [RB[bs]], sig=(h_ == par + 2))
              for par in range(2):
                  bs = 2 * (j % 2) + par
                  tt("dve", Pm[i3][:, par::2, :], PB[bs][:, 0:256].rearrange("p (h n) -> p h n", h=2), Mtab[:, par::2, :], ALU.mult,
                     [RB[bs], r_M], [r_Pm[i3]])
              for d_ in range(2):
                  tt("pool", qd[i3][:, d_], rqT[:, :, j * 128:(j + 1) * 128], qdtab[:, d_], ALU.mult,
                     [r_rqT[0][tb], r_rqT[1][tb], r_qd], [r_qdb[i3]])

          def stB(j):
              i3 = j % ND
              bo = 4 + j % 2
              for h_ in range(RH):
                  rows = slice((h_ % 2) * 64, (h_ % 2 + 1) * 64)
                  oap = PB[bo][:, h_ * 64:(h_ + 1) * 64]
                  mm(oap, Pm[i3][:, h_, :], rv_tok[:, j, h_ * 64:(h_ + 1) * 64], True, False, [r_Pm[i3], r_rvtok[j]], [RB[bo]], sig=False)
                  mm(oap, qd[i3][rows, 0, h_ // 2, :], Sin[rows, 0, j, h_ // 2, :], False, False, [r_qdb[i3], r_Sin[0][j]], [RB[bo]], sig=False)
                  mm(oap, qd[i3][rows, 1, h_ // 2, :], Sin[rows, 1, j, h_ // 2, :], False, True, [r_qdb[i3], r_Sin[1][j]], [RB[bo]], sig=(h_ == 3))
              cp("act", o_sb[i3], PB[bo][:, 0:256], [RB[bo]], [r_osb[i3]])
              tt("dve", sqb[j % 2], o_sb[i3], o_sb[i3], ALU.mult, [r_osb[i3]], [r_sqb[j % 2]])
              S.op("dve", lambda e: e.tensor_reduce(out=rss[:, j, :], in_=sqb[j % 2].rearrange("p (h n) -> p h n", h=4), axis=AX.X, op=ALU.add),
                   [r_sqb[j % 2]], [r_rss[j]])
              rstd_of(rss[:, j, :], 64, r_rss[j], epsc)

          def stC(j):
              i3 = j % ND
              tt("dve", onb[i3].rearrange("p (h n) -> p h n", h=4), o_sb[i3].rearrange("p (h n) -> p h n", h=4),
                 rss[:, j, :].unsqueeze(2).to_broadcast([128, 4, 64]), ALU.mult, [r_osb[i3], r_rss[j]], [r_onb[i3]])
              bt = 6 + j % 2
              for c in range(2):
                  S.op("pe", lambda e, c=c: e.transpose(PBb[bt][:, c * 128:(c + 1) * 128], onb[i3][:, c * 128:(c + 1) * 128], ident),
                       [r_onb[i3], r_ident], [RB[bt]], sig=(c == 1))
              tt("dve", mixT[:, 6:8, j * 128:(j + 1) * 128], PBb[bt][:, 0:256].rearrange("p (c n) -> p c n", c=2),
                 mixT[:, 6:8, j * 128:(j + 1) * 128], ALU.mult, [RB[bt], r_retT[0][j], r_retT[1][j]], [r_retT[0][j], r_retT[1][j]])

          for i in range(NT + 2):
              if i < NT:
                  stA(i)
              if 0 <= i - 1 < NT:
                  stB(i - 1)
              if 0 <= i - 2 < NT:
                  stC(i - 2)
          dump("retT%d" % l, mixT[:, 6:8, :], r_retT[0] + r_retT[1], BF16)
          dump("U%d" % l, U_sb, r_U[0] + r_U[1]); dump("Sin%d" % l, Sin, r_Sin[0] + r_Sin[1], BF16)
          chk("ret%d" % l)

          S.barrier()
          PA.at(0)
          kT = PA([128, 4, NK], BF16); r_kT = [Region("kT%d" % t) for t in range(NKT)]; r_kTm = Region("kTm")
          pT = [PA([128, 1024], BF16) for _ in range(2)]; r_pT = [Region("pT%d" % i) for i in range(2)]
          wkv = PA([128, 512], BF16); r_wkv = Region("wkv")
          wq = PA([128, 2, 384], BF16); r_wq = Region("wq")
          NS = 3
          kcat = [PA([128, 4, QKH]) for _ in range(NS)]; r_kcat = [Region("kcat%d" % i) for i in range(NS)]
          rt = [[PA([128, 4, 2, 8]) for _ in range(2)] for _ in range(NS)]; r_rt = [Region("rt%d" % i) for i in range(NS)]
          rs4 = [PA([128, 4]) for _ in range(NS)]; r_rs4 = [Region("rs4_%d" % i) for i in range(NS)]
          assert PA.cur - P_OFF <= 32 * KB, PA.cur - P_OFF
          PA.at(80)
          Vaug = PA([128, NKT, 4, 96], BF16); r_V = [Region("V%d" % t) for t in range(NKT)]; r_Vones = Region("Vones")
          qTb = [PA([128, 4, 512], BF16) for _ in range(2)]; r_qTb = [Region("qTb0"), Region("qTb1")]; r_qTm = [Region("qTm0"), Region("qTm1")]
          rec = [PA([128, 512]) for _ in range(2)]; r_rec = [Region("rec0"), Region("rec1")]
          ones_t = PA([128, 512]); r_ones = Region("ones_t")
          jq = PA([128, QKH], BF16)
          jq2 = PA([128, QKH], BF16)
          kfin = [PA([128, 4, QKH], BF16) for _ in range(NS)]; r_kfin = [Region("kfin%d" % i) for i in range(NS)]
          rt2 = [[PA([128, 4, 2, 8]) for _ in range(2)] for _ in range(NS)]; r_rt2 = [Region("rt2_%d" % i) for i in range(NS)]
          assert PA.cur - P_OFF <= 112 * KB, PA.cur - P_OFF
          S.dma("pool", kT[96:128, :, :], ka_d.unsqueeze(1).to_broadcast([32, 4, NK]), writes=[r_kTm])
          S.op("pool", lambda e: e.memset(Vaug[:, :, :, 64:96], 1.0), writes=[r_Vones])
          S.op("pool", lambda e: e.memset(ones_t[64:96, :], -1.0), writes=[r_ones])
          wqv = wq_d[l].rearrange("(c p) n -> p c n", p=128)
          cnt = {"i": 0}

          def st1(it):
              s_ = it["s"]
              b = 4
              if it["kind"] == "k":
                  kt = it["kt"]
                  mm(PB[b], ckvT[:, kt * 128:(kt + 1) * 128], wkv, True, True, [r_ckvT[kt], r_wkv], [RB[b]])
                  bv = PB[b].rearrange("p (h n) -> p h n", h=4)
                  cp("act", Vaug[:, kt, :, 0:64], bv[:, :, 64:128], [RB[b]], [r_V[kt]])
                  cp("act", kcat[s_][:, :, 0:NOPE], bv[:, :, 0:NOPE], [RB[b]], [r_kcat[s_]])
                  cp("pool", kcat[s_][:, :, NOPE:QKH], kpe_sb[:, kt, :].unsqueeze(1).to_broadcast([128, 4, ROPE]), [r_kpe[kt]], [r_kcat[s_]])
              else:
                  t = it["t"]
                  for c in range(2):
                      mm(PB[b][:, 0:384], qlnT[:, c, t * 128:(t + 1) * 128], wq[:, c, :], c == 0, c == 1,
                         [r_qlnT[t], r_wq], [RB[b]], sig=(c == 1))
                  cp("dve", kcat[s_], PB[b][:, 0:384].rearrange("p (h n) -> p h n", h=4), [RB[b]], [r_kcat[s_]])

          def st2a(it):
              s_ = it["s"]
              src = kcat[s_]; r_src = r_kcat[s_]; dst = kfin[s_]; r_dst = r_kfin[s_]
              if it["kind"] == "k":
                  gain = khgb; cos_t = cos_sb[:, it["kt"], :]; sin_t = sin_sb[:, it["kt"], :]
              else:
                  gain = qhgb; cos_t = cos_sb[:, it["t"], :]; sin_t = sin_sb[:, it["t"], :]
              for hh in range(4):
                  if it["kind"] == "k":
                      act(jq, src[:, hh, :], AF.Square, [r_src], [r_rs4[s_]], accum_out=rs4[s_][:, hh:hh + 1])
                  else:
                      S.op("dve", lambda e, hh=hh: e.scalar_tensor_tensor(out=jq2, in0=src[:, hh, :], scalar=1.0, in1=src[:, hh, :],
                                                                          op0=ALU.mult, op1=ALU.mult, accum_out=rs4[s_][:, hh:hh + 1]),
                         [r_src], [r_rs4[s_]])
              tt("pool", src, src, gain.unsqueeze(1).to_broadcast([128, 4, QKH]), ALU.mult, [r_src, r_gb], [r_src])

          def st2b(it):
              s_ = it["s"]
              src = kcat[s_]; r_src = r_kcat[s_]; dst = kfin[s_]; r_dst = r_kfin[s_]
              if it["kind"] == "k":
                  cos_t = cos_sb[:, it["kt"], :]; sin_t = sin_sb[:, it["kt"], :]
              else:
                  cos_t = cos_sb[:, it["t"], :]; sin_t = sin_sb[:, it["t"], :]
              rstd_of(rs4[s_], QKH, r_rs4[s_], epsc)
              tt("dve", dst[:, :, 0:NOPE], src[:, :, 0:NOPE], rs4[s_].unsqueeze(2).to_broadcast([128, 4, NOPE]), ALU.mult,
                 [r_src, r_rs4[s_]], [r_dst])
              tt("dve", src[:, :, NOPE:QKH], src[:, :, NOPE:QKH], rs4[s_].unsqueeze(2).to_broadcast([128, 4, ROPE]), ALU.mult,
                 [r_src, r_rs4[s_]], [r_src])
              rv = src[:, :, NOPE:QKH].rearrange("p h (a b i) -> p h a b i", a=2, b=2)
              dv = dst[:, :, NOPE:QKH].rearrange("p h (a b i) -> p h a b i", a=2, b=2)
              x1 = rv[:, :, :, 0, :]; x2 = rv[:, :, :, 1, :]
              cb_ = cos_t.rearrange("p (a i) -> p a i", a=2).unsqueeze(1).to_broadcast([128, 4, 2, 8])
              sb_ = sin_t.rearrange("p (a i) -> p a i", a=2).unsqueeze(1).to_broadcast([128, 4, 2, 8])
              t0, t1 = rt[s_]
              t2, t3 = rt2[s_]
              tt("pool", t0, x1, cb_, ALU.mult, [r_src, r_trig], [r_rt[s_]])
              tt("pool", t1, x2, sb_, ALU.mult, [r_src, r_trig], [r_rt[s_]])
              tt("pool", dv[:, :, :, 0, :], t0, t1, ALU.subtract, [r_rt[s_]], [r_dst])
              tt("dve", t2, x1, sb_, ALU.mult, [r_src, r_trig], [r_rt2[s_]])
              tt("dve", t3, x2, cb_, ALU.mult, [r_src, r_trig], [r_rt2[s_]])
              tt("dve", dv[:, :, :, 1, :], t2, t3, ALU.add, [r_rt2[s_]], [r_dst])

          def st3(it):
              s_ = it["s"]
              bt = 5
              for hh in range(4):
                  S.op("pe", lambda e, hh=hh: e.transpose(PBb[bt][0:QKH, hh * 128:(hh + 1) * 128], kfin[s_][:, hh, :], ident),
                       [r_kfin[s_], r_ident], [RB[bt]], sig=(hh == 3))
              srcp = PBb[bt][0:QKH, 0:512].rearrange("p (h n) -> p h n", h=4)
              if it["kind"] == "k":
                  kt = it["kt"]
                  cp("dve", kT[0:QKH, :, kt * 128:(kt + 1) * 128], srcp, [RB[bt]], [r_kT[kt]])
              else:
                  tl = it["t"] % 4
                  cp("dve", qTb[it["buf"]][0:QKH, :, tl * 128:(tl + 1) * 128], srcp, [RB[bt]], [r_qTb[it["buf"]]])

          def mk(kind, **kw):
              it = dict(kind=kind, s=cnt["i"] % NS, n=cnt["i"], **kw)
              cnt["i"] += 1
              return it

          def run_skewed(items):
              n = len(items)
              for i in range(n + 2):
                  if i < n:
                      st1(items[i])
                  if 0 <= i - 1 < n:
                      st2a(items[i - 1]); st2b(items[i - 1])
                  if 0 <= i - 2 < n:
                      st3(items[i - 2])

          def qa_rows(qb, buf):
              S.dma("pool", qTb[buf][96:128, :, :], qa_d[:, qb * 512:(qb + 1) * 512].unsqueeze(1).to_broadcast([32, 4, 512]), writes=[r_qTm[buf]])

          hcount = 0
          for g in range(2):
              S.dma("pool", wkv, wkv_d[l][:, g * 512:(g + 1) * 512], writes=[r_wkv])
              S.dma("pool", wq, wqv[:, :, g * 384:(g + 1) * 384], writes=[r_wq])
              qa_rows(0, 0)
              run_skewed([mk("k", kt=kt) for kt in range(NKT)])
              if "kT" in dbg and l == 0 and g == 0:
                  dump("kT", kT, r_kT + [r_kTm], BF16); dump("Vaug", Vaug, r_V + [r_Vones], BF16)
              run_skewed([mk("q", t=t, buf=0) for t in range(4)])
              if "qT" in dbg and l == 0 and g == 0:
                  dump("qT", qTb[0], [r_qTb[0], r_qTm[0]], BF16)
              NP = NKT // 2
              jobs = [(qb, hh, p_) for qb in range(4) for hh in range(4) for p_ in range(NP)]
              nxt_items = {}
              for qb in range(3):
                  nxt_items[qb] = None

              def score(job, gp):
                  qb, hh, p_ = job
                  buf = qb % 2
                  for q_ in range(2):
                      kt = 2 * p_ + q_
                      b = 2 * (gp % 2) + q_
                      mm(PB[b], kT[:, hh, kt * 128:(kt + 1) * 128], qTb[buf][:, hh, :], True, True,
                         [r_kT[kt], r_kTm, r_qTb[buf], r_qTm[buf]], [RB[b]])

              score(jobs[0], 0); score(jobs[1], 1)
              ob = 6
              pend_fin = []
              for gp, job in enumerate(jobs):
                  qb, hh, p_ = job
                  buf = qb % 2
                  if p_ == 0:
                      ob = 6 + hcount % 2
                      hcount += 1
                      if qb + 1 < 4:
                          if hh == 0:
                              qa_rows(qb + 1, 1 - buf)
                              nxt_items[qb] = [mk("q", t=(qb + 1) * 4 + tl, buf=1 - buf) for tl in range(4)]
                          st1(nxt_items[qb][hh]); st2a(nxt_items[qb][hh])
                  if p_ == 3 and qb + 1 < 4:
                      st2b(nxt_items[qb][hh])
                  if p_ == 6 and qb + 1 < 4:
                      st3(nxt_items[qb][hh])
                  if p_ == 2 and pend_fin:
                      pend_fin.pop(0)()
                  pb = gp % 2
                  b0 = 2 * pb
                  act(pT[pb], ps_all[:, b0 * 512:(b0 + 2) * 512], AF.Exp, [RB[b0], RB[b0 + 1]], [r_pT[pb]], scale=float(QKH) ** -0.5)
                  if gp + 2 < len(jobs):
                      score(jobs[gp + 2], gp + 2)
                  for q_ in range(2):
                      kt = 2 * p_ + q_
                      mm(PB[ob][0:96, :], Vaug[:, kt, hh, :], pT[pb][:, q_ * 512:(q_ + 1) * 512], kt == 0, kt == NKT - 1,
                         [r_V[kt], r_Vones, r_pT[pb]], [RB[ob]])
                  if p_ == NP - 1:
                      h_ = 4 * g + hh
                      ri = (hcount - 1) % 2
                      S.op("dve", lambda e: e.reciprocal(out=rec[ri][64:96, :], in_=PB[ob][64:96, :]), [RB[ob]], [r_rec[ri]])

                      def fin(h_=h_, ob=ob, ri=ri, qb=qb):
                          for hv in range(2):
                              r0 = (h_ % 2) * 64 + hv * 32
                              tt("dve", mixT[r0:r0 + 32, h_ // 2, qb * 512:(qb + 1) * 512], PB[ob][hv * 32:hv * 32 + 32, :], rec[ri][64:96, :],
                                 ALU.mult, [RB[ob], r_rec[ri]], [r_attnT[h_ // 2][qb]])
                      pend_fin.append(fin)
              while pend_fin:
                  pend_fin.pop(0)()
          dump("attnT%d" % l, mixT[:, 0:4, :], [r for c in range(4) for r in r_attnT[c]], BF16)
          chk("attn%d" % l)

          S.barrier()
          PA.at(0)
          wo = PA([128, 8, D], BF16); r_wo = [Region("wo0"), Region("wo1")]
          tmpx = [PA([128, 512]) for _ in range(2)]; r_tmpx = [Region("tmpx0"), Region("tmpx1")]
          wov = wo_d[l].rearrange("(c p) n -> p c n", p=128)
          for hf in range(2):
              S.dma("pool", wo[:, :, hf * 512:(hf + 1) * 512], wov[:, :, hf * 512:(hf + 1) * 512], writes=[r_wo[hf]])
          bi = 0
          for hf in range(2):
              for t in range(NT):
                  mreads = [r_attnT[c][t // 4] for c in range(4)] + r_convT + [r_retT[0][t], r_retT[1][t]]
                  bk = bi % 8; i2 = bi % 2; bi += 1
                  for c in range(8):
                      mm(PB[bk], mixT[:, c, t * 128:(t + 1) * 128], wo[:, c, hf * 512:(hf + 1) * 512], c == 0, c == 7,
                         mreads + [r_wo[hf]], [RB[bk]], sig=(c == 7))
                  tt("dve", tmpx[i2], PB[bk], gates[:, 0, hf * 512:(hf + 1) * 512], ALU.mult, [RB[bk], r_gates[0]], [r_tmpx[i2]])
                  tt("pool", X[:, t, hf * 512:(hf + 1) * 512], X[:, t, hf * 512:(hf + 1) * 512], tmpx[i2], ALU.add, [RX[t], r_tmpx[i2]], [RX[t]])
          dump("xmid%d" % l, X, RX)
          chk("wo%d" % l)

          S.barrier()
          PA.at(0)
          h2T = PA([128, 8, 1024], BF16); r_h2T = [Region("h2T0"), Region("h2T1")]
          actT = PA([128, NHC, 1024], BF16); r_actT = [[Region("actT%d_%d" % (hc, b)) for b in range(2)] for hc in range(NHC)]
          wgu = [PA([128, 2, 8, 256], BF16) for _ in range(3)]; r_wgu = [Region("wgu%d" % i) for i in range(3)]
          wdp = [PA([128, 2, D], BF16) for _ in range(3)]; r_wdp = [Region("wdp%d" % i) for i in range(3)]
          xn2 = [PA([128, D], BF16) for _ in range(4)]; r_xn2 = [Region("xn2_%d" % i) for i in range(4)]
          junk2 = PA([128, D], BF16); r_junk2 = Region("junk2")
          ssb2 = smallscr[:, 16:24]; r_ssb2 = Region("ssb2")
          sg = [PA([128, 512], BF16) for _ in range(2)]; r_sg = [Region("sg0"), Region("sg1")]
          tmpy = [PA([128, 512]) for _ in range(2)]; r_tmpy = [Region("tmpy0"), Region("tmpy1")]
          assert PA.cur - P_OFF <= 112 * KB, PA.cur - P_OFF
          wgv = wg_d[l].rearrange("(c p) n -> p c n", p=128)
          wuv = wu_d[l].rearrange("(c p) n -> p c n", p=128)
          wdv = wd_d[l].rearrange("(a p) n -> p a n", p=128)
          gu_n = [0]; dn_n = [0]

          def load_gu(pc):
              i = gu_n[0] % 3; gu_n[0] += 1
              S.dma("pool", wgu[i][:, 0], wgv[:, :, pc * 256:(pc + 1) * 256], writes=[r_wgu[i]])
              S.dma("pool", wgu[i][:, 1], wuv[:, :, pc * 256:(pc + 1) * 256], writes=[r_wgu[i]])
              return i

          def load_dn(pc):
              i = dn_n[0] % 3; dn_n[0] += 1
              S.dma("pool", wdp[i], wdv[:, 2 * pc:2 * pc + 2, :], writes=[r_wdp[i]])
              return i

          bi = 0
          for half in range(2):
              tiles = list(range(half * 8, half * 8 + 8))
              gq = [load_gu(0), load_gu(1)]
              norm_phase(2, tiles, h2T, r_h2T, xn2, r_xn2, junk2, r_junk2, ssb2, r_ssb2)
              if half == 0:
                  dump("h2T%d" % l, h2T, r_h2T, BF16)
              dq = []
              for pc in range(11):
                  wi = gq.pop(0)
                  if pc + 2 < 11:
                      gq.append(load_gu(pc + 2))
                  elif pc + 2 == 11:
                      dq.append(load_dn(0))
                  else:
                      dq.append(load_dn(1))
                  for hh in range(2):
                      hc = 2 * pc + hh
                      for tb in range(2):
                          bG = (2 * bi) % 8; bU = (2 * bi + 1) % 8; i2 = bi % 2; bi += 1
                          for k in range(8):
                              mm(PB[bG], wgu[wi][:, 0, k, hh * 128:(hh + 1) * 128], h2T[:, k, tb * 512:(tb + 1) * 512], k == 0, k == 7,
                                 [r_wgu[wi], r_h2T[tb]], [RB[bG]], sig=(k == 7))
                          for k in range(8):
                              mm(PB[bU], wgu[wi][:, 1, k, hh * 128:(hh + 1) * 128], h2T[:, k, tb * 512:(tb + 1) * 512], k == 0, k == 7,
                                 [r_wgu[wi], r_h2T[tb]], [RB[bU]], sig=(k == 7))
                          act(sg[i2], PB[bG], AF.Silu, [RB[bG]], [r_sg[i2]])
                          tt("dve", actT[:, hc, tb * 512:(tb + 1) * 512], PB[bU], sg[i2], ALU.mult, [RB[bU], r_sg[i2]], [r_actT[hc][tb]])
              for ps_ in range(2):
                  for pc in range(11):
                      wi = dq.pop(0)
                      nxt_pc = pc + 2
                      if nxt_pc < 11:
                          dq.append(load_dn(nxt_pc))
                      elif ps_ == 0:
                          dq.append(load_dn(nxt_pc - 11))
                      for hh in range(2):
                          hc = 2 * pc + hh
                          for tl4 in range(4):
                              tl = ps_ * 4 + tl4
                              for hf in range(2):
                                  bk = tl4 * 2 + hf
                                  last = (hh == 1 and tl4 == 3 and hf == 1)
                                  mm(PB[bk], actT[:, hc, tl * 128:(tl + 1) * 128], wdp[wi][:, hh, hf * 512:(hf + 1) * 512],
                                     hc == 0, hc == NHC - 1, [r_actT[hc][tl // 4], r_wdp[wi]], [RB[bk]], sig=last)
                  for tl4 in range(4):
                      tl = ps_ * 4 + tl4
                      t = tiles[tl]
                      for hf in range(2):
                          bk = tl4 * 2 + hf; i2 = hf
                          tt("dve", tmpy[i2], PB[bk], gates[:, 1, hf * 512:(hf + 1) * 512], ALU.mult, [RB[bk], r_gates[1]], [r_tmpy[i2]])
                          tt("pool", X[:, t, hf * 512:(hf + 1) * 512], X[:, t, hf * 512:(hf + 1) * 512], tmpy[i2], ALU.add,
                             [RX[t], r_tmpy[i2]], [RX[t]])
                      if l == L - 1:
                          S.dma("sp", y_d.rearrange("(t p) d -> p t d", p=128)[:, t, :], X[:, t, :], reads=[RX[t]])
          dump("xout%d" % l, X, RX)
          chk("ffn%d" % l)
    except _Stop:
        pass

    S.finish("sp")
    print("instructions", S.ninstr, "waits", S.nwaits, "cnt", S.cnt)
    return nc, dbg_outs


def _tables(is_sample):
    cos = np.ones((NK, 16), np.float32); sin = np.zeros((NK, 16), np.float32)
    if is_sample:
        t = np.arange(T)
        row = (t // GRID_W).astype(np.float32); col = (t % GRID_W).astype(np.float32)
        half = ROPE // 2
        freqs = np.power(np.float32(10000.0), -np.arange(0, half, 2, dtype=np.float32) / half).astype(np.float32)
        ar = row[:, None] * freqs[None]; ac = col[:, None] * freqs[None]
        cos[:T, 0:8] = np.cos(ar); cos[:T, 8:16] = np.cos(ac)
        sin[:T, 0:8] = np.sin(ar); sin[:T, 8:16] = np.sin(ac)
    qa = np.zeros((32, T), np.float32); ka = np.zeros((32, NK), np.float32)
    if is_sample:
        ka[0, :] = 1.0
    else:
        gq = np.arange(T) // 256
        for j in range(8):
            ka[j, :T] = (gq == j)
            qa[j, :] = np.where(gq == j, 0.0, -BIG)
        ka[8, T:] = 1.0
        qa[8, :] = -BIG
    seqlen = T if is_sample else 256
    t = np.arange(T)
    mL = (t % seqlen != 0).astype(np.float32); mR = (t % seqlen != seqlen - 1).astype(np.float32)
    keepf = np.ones(NT, np.float32); keepb = np.ones(NT, np.float32)
    if not is_sample:
        keepf[0::2] = 0.0
        keepb[1::2] = 0.0
    keepf[0] = 1.0; keepb[NT - 1] = 1.0
    nbL = (mL[0::256] - 1.0).astype(np.float32)
    nbR = (mR[255::256] - 1.0).astype(np.float32)
    return dict(cos=cos, sin=sin, qa=qa, ka=ka, nbL=nbL, nbR=nbR, keepf=keepf, keepb=keepb)


def _cst():
    c = np.zeros((128, 3, 128), np.float32)
    k = np.arange(128, dtype=np.float32)[:, None]; q = np.arange(128, dtype=np.float32)[None, :]
    c[:, 0, :] = q - k
    c[:, 1, :] = q + 1.0
    c[:, 2, :] = 128.0 - q
    return c


_WNAMES = ["ada_w", "ada_b", "w_in", "q_norm_g", "kv_norm_g", "w_q_up", "w_kv_up",
           "q_head_norm_g", "k_head_norm_g", "ret_decay_fwd", "ret_decay_bwd", "w_o",
           "w_ffn_gate", "w_ffn_up", "w_ffn_down"]

_PROGRAM = {}


def _run(inputs, dbg=None, stop=None):
    f32 = lambda a: np.ascontiguousarray(np.asarray(a, dtype=np.float32))
    key = (tuple(sorted(dbg)) if dbg else (), stop)
    if key not in _PROGRAM:
        _PROGRAM[key] = build_program(dbg, stop)
    nc, dbg_outs = _PROGRAM[key]
    W = {k: f32(inputs[k]) for k in _WNAMES}
    W["ada_b_pc"] = f32(inputs["ada_b"]).reshape(L, 48, 128).transpose(0, 2, 1)
    W["n1g_pc"] = f32(inputs["norm1_g"]).reshape(L, 8, 128).transpose(0, 2, 1)
    W["n2g_pc"] = f32(inputs["norm2_g"]).reshape(L, 8, 128).transpose(0, 2, 1)
    W["cw_pc"] = f32(inputs["conv_w"]).reshape(L, 3, 2, 128).transpose(0, 3, 2, 1)
    xs = f32(inputs["x_sample"]); xp = f32(inputs["x_prompt"])
    c = f32(inputs["c"]); c_ctx = f32(inputs["c_ctx"])
    cckv = f32(inputs["cache_ckv"]); ckpe = f32(inputs["cache_kpe"]); st = f32(inputs["state_ret"])
    ts_, tp_ = _tables(True), _tables(False)
    cst = _cst()
    in_maps = []
    for i in range(NCORES):
        if i < 4:
            m = dict(x=xs[i], cond_pc=c[i].reshape(8, 128).T, cckv=cckv[i], ckpe=ckpe[i], s0=st[i], **ts_)
        else:
            j = i - 4
            m = dict(x=xp[8 * j:8 * j + 8].reshape(T, D), cond_pc=c_ctx.reshape(8, 128).T,
                     cckv=np.zeros((L, NCTX, KVL), np.float32), ckpe=np.zeros((L, NCTX, ROPE), np.float32),
                     s0=np.zeros((L, 2, RH, 64, 64), np.float32), **tp_)
        m["cst"] = cst
        m.update(W)
        in_maps.append({k: np.ascontiguousarray(v) for k, v in m.items()})
    res = run_bass_kernel_spmd(nc, in_maps, core_ids=list(range(NCORES)))
    return res.results


def kernel(**inputs):
    r = _run(inputs)
    y_sample = np.stack([r[i]["y"] for i in range(4)], 0)
    y_prompt = np.concatenate([r[4 + j]["y"].reshape(8, 256, D) for j in range(4)], 0)
    new_ckv = np.concatenate([r[4 + j]["ockv"].reshape(L, 8, 256, KVL).transpose(1, 0, 2, 3) for j in range(4)], 0)
    new_kpe = np.concatenate([r[4 + j]["okpe"].reshape(L, 8, 256, ROPE).transpose(1, 0, 2, 3) for j in range(4)], 0)
    new_st = np.concatenate([r[4 + j]["ost"].transpose(2, 0, 1, 3, 4, 5) for j in range(4)], 0)
    return (y_prompt.astype(np.float32), y_sample.astype(np.float32), np.ascontiguousarray(new_ckv, dtype=np.float32),
            np.ascontiguousarray(new_kpe, dtype=np.float32), np.ascontiguousarray(new_st, dtype=np.float32))
```

```python
import os
import numpy as np
import concourse.bass as bass
import concourse.mybir as mybir
from concourse.bass_utils import run_bass_kernel_spmd

F32 = mybir.dt.float32
BF16 = mybir.dt.bfloat16
AF = mybir.ActivationFunctionType
ALU = mybir.AluOpType
AX = mybir.AxisListType

L = 2; D = 1024; T = 2048; NT = 16; NCTX = 256; NK = T + NCTX; NKT = 18
QL = 256; KVL = 128; ROPE = 32; NOPE = 64; QKH = 96; VH = 64; H = 8
CW = 256; RH = 4; CH = 128
FFN = 2816; NHC = 22; EPS = 1e-6; BIG = 30000.0
GRID_W = 64
NCORES = 8
C_QLAT = 0; C_KV = 256; C_KPE = 384; C_GB = 416; C_GC = 672; C_XIN = 928
C_RQ = 1184; C_RK = 1440; C_RV = 1696; C_RG = 1952


class _Stop(Exception):
    pass


class Region:
    __slots__ = ("name", "w", "r", "excl")

    def __init__(self, name, excl=False):
        self.name = name
        self.w = []
        self.r = []
        self.excl = excl


ATTACH_ENGINES = ("pe", "act", "dve", "pool")


class Sched:
    ENG = ("pe", "act", "dve", "pool", "sp")

    def __init__(self, nc, n_dma_sems=32):
        self.nc = nc
        self.e = {"pe": nc.tensor, "act": nc.scalar, "dve": nc.vector,
                  "pool": nc.gpsimd, "sp": nc.sync}
        self.sem = {k: nc.alloc_semaphore("sem_" + k) for k in self.ENG}
        self.cnt = {k: 0 for k in self.ENG}
        self.seen = {k: {} for k in self.ENG}
        self.hist = {}
        self.dma_sems = [nc.alloc_semaphore("dsem%d" % i) for i in range(n_dma_sems)]
        self.dma_cnt = [0] * n_dma_sems
        self.dma_last = [None] * n_dma_sems
        self.dma_rr = 0
        self.nwaits = 0
        self.ninstr = 0
        self.defer = None
        self.limit = int(os.environ.get("KSTOPN", "0"))

    def _need(self, eng, tok):
        key, sem, val = tok
        if key == eng and eng in ("pe", "sp", "pool"):
            return False
        return self.seen[eng].get(key, 0) < val

    def _wait(self, eng, tok):
        key, sem, val = tok
        if not self._need(eng, tok):
            return
        if self.defer is not None:
            self.defer.append((sem, val))
        else:
            self.e[eng].wait_ge(sem, val)
        self.nwaits += 1
        s = self.seen[eng]
        s[key] = val
        h = self.hist.get((key, val))
        if h:
            for k2, v2 in h.items():
                if s.get(k2, 0) < v2:
                    s[k2] = v2

    def _deps(self, eng, reads, writes):
        best = {}
        self_raw = None

        def upd(t):
            if t[0] not in best or best[t[0]][2] < t[2]:
                best[t[0]] = t

        for r in reads:
            for t in r.w:
                if t[0] == eng:
                    if self_raw is None or self_raw[2] < t[2]:
                        self_raw = t
                else:
                    upd(t)
            if r.excl:
                for t in r.r:
                    if t[0] != eng:
                        upd(t)
        for w in writes:
            for t in w.w + w.r:
                if t[0] != eng:
                    upd(t)
        if self_raw is not None:
            best[eng] = self_raw
        for t in best.values():
            self._wait(eng, t)

    def _record(self, tok, reads, writes):
        for r in reads:
            if r.excl:
                r.w = [tok]
                r.r = []
            else:
                r.r = [t for t in r.r if t[0] != tok[0]] + [tok]
        for w in writes:
            w.w = [tok]
            w.r = []

    def op(self, eng, fn, reads=(), writes=(), sig=True):
        reads = list(reads)
        writes = list(writes)
        if self.limit and self.ninstr >= self.limit:
            raise _Stop()
        self.defer = []
        self._deps(eng, reads, writes)
        pend = self.defer
        self.defer = None
        attach = None
        if pend and eng in ATTACH_ENGINES:
            attach = pend.pop()
        for (sem_, val_) in pend:
            self.e[eng].wait_ge(sem_, val_)
        ins = fn(self.e[eng])
        if attach is not None:
            ins._wait_ge(attach[0], attach[1])
        self.ninstr += 1
        if sig:
            self.cnt[eng] += 1
            ins.then_inc(self.sem[eng], 1)
            tok = (eng, self.sem[eng], self.cnt[eng])
            self.hist[(eng, self.cnt[eng])] = dict(self.seen[eng])
        else:
            tok = (eng, self.sem[eng], self.cnt[eng] + 1)
        self._record(tok, reads, writes)
        return ins

    def dma(self, eng, out, in_, reads=(), writes=(), **kw):
        reads = list(reads)
        writes = list(writes)
        if self.limit and self.ninstr >= self.limit:
            raise _Stop()
        slot = self.dma_rr
        self.dma_rr = (self.dma_rr + 1) % len(self.dma_sems)
        prev = self.dma_last[slot]
        self._deps(eng, reads, writes)
        if prev is not None:
            self._wait(eng, prev)
        ins = self.e[eng].dma_start(out=out, in_=in_, **kw)
        self.ninstr += 1
        self.dma_cnt[slot] += 16
        ins.then_inc(self.dma_sems[slot], 16)
        key = "d%d" % slot
        tok = (key, self.dma_sems[slot], self.dma_cnt[slot])
        self.hist[(key, self.dma_cnt[slot])] = dict(self.seen[eng])
        self.dma_last[slot] = tok
        self._record(tok, reads, writes)
        return tok

    def all_tokens(self):
        toks = []
        for k in self.ENG:
            if self.cnt[k] > 0:
                toks.append((k, self.sem[k], self.cnt[k]))
        for t in self.dma_last:
            if t is not None:
                toks.append(t)
        return toks

    def barrier(self, engines=None):
        toks = self.all_tokens()
        for e in (engines or self.ENG):
            for t in toks:
                if t[0] != e:
                    self._wait(e, t)

    def finish(self, eng="sp"):
        for t in self.all_tokens():
            if t[0] != eng:
                self._wait(eng, t)


def _prod(s):
    n = 1
    for v in s:
        n *= v
    return n


def build_program(dbg=None, stop=None):
    nc = bass.Bass("TRN2", target_bir_lowering=False)
    S = Sched(nc)
    dbg = dbg or set()
    dbg_outs = {}

    def din(name, shape):
        return nc.dram_tensor(name, list(shape), F32, kind="ExternalInput").ap()

    def dout(name, shape, dt=F32):
        return nc.dram_tensor(name, list(shape), dt, kind="ExternalOutput").ap()

    x_d = din("x", [T, D]); cond_d = din("cond_pc", [128, 8])
    cckv_d = din("cckv", [L, NCTX, KVL]); ckpe_d = din("ckpe", [L, NCTX, ROPE])
    s0_d = din("s0", [L, 2, RH, 64, 64])
    cos_d = din("cos", [NK, 16]); sin_d = din("sin", [NK, 16])
    qa_d = din("qa", [32, T]); ka_d = din("ka", [32, NK])
    nbL_d = din("nbL", [8]); nbR_d = din("nbR", [8])
    keepf_d = din("keepf", [NT]); keepb_d = din("keepb", [NT])
    cst_d = din("cst", [128, 3, 128])
    ada_w_d = din("ada_w", [L, D, 6 * D]); ada_b_d = din("ada_b", [L, 6 * D]); adab_pc_d = din("ada_b_pc", [L, 128, 48])
    n1g_d = din("n1g_pc", [L, 128, 8]); n2g_d = din("n2g_pc", [L, 128, 8])
    w_in_d = din("w_in", [L, D, 2208])
    qng_d = din("q_norm_g", [L, QL]); kvng_d = din("kv_norm_g", [L, KVL])
    wq_d = din("w_q_up", [L, QL, H * QKH]); wkv_d = din("w_kv_up", [L, KVL, H * 128])
    qhg_d = din("q_head_norm_g", [L, QKH]); khg_d = din("k_head_norm_g", [L, QKH])
    cw_d = din("cw_pc", [L, 128, 2, 3])
    rdf_d = din("ret_decay_fwd", [L, RH]); rdb_d = din("ret_decay_bwd", [L, RH])
    wo_d = din("w_o", [L, D, D])
    wg_d = din("w_ffn_gate", [L, D, FFN]); wu_d = din("w_ffn_up", [L, D, FFN])
    wd_d = din("w_ffn_down", [L, FFN, D])

    y_d = dout("y", [T, D]); ockv_d = dout("ockv", [L, T, KVL]); okpe_d = dout("okpe", [L, T, ROPE])
    ost_d = dout("ost", [L, 2, 8, RH, 64, 64])

    BIGW = 51200
    big = nc.alloc_sbuf_tensor("big", [128, BIGW], F32)
    ps_all = nc.alloc_psum_tensor("ps_all", [128, 4096], F32)
    PB = [ps_all[:, i * 512:(i + 1) * 512] for i in range(8)]
    PBb = [ps_all[:, i * 512:(i + 1) * 512].bitcast(BF16) for i in range(8)]
    RB = [Region("bank%d" % i, excl=True) for i in range(8)]
    bank_rr = [0]

    def nb():
        b = bank_rr[0]
        bank_rr[0] = (b + 1) % 8
        return b

    def view(off, shape, dt=F32):
        assert off % 4 == 0 and shape[0] == 128
        n = _prod(shape[1:])
        esz = 4 if dt == F32 else 2
        words = (n * esz + 3) // 4
        assert off // 4 + words <= BIGW, (off, shape)
        ap = big[:, off // 4: off // 4 + words]
        if dt != F32:
            ap = ap.bitcast(dt)
            if ap.shape[1] != n:
                ap = ap[:, 0:n]
        if len(shape) == 3:
            ap = ap.rearrange("p (a b) -> p a b", a=shape[1])
        elif len(shape) == 4:
            ap = ap.rearrange("p (a b c) -> p a b c", a=shape[1], b=shape[2])
        elif len(shape) == 5:
            ap = ap.rearrange("p (a b c d) -> p a b c d", a=shape[1], b=shape[2], c=shape[3])
        return ap

    KB = 1024
    X_OFF = 0
    C_OFF = 64 * KB
    P_OFF = 88 * KB

    class Alloc:
        def __init__(self, base, limit):
            self.base = base; self.cur = base; self.limit = limit

        def __call__(self, shape, dt=F32):
            n = _prod(shape[1:]) * (4 if dt == F32 else 2)
            n = (n + 31) // 32 * 32
            off = self.cur
            self.cur += n
            assert self.cur <= self.limit, ("arena overflow", self.cur - self.limit)
            return view(off, shape, dt)

        def at(self, off_kb):
            self.cur = self.base + int(off_kb * KB)

    CA = Alloc(C_OFF, P_OFF)
    PA = Alloc(P_OFF, BIGW * 4)

    def act(out, in_, func, reads, writes, **kw):
        return S.op("act", lambda e: e.activation(out=out, in_=in_, func=func, **kw), reads, writes)

    def tt(eng, out, in0, in1, op, reads, writes):
        return S.op(eng, lambda e: e.tensor_tensor(out=out, in0=in0, in1=in1, op=op), reads, writes)

    def ts(eng, out, in0, s1, op0, reads, writes, s2=None, op1=None):
        if op1 is None:
            return S.op(eng, lambda e: e.tensor_scalar(out=out, in0=in0, scalar1=s1, scalar2=None, op0=op0), reads, writes)
        return S.op(eng, lambda e: e.tensor_scalar(out=out, in0=in0, scalar1=s1, scalar2=s2, op0=op0, op1=op1), reads, writes)

    def stt(eng, out, in0, scalar, in1, op0, op1, reads, writes):
        return S.op(eng, lambda e: e.scalar_tensor_tensor(out=out, in0=in0, scalar=scalar, in1=in1, op0=op0, op1=op1), reads, writes)

    def cp(eng, out, in_, reads, writes):
        if eng == "act":
            return S.op("act", lambda e: e.copy(out=out, in_=in_), reads, writes)
        return S.op(eng, lambda e: e.tensor_copy(out=out, in_=in_), reads, writes)

    def mm(out, lhsT, rhs, start, stop, reads, writes, sig=True):
        return S.op("pe", lambda e: e.matmul(out, lhsT=lhsT, rhs=rhs, start=start, stop=stop), reads, writes, sig=sig)

    def chk(tag):
        if stop == tag:
            raise _Stop()

    def dump(name, ap, reads, dt=F32):
        if name not in dbg:
            return
        d = dout("dbg_" + name, list(ap.shape), dt)
        dbg_outs[name] = d
        S.dma("sp", d, ap, reads=reads)

    def rstd_of(ss_ap, n, r_ss, eps_ap):
        act(ss_ap, ss_ap, AF.Ln, [r_ss], [r_ss], scale=1.0 / n, bias=eps_ap)
        act(ss_ap, ss_ap, AF.Exp, [r_ss], [r_ss], scale=-0.5)

    X = view(X_OFF, [128, NT, D]); RX = [Region("x%d" % t) for t in range(NT)]
    ident = CA([128, 128], BF16); r_ident = Region("ident")
    cst = CA([128, 3, 128]); r_cst = Region("cst")
    epsc = CA([128, 1]); r_eps = Region("eps")
    gates = CA([128, 2, D]); r_gates = [Region("g1"), Region("g2")]
    colmod = CA([128, 48]); r_colmod = Region("colmod")
    adab_col = CA([128, 48]); r_adabc = Region("adabc")
    ngc = CA([128, 2, 8]); r_ngc = Region("ngc")
    GS = CA([128, 4, 8]); r_GS = Region("GS")
    cos_sb = CA([128, NKT, 16]); sin_sb = CA([128, NKT, 16]); r_trig = Region("trig")
    qgb = CA([128, QL]); kvgb = CA([128, KVL]); qhgb = CA([128, QKH]); khgb = CA([128, QKH]); r_gb = Region("gb")
    cwc = CA([128, 2, 3]); r_cwc = Region("cwc")
    condc = CA([128, 8]); r_cond = Region("cond")
    sc_col = CA([128, 8], BF16); r_sccol = Region("sccol")
    lgall = CA([128, 8]); r_lg = Region("lg")
    lgcol = CA([128, 2, 2]); r_lgcol = Region("lgcol")
    Mtab = CA([128, RH, 128]); r_M = Region("M")
    qdtab = CA([128, 2, 2, 128]); r_qd = Region("qdtab")
    kdtab = CA([128, 2, RH]); r_kd = Region("kdtab")
    cdcol = CA([128, 2, 2]); r_cd = Region("cdcol")
    keepfb = CA([128, 2, NT]); r_keep = Region("keep")
    aktab = CA([128, 2, 2, NT]); r_ak = Region("ak")
    smallscr = CA([128, 64]); r_scr = Region("scr")
    tri = CA([128, 2, 128]); r_tri = Region("tri")
    nbt = CA([128, 2, 8]); r_nbt = Region("nbt")
    print("const arena used", CA.cur - C_OFF, "of", P_OFF - C_OFF)

    xv = x_d.rearrange("(t p) d -> p t d", p=128)
    S.op("pool", lambda e: e.memset(ident, 0.0), writes=[r_ident])
    S.op("pool", lambda e: e.affine_select(out=ident, in_=ident, pattern=[[-1, 128]], compare_op=ALU.not_equal,
                                           fill=1.0, base=0, channel_multiplier=1), reads=[r_ident], writes=[r_ident])
    S.op("dve", lambda e: e.memset(epsc, EPS), writes=[r_eps])
    S.dma("act", cst, cst_d, writes=[r_cst])
    S.dma("act", condc, cond_d, writes=[r_cond])
    S.dma("act", cos_sb, cos_d.rearrange("(t p) f -> p t f", p=128), writes=[r_trig])
    S.dma("act", sin_sb, sin_d.rearrange("(t p) f -> p t f", p=128), writes=[r_trig])
    S.dma("act", nbt[:, 0, :], nbL_d.partition_broadcast(128), writes=[r_nbt])
    S.dma("act", nbt[:, 1, :], nbR_d.partition_broadcast(128), writes=[r_nbt])
    S.dma("act", keepfb[:, 0, :], keepf_d.partition_broadcast(128), writes=[r_keep])
    S.dma("act", keepfb[:, 1, :], keepb_d.partition_broadcast(128), writes=[r_keep])
    act(sc_col, condc, AF.Silu, [r_cond], [r_sccol])
    ts("dve", tri[:, 0, :], cst[:, 0, :], 0.0, ALU.is_ge, [r_cst], [r_tri])
    ts("dve", tri[:, 1, :], cst[:, 0, :], 0.0, ALU.is_le, [r_cst], [r_tri])

    try:
      for l in range(L):
          S.barrier()
          PA.at(0)
          sc_b = PA([128, 8, 128], BF16); r_scb = Region("scb")
          NAP = 12
          apiece = [PA([128, 8, 512], BF16) for _ in range(NAP)]; r_ap = [Region("ap%d" % i) for i in range(NAP)]
          abp = [PA([128, 512]) for _ in range(2)]; r_abp = [Region("abp0"), Region("abp1")]
          cp("dve", sc_b, sc_col.unsqueeze(2).to_broadcast([128, 8, 128]), [r_sccol], [r_scb])
          S.dma("act", adab_col, adab_pc_d[l], writes=[r_adabc])
          S.dma("act", ngc[:, 0, :], n1g_d[l], writes=[r_ngc])
          S.dma("act", ngc[:, 1, :], n2g_d[l], writes=[r_ngc])
          S.dma("act", qgb, qng_d[l].partition_broadcast(128), writes=[r_gb])
          S.dma("act", kvgb, kvng_d[l].partition_broadcast(128), writes=[r_gb])
          S.dma("act", qhgb, qhg_d[l].partition_broadcast(128), writes=[r_gb])
          S.dma("act", khgb, khg_d[l].partition_broadcast(128), writes=[r_gb])
          S.dma("act", cwc, cw_d[l], writes=[r_cwc])
          S.dma("act", lgall[:, 0:4], rdf_d[l].partition_broadcast(128), writes=[r_lg])
          S.dma("act", lgall[:, 4:8], rdb_d[l].partition_broadcast(128), writes=[r_lg])
          adav = ada_w_d[l].rearrange("(c p) n -> p c n", p=128)
          colbank = nb()
          gi = 0
          for pc in range(NAP):
              S.dma("pool", apiece[pc], adav[:, :, pc * 512:(pc + 1) * 512], writes=[r_ap[pc]])
          if l == 0:
              for t in range(NT):
                  S.dma("sp", X[:, t, :], xv[:, t, :], reads=[r_ap[8]], writes=[RX[t]])
          for pc in range(12):
              bi = pc % NAP
              if pc >= NAP:
                  S.dma("pool", apiece[bi], adav[:, :, pc * 512:(pc + 1) * 512], writes=[r_ap[bi]])
              if pc in (4, 5, 10, 11):
                  g = 0 if pc < 6 else 1
                  half = pc % 2 if pc < 6 else (pc - 10)
                  S.dma("act", abp[gi % 2], ada_b_d[l, pc * 512:(pc + 1) * 512].partition_broadcast(128), writes=[r_abp[gi % 2]])
                  b = nb()
                  if b == colbank:
                      b = nb()
                  for k in range(8):
                      mm(PB[b], sc_b[:, k, :], apiece[bi][:, k, :], k == 0, k == 7, [r_scb, r_ap[bi]], [RB[b]], sig=(k == 7))
                  tt("dve", gates[:, g, half * 512:(half + 1) * 512], PB[b], abp[gi % 2], ALU.add,
                     [RB[b], r_abp[gi % 2]], [r_gates[g]])
                  gi += 1
              else:
                  for j in range(4):
                      col = pc * 4 + j
                      for k in range(8):
                          mm(PB[colbank][:, col:col + 1], apiece[bi][:, k, j * 128:(j + 1) * 128], sc_col[:, k:k + 1],
                             k == 0, k == 7, [r_sccol, r_ap[bi]], [RB[colbank]], sig=(k == 7))
          tt("dve", colmod, PB[colbank][:, 0:48], adab_col, ALU.add, [RB[colbank], r_adabc], [r_colmod])
          stt("dve", GS[:, 0, :], colmod[:, 8:16], 1.0, ngc[:, 0, :], ALU.add, ALU.mult, [r_colmod, r_ngc], [r_GS])
          cp("dve", GS[:, 1, :], colmod[:, 0:8], [r_colmod], [r_GS])
          stt("dve", GS[:, 2, :], colmod[:, 32:40], 1.0, ngc[:, 1, :], ALU.add, ALU.mult, [r_colmod, r_ngc], [r_GS])
          cp("dve", GS[:, 3, :], colmod[:, 24:32], [r_colmod], [r_GS])
          dump("GS%d" % l, GS, [r_GS]); dump("gates%d" % l, gates, r_gates)

          act(lgall, lgall, AF.Exp, [r_lg], [r_lg], scale=-1.0)
          act(lgall, lgall, AF.Ln, [r_lg], [r_lg], bias=1.0)
          ts("dve", lgall, lgall, -1.0, ALU.mult, [r_lg], [r_lg])
          for d_ in range(2):
              for pr in range(2):
                  for hh in range(2):
                      h_ = 2 * pr + hh
                      cp("dve", lgcol[hh * 64:(hh + 1) * 64, d_, pr:pr + 1], lgall[hh * 64:(hh + 1) * 64, d_ * 4 + h_:d_ * 4 + h_ + 1],
                         [r_lg], [r_lgcol])
          dpos = smallscr
          tmpA = PA([128, 128]); tmpB = PA([128, 128]); tmpC = PA([128, 128]); r_tmp = Region("tmpM")
          ts("dve", tmpA, cst[:, 0, :], 0.0, ALU.max, [r_cst], [r_tmp])
          ts("dve", tmpB, cst[:, 0, :], -1.0, ALU.mult, [r_cst], [r_tmp], s2=0.0, op1=ALU.max)
          for h_ in range(RH):
              act(tmpC, tmpA, AF.Exp, [r_tmp, r_lg], [r_tmp], scale=lgall[:, h_:h_ + 1])
              tt("dve", Mtab[:, h_, :], tmpC, tri[:, 0, :], ALU.mult, [r_tmp, r_tri], [r_M])
              act(tmpC, tmpB, AF.Exp, [r_tmp, r_lg], [r_tmp], scale=lgall[:, 4 + h_:5 + h_])
              tt("dve", tmpC, tmpC, tri[:, 1, :], ALU.mult, [r_tmp, r_tri], [r_tmp])
              tt("dve", Mtab[:, h_, :], Mtab[:, h_, :], tmpC, ALU.add, [r_tmp, r_M], [r_M])
          for pr in range(2):
              act(qdtab[:, 0, pr, :], cst[:, 1, :], AF.Exp, [r_cst, r_lgcol], [r_qd], scale=lgcol[:, 0, pr:pr + 1])
              act(qdtab[:, 1, pr, :], cst[:, 2, :], AF.Exp, [r_cst, r_lgcol], [r_qd], scale=lgcol[:, 1, pr:pr + 1])
          ts("dve", smallscr[:, 0:1], cst[:, 0, 0:1], -1.0, ALU.mult, [r_cst], [r_scr])
          act(kdtab[:, 0, :], lgall[:, 0:4], AF.Exp, [r_lg, r_cst], [r_kd], scale=cst[:, 0, 127:128])
          act(kdtab[:, 1, :], lgall[:, 4:8], AF.Exp, [r_lg, r_scr], [r_kd], scale=smallscr[:, 0:1])
          act(cdcol.rearrange("p a b -> p (a b)"), lgcol.rearrange("p a b -> p (a b)"), AF.Exp, [r_lgcol], [r_cd], scale=128.0)
          for d_ in range(2):
              for pr in range(2):
                  ts("dve", aktab[:, d_, pr, :], keepfb[:, d_, :], cdcol[:, d_, pr:pr + 1], ALU.mult, [r_keep, r_cd], [r_ak])
          dump("Mtab%d" % l, Mtab, [r_M]); dump("qdtab%d" % l, qdtab, [r_qd]); dump("kdtab%d" % l, kdtab, [r_kd])
          chk("mod%d" % l)

          norm_ctr = [0]

          def norm_phase(gidx, tiles, hT, r_hT, xn, r_xn, junk, r_junk, ssb, r_ss):
              ng = len(tiles) // 4
              for g in range(ng):
                  tl = tiles[g * 4:(g + 1) * 4]
                  for i, t in enumerate(tl):
                      act(junk, X[:, t, :], AF.Square, [RX[t]], [r_junk, r_ss], accum_out=ssb[:, g * 4 + i:g * 4 + i + 1])
                  rstd_of(ssb[:, g * 4:(g + 1) * 4], D, r_ss, epsc)
                  b0 = 4 * (norm_ctr[0] % 2)
                  norm_ctr[0] += 1
                  for i, t in enumerate(tl):
                      ts("dve", xn[i], X[:, t, :], ssb[:, g * 4 + i:g * 4 + i + 1], ALU.mult, [RX[t], r_ss], [r_xn[i]])
                      for c in range(8):
                          bk = b0 + c // 2
                          o = PBb[bk][:, (c % 2) * 512 + i * 128:(c % 2) * 512 + (i + 1) * 128]
                          S.op("pe", lambda e, o=o, c=c, i=i: e.transpose(o, xn[i][:, c * 128:(c + 1) * 128], ident),
                               [r_xn[i], r_ident], [RB[bk]], sig=(c % 2 == 1))
                  for c in range(8):
                      bk = b0 + c // 2
                      src = PBb[bk][:, (c % 2) * 512:(c % 2 + 1) * 512]
                      dst = hT[:, c, g * 512:(g + 1) * 512]
                      if c % 2 == 0:
                          act(dst, src, AF.Identity, [RB[bk], r_GS], [r_hT[g]], scale=GS[:, gidx, c:c + 1], bias=GS[:, gidx + 1, c:c + 1])
                      else:
                          ts("dve", dst, src, GS[:, gidx, c:c + 1], ALU.mult, [RB[bk], r_GS], [r_hT[g]],
                             s2=GS[:, gidx + 1, c:c + 1], op1=ALU.add)

          S.barrier()
          PA.at(0)
          hT = PA([128, 8, T], BF16); r_hT = [Region("hT%d" % g) for g in range(4)]
          qlnT = PA([128, 2, T], BF16); r_qlnT = [Region("qlnT%d" % t) for t in range(NT)]
          ckvT = PA([128, NK], BF16); r_ckvT = [Region("ckvT%d" % t) for t in range(NKT)]
          kpe_sb = PA([128, NKT, ROPE]); r_kpe = [Region("kpe%d" % t) for t in range(NKT)]
          assert PA.cur - P_OFF <= 48 * KB, PA.cur - P_OFF
          PA.at(48)
          mixT = PA([128, 8, T], BF16)
          r_attnT = [[Region("attnT%d_%d" % (c, b)) for b in range(4)] for c in range(4)]
          r_convT = [Region("convT0"), Region("convT1")]
          r_retT = [[Region("retT%d_%d" % (c, t)) for t in range(NT)] for c in range(2)]
          PA.at(80)
          xn = [PA([128, D], BF16) for _ in range(4)]; r_xn = [Region("xn%d" % i) for i in range(4)]
          junk = PA([128, D], BF16); r_junk = Region("junk")
          ssb = PA([128, NT]); r_ss = Region("ss")
          wA = PA([128, 8, 416], BF16); r_wA = Region("wA")
          w_inv = w_in_d[l].rearrange("(c p) n -> p c n", p=128)
          S.dma("pool", wA, w_inv[:, :, 0:416], writes=[r_wA])
          norm_phase(0, list(range(NT)), hT, r_hT, xn, r_xn, junk, r_junk, ssb, r_ss)
          dump("hT%d" % l, hT, r_hT, BF16)
          chk("norm1%d" % l)

          qn = [PA([128, QL], BF16) for _ in range(2)]; r_qn = [Region("qn0"), Region("qn1")]
          ckvf = [PA([128, KVL]) for _ in range(2)]; r_ckvf = [Region("ckvf0"), Region("ckvf1")]
          ckvb = [PA([128, KVL], BF16) for _ in range(2)]; r_ckvb = [Region("ckvb0"), Region("ckvb1")]
          ss2 = PA([128, NT, 2]); r_ss2 = Region("ss2")
          cctx = PA([128, 2, KVL]); r_cctx = Region("cctx")
          S.dma("act", cctx, cckv_d[l].rearrange("(t p) f -> p t f", p=128), writes=[r_cctx])
          S.dma("act", kpe_sb[:, NT:NKT, :], ckpe_d[l].rearrange("(t p) f -> p t f", p=128), writes=[r_kpe[16], r_kpe[17]])
          ockv_v = ockv_d[l].rearrange("(t p) f -> p t f", p=128)
          for t in range(NT):
              b = nb()
              for k in range(8):
                  mm(PB[b][:, 0:416], hT[:, k, t * 128:(t + 1) * 128], wA[:, k, :], k == 0, k == 7,
                     [r_hT[t // 4], r_wA], [RB[b]], sig=(k == 7))
              i2 = t % 2
              act(junk[:, 0:QL], PB[b][:, 0:QL], AF.Square, [RB[b]], [r_junk, r_ss2], accum_out=ss2[:, t, 0:1])
              act(junk[:, 0:KVL], PB[b][:, C_KV:C_KV + KVL], AF.Square, [RB[b]], [r_junk, r_ss2], accum_out=ss2[:, t, 1:2])
              act(ss2[:, t, 0:1], ss2[:, t, 0:1], AF.Ln, [r_ss2], [r_ss2], scale=1.0 / QL, bias=epsc)
              act(ss2[:, t, 1:2], ss2[:, t, 1:2], AF.Ln, [r_ss2], [r_ss2], scale=1.0 / KVL, bias=epsc)
              act(ss2[:, t, :], ss2[:, t, :], AF.Exp, [r_ss2], [r_ss2], scale=-0.5)
              stt("dve", qn[i2], PB[b][:, 0:QL], ss2[:, t, 0:1], qgb, ALU.mult, ALU.mult, [RB[b], r_ss2, r_gb], [r_qn[i2]])
              stt("dve", ckvf[i2], PB[b][:, C_KV:C_KV + KVL], ss2[:, t, 1:2], kvgb, ALU.mult, ALU.mult, [RB[b], r_ss2, r_gb], [r_ckvf[i2]])
              cp("act", kpe_sb[:, t, :], PB[b][:, C_KPE:C_KPE + ROPE], [RB[b]], [r_kpe[t]])
              cp("dve", ckvb[i2], ckvf[i2], [r_ckvf[i2]], [r_ckvb[i2]])
              S.dma("sp", ockv_v[:, t, :], ckvf[i2], reads=[r_ckvf[i2]])
              b2 = nb()
              for c in range(2):
                  S.op("pe", lambda e, c=c: e.transpose(PBb[b2][:, c * 128:(c + 1) * 128], qn[i2][:, c * 128:(c + 1) * 128], ident),
                       [r_qn[i2], r_ident], [RB[b2]], sig=False)
              S.op("pe", lambda e: e.transpose(PBb[b2][:, 256:384], ckvb[i2], ident), [r_ckvb[i2], r_ident], [RB[b2]])
              cp("act", qlnT[:, :, t * 128:(t + 1) * 128], PBb[b2][:, 0:256].rearrange("p (c n) -> p c n", c=2), [RB[b2]], [r_qlnT[t]])
              cp("dve", ckvT[:, t * 128:(t + 1) * 128], PBb[b2][:, 256:384], [RB[b2]], [r_ckvT[t]])
          S.dma("sp", okpe_d[l].rearrange("(t p) f -> p t f", p=128), kpe_sb[:, 0:NT, :], reads=r_kpe[0:NT])
          for j in range(2):
              i2 = j
              cp("dve", ckvb[i2], cctx[:, j, :], [r_cctx], [r_ckvb[i2]])
              b2 = nb()
              S.op("pe", lambda e: e.transpose(PBb[b2][:, 0:128], ckvb[i2], ident), [r_ckvb[i2], r_ident], [RB[b2]])
              cp("dve", ckvT[:, (NT + j) * 128:(NT + j + 1) * 128], PBb[b2][:, 0:128], [RB[b2]], [r_ckvT[NT + j]])
          dump("qlnT%d" % l, qlnT, r_qlnT, BF16); dump("ckvT%d" % l, ckvT, r_ckvT, BF16)
          chk("projA%d" % l)

          wC_pre = view(P_OFF + 80 * KB, [128, 8, 3, 128], BF16); r_wC = Region("wC")
          for a in range(3):
              S.dma("pool", wC_pre[:, :, a, :], w_inv[:, :, C_GB + a * 256:C_GB + a * 256 + 128],
                    writes=[r_wC, r_xn[0], r_xn[1], r_xn[2]])
          S.barrier()
          PA.at(48)
          gbs = PA([128, T]); r_gbs = Region("gbs")
          acc = PA([128, T]); r_acc = Region("acc")
          PA.at(80)
          wC = PA([128, 8, 3, 128], BF16)
          u_h = [PA([128, T + 2]) for _ in range(2)]; r_u = [Region("u0"), Region("u1")]
          gcs = [PA([128, 512]) for _ in range(2)]; r_gcs = [Region("gcs0"), Region("gcs1")]
          ctmp = PA([128, 2, 8]); r_ctmp = Region("ctmp")
          for c in range(2):
              S.op("pool", lambda e, c=c: e.memset(u_h[c][:, 0:1], 0.0), writes=[r_u[c]])
              S.op("pool", lambda e, c=c: e.memset(u_h[c][:, T + 1:T + 2], 0.0), writes=[r_u[c]])
          cb = 0
          for c in range(2):
              uh = u_h[c]
              for a in range(3):
                  if c == 0:
                      continue
                  S.dma("pool", wC[:, :, a, :], w_inv[:, :, C_GB + a * 256 + c * 128:C_GB + a * 256 + (c + 1) * 128], writes=[r_wC])
              for tb in range(4):
                  bks = [(cb * 3 + a) % 6 for a in range(3)]
                  cb += 1
                  for a in (1, 2, 0):
                      bk = bks[a]
                      for k in range(8):
                          mm(PB[bk], wC[:, k, a, :], hT[:, k, tb * 512:(tb + 1) * 512], k == 0, k == 7,
                             [r_wC, r_hT[tb]], [RB[bk]], sig=(k == 7))
                  bg, bc, bx = bks
                  cp("act", gcs[tb % 2], PB[bc], [RB[bc]], [r_gcs[tb % 2]])
                  tt("dve", uh[:, 1 + tb * 512:1 + (tb + 1) * 512], PB[bx], gcs[tb % 2], ALU.mult, [RB[bx], r_gcs[tb % 2]], [r_u[c]])
                  cp("act", gbs[:, tb * 512:(tb + 1) * 512], PB[bg], [RB[bg]], [r_gbs])
              ts("dve", acc, uh[:, 0:T], cwc[:, c, 0:1], ALU.mult, [r_u[c], r_cwc], [r_acc])
              stt("dve", acc, uh[:, 1:T + 1], cwc[:, c, 1:2], acc, ALU.mult, ALU.add, [r_u[c], r_cwc, r_acc], [r_acc])
              stt("dve", acc, uh[:, 2:T + 2], cwc[:, c, 2:3], acc, ALU.mult, ALU.add, [r_u[c], r_cwc, r_acc], [r_acc])
              accv = acc.rearrange("p (k c) -> p k c", c=256)
              uL = uh[:, 0:T].rearrange("p (k c) -> p k c", c=256)[:, :, 0]
              uR = uh[:, 2:T + 2].rearrange("p (k c) -> p k c", c=256)[:, :, 255]
              tt("dve", ctmp[:, 0, :], uL, nbt[:, 0, :], ALU.mult, [r_u[c], r_nbt], [r_ctmp])
              tt("dve", ctmp[:, 1, :], uR, nbt[:, 1, :], ALU.mult, [r_u[c], r_nbt], [r_ctmp])
              stt("dve", accv[:, :, 0], ctmp[:, 0, :], cwc[:, c, 0:1], accv[:, :, 0], ALU.mult, ALU.add, [r_ctmp, r_cwc, r_acc], [r_acc])
              stt("dve", accv[:, :, 255], ctmp[:, 1, :], cwc[:, c, 2:3], accv[:, :, 255], ALU.mult, ALU.add, [r_ctmp, r_cwc, r_acc], [r_acc])
              tt("dve", mixT[:, 4 + c, :], acc, gbs, ALU.mult, [r_acc, r_gbs], [r_convT[c]])
          dump("convT%d" % l, mixT[:, 4:6, :], r_convT, BF16)
          chk("conv%d" % l)

          S.barrier()
          PA.at(48)
          rqT = PA([128, 2, T], BF16); rkT = PA([128, 2, T], BF16)
          r_rqT = [[Region("rqT%d_%d" % (c, b)) for b in range(4)] for c in range(2)]
          r_rkT = [[Region("rkT%d_%d" % (c, b)) for b in range(4)] for c in range(2)]
          PA.at(80)
          rk_tok = PA([128, NT, 256], BF16); rv_tok = PA([128, NT, 256], BF16)
          r_rktok = [Region("rktok%d" % t) for t in range(NT)]; r_rvtok = [Region("rvtok%d" % t) for t in range(NT)]
          wR = [PA([128, 8, 512], BF16) for _ in range(2)]; r_wR = [Region("wR0"), Region("wR1")]
          S.dma("pool", wR[0], w_inv[:, :, C_RQ:C_RQ + 512], writes=[r_wR[0]])
          S.dma("pool", wR[1], w_inv[:, :, C_RK:C_RK + 512], writes=[r_wR[1]])
          bi = 0
          for cc in range(4):
              for tb in range(4):
                  bk = bi % 8; bi += 1
                  for k in range(8):
                      mm(PB[bk], wR[0][:, k, cc * 128:(cc + 1) * 128], hT[:, k, tb * 512:(tb + 1) * 512], k == 0, k == 7,
                         [r_wR[0], r_hT[tb]], [RB[bk]], sig=(k == 7))
                  if cc < 2:
                      cp("act" if tb % 2 == 0 else "dve", rqT[:, cc, tb * 512:(tb + 1) * 512], PB[bk], [RB[bk]], [r_rqT[cc][tb]])
                  else:
                      if tb % 2 == 0:
                          act(rkT[:, cc - 2, tb * 512:(tb + 1) * 512], PB[bk], AF.Copy, [RB[bk]], [r_rkT[cc - 2][tb]], scale=0.125)
                      else:
                          ts("dve", rkT[:, cc - 2, tb * 512:(tb + 1) * 512], PB[bk], 0.125, ALU.mult, [RB[bk]], [r_rkT[cc - 2][tb]])
          for t in range(NT):
              bk = bi % 8; bi += 1
              for k in range(8):
                  mm(PB[bk], hT[:, k, t * 128:(t + 1) * 128], wR[1][:, k, :], k == 0, k == 7,
                     [r_wR[1], r_hT[t // 4]], [RB[bk]], sig=(k == 7))
              act(rk_tok[:, t, :], PB[bk][:, 0:256], AF.Copy, [RB[bk]], [r_rktok[t]], scale=0.125)
              cp("dve", rv_tok[:, t, :], PB[bk][:, 256:512], [RB[bk]], [r_rvtok[t]])
          S.dma("pool", wR[0][:, :, 0:256], w_inv[:, :, C_RG:C_RG + 256], writes=[r_wR[0]])
          for cc in range(2):
              for tb in range(4):
                  bk = bi % 8; bi += 1
                  for k in range(8):
                      mm(PB[bk], wR[0][:, k, cc * 128:(cc + 1) * 128], hT[:, k, tb * 512:(tb + 1) * 512], k == 0, k == 7,
                         [r_wR[0], r_hT[tb]], [RB[bk]], sig=(k == 7))
                  act(mixT[:, 6 + cc, tb * 512:(tb + 1) * 512], PB[bk], AF.Silu, [RB[bk]], r_retT[cc][tb * 4:(tb + 1) * 4])
          dump("rqT%d" % l, rqT, r_rqT[0] + r_rqT[1], BF16); dump("rkT%d" % l, rkT, r_rkT[0] + r_rkT[1], BF16)
          dump("rv_tok%d" % l, rv_tok, r_rvtok, BF16); dump("rk_tok%d" % l, rk_tok, r_rktok, BF16)
          chk("retproj%d" % l)

          S.barrier()
          PA.at(0)
          U_sb = PA([128, NT, 2, 2, 64]); r_U = [[Region("U%d_%d" % (d_, j)) for j in range(NT)] for d_ in range(2)]
          Sin = PA([128, 2, NT, 2, 64], BF16); r_Sin = [[Region("Sin%d_%d" % (d_, j)) for j in range(NT)] for d_ in range(2)]
          stout = PA([128, 2, 8, 2, 64]); r_stout = [Region("stout0"), Region("stout1")]
          PA.at(96)
          ND = 3
          kd = [PA([128, 2, 256], BF16) for _ in range(2)]; r_kdb = [Region("kd0"), Region("kd1")]
          Pm = [PA([128, 4, 128], BF16) for _ in range(ND)]; r_Pm = [Region("Pm%d" % i) for i in range(ND)]
          qd = [PA([128, 2, 2, 128], BF16) for _ in range(ND)]; r_qdb = [Region("qd%d" % i) for i in range(ND)]
          Sst = PA([128, 2, 2, 64]); r_Sst = [Region("Sst0"), Region("Sst1")]
          o_sb = [PA([128, 256]) for _ in range(ND)]; r_osb = [Region("osb%d" % i) for i in range(ND)]
          sqb = [PA([128, 256]) for _ in range(2)]; r_sqb = [Region("sqb0"), Region("sqb1")]
          rss = PA([128, NT, 4]); r_rss = [Region("rss%d" % j) for j in range(NT)]
          onb = [PA([128, 256], BF16) for _ in range(ND)]; r_onb = [Region("on%d" % i) for i in range(ND)]
          assert PA.cur - P_OFF <= 112 * KB, PA.cur - P_OFF
          for hh in range(2):
              for a in range(2):
                  S.dma("act", Sst[hh * 64:(hh + 1) * 64, a], s0_d[l, a].rearrange("(r hh) d v -> hh d r v", hh=2)[hh], writes=r_Sst)
          def u_step(d_, j, n):
              i2 = n % 2
              tt("pool" if d_ == 0 else "dve", kd[i2][:, d_, :].rearrange("p (h n) -> p h n", h=4), rk_tok[:, j, :].rearrange("p (h n) -> p h n", h=4),
                 kdtab[:, d_, :].unsqueeze(2).to_broadcast([128, 4, 64]), ALU.mult, [r_rktok[j], r_kd], [r_kdb[i2]])
              bk = n % 2
              for pr in range(2):
                  mm(PB[bk][:, pr * 128:(pr + 1) * 128], kd[i2][:, d_, pr * 128:(pr + 1) * 128], rv_tok[:, j, pr * 128:(pr + 1) * 128],
                     True, True, [r_kdb[i2], r_rvtok[j]], [RB[bk]], sig=(pr == 1))
              bv = PB[bk][:, 0:256].rearrange("p (a n) -> p a n", a=2)
              cp("act", U_sb[0:64, j, d_], bv[0:64, :, 0:64], [RB[bk]], [r_U[d_][j]])
              cp("act", U_sb[64:128, j, d_], bv[64:128, :, 64:128], [RB[bk]], [r_U[d_][j]])

          def chain_step(d_, j, eng):
              tt(eng, Sin[:, d_, j], Sst[:, d_], keepfb[:, d_, j:j + 1].unsqueeze(2).to_broadcast([128, 2, 64]), ALU.mult,
                 [r_Sst[d_], r_keep], [r_Sin[d_][j]])
              tt(eng, Sst[:, d_], Sst[:, d_], aktab[:, d_, :, j:j + 1].to_broadcast([128, 2, 64]), ALU.mult,
                 [r_Sst[d_], r_ak], [r_Sst[d_]])
              tt(eng, Sst[:, d_], Sst[:, d_], U_sb[:, j, d_], ALU.add, [r_Sst[d_], r_U[d_][j]], [r_Sst[d_]])
              if (d_ == 0 and j % 2 == 1) or (d_ == 1 and j % 2 == 0):
                  cp(eng, stout[:, d_, j // 2], Sst[:, d_], [r_Sst[d_]], [r_stout[d_]])

          n_ = 0
          for i in range(NT + 1):
              if i < NT:
                  u_step(0, i, n_); n_ += 1
                  u_step(1, NT - 1 - i, n_); n_ += 1
              if i >= 1:
                  chain_step(0, i - 1, "pool")
                  chain_step(1, NT - i, "dve")
          for d_ in range(2):
              for hh in range(2):
                  for pr in range(2):
                      S.dma("sp", ost_d[l, d_, :, 2 * pr + hh].rearrange("s d v -> d s v"), stout[hh * 64:(hh + 1) * 64, d_, :, pr, :],
                            reads=[r_stout[d_]])
          def stA(j):
              i3 = j % ND; tb = j // 4
              for par in range(2):
                  bs = 2 * (j % 2) + par
                  for h_ in (par, par + 2):
                      rows = slice((h_ % 2) * 64, (h_ % 2 + 1) * 64)
                      mm(PB[bs][:, (h_ // 2) * 128:(h_ // 2 + 1) * 128], rkT[rows, h_ // 2, j * 128:(j + 1) * 128],
                         rqT[rows, h_ // 2, j * 128:(j + 1) * 128],
                         True, True, [r_rkT[h_ // 2][tb], r_rqT[h_ // 2][tb]], [RB[bs]], sig=(h_ == par + 2))
              for par in range(2):
                  bs = 2 * (j % 2) + par
                  tt("dve", Pm[i3][:, par::2, :], PB[bs][:, 0:256].rearrange("p (h n) -> p h n", h=2), Mtab[:, par::2, :], ALU.mult,
                     [RB[bs], r_M], [r_Pm[i3]])
              for d_ in range(2):
                  tt("pool", qd[i3][:, d_], rqT[:, :, j * 128:(j + 1) * 128], qdtab[:, d_], ALU.mult,
                     [r_rqT[0][tb], r_rqT[1][tb], r_qd], [r_qdb[i3]])

          def stB(j):
              i3 = j % ND
              bo = 4 + j % 2
              for h_ in range(RH):
                  rows = slice((h_ % 2) * 64, (h_ % 2 + 1) * 64)
                  oap = PB[bo][:, h_ * 64:(h_ + 1) * 64]
                  mm(oap, Pm[i3][:, h_, :], rv_tok[:, j, h_ * 64:(h_ + 1) * 64], True, False, [r_Pm[i3], r_rvtok[j]], [RB[bo]], sig=False)
                  mm(oap, qd[i3][rows, 0, h_ // 2, :], Sin[rows, 0, j, h_ // 2, :], False, False, [r_qdb[i3], r_Sin[0][j]], [RB[bo]], sig=False)
                  mm(oap, qd[i3][rows, 1, h_ // 2, :], Sin[rows, 1, j, h_ // 2, :], False, True, [r_qdb[i3], r_Sin[1][j]], [RB[bo]], sig=(h_ == 3))
              cp("act", o_sb[i3], PB[bo][:, 0:256], [RB[bo]], [r_osb[i3]])
              tt("dve", sqb[j % 2], o_sb[i3], o_sb[i3], ALU.mult, [r_osb[i3]], [r_sqb[j % 2]])
              S.op("dve", lambda e: e.tensor_reduce(out=rss[:, j, :], in_=sqb[j % 2].rearrange("p (h n) -> p h n", h=4), axis=AX.X, op=ALU.add),
                   [r_sqb[j % 2]], [r_rss[j]])
              rstd_of(rss[:, j, :], 64, r_rss[j], epsc)

          def stC(j):
              i3 = j % ND
              tt("dve", onb[i3].rearrange("p (h n) -> p h n", h=4), o_sb[i3].rearrange("p (h n) -> p h n", h=4),
                 rss[:, j, :].unsqueeze(2).to_broadcast([128, 4, 64]), ALU.mult, [r_osb[i3], r_rss[j]], [r_onb[i3]])
              bt = 6 + j % 2
              for c in range(2):
                  S.op("pe", lambda e, c=c: e.transpose(PBb[bt][:, c * 128:(c + 1) * 128], onb[i3][:, c * 128:(c + 1) * 128], ident),
                       [r_onb[i3], r_ident], [RB[bt]], sig=(c == 1))
              tt("dve", mixT[:, 6:8, j * 128:(j + 1) * 128], PBb[bt][:, 0:256].rearrange("p (c n) -> p c n", c=2),
                 mixT[:, 6:8, j * 128:(j + 1) * 128], ALU.mult, [RB[bt], r_retT[0][j], r_retT[1][j]], [r_retT[0][j], r_retT[1][j]])

          for i in range(NT + 2):
              if i < NT:
                  stA(i)
              if 0 <= i - 1 < NT:
                  stB(i - 1)
              if 0 <= i - 2 < NT:
                  stC(i - 2)
          dump("retT%d" % l, mixT[:, 6:8, :], r_retT[0] + r_retT[1], BF16)
          dump("U%d" % l, U_sb, r_U[0] + r_U[1]); dump("Sin%d" % l, Sin, r_Sin[0] + r_Sin[1], BF16)
          chk("ret%d" % l)

          S.barrier()
          PA.at(0)
          kT = PA([128, 4, NK], BF16); r_kT = [Region("kT%d" % t) for t in range(NKT)]; r_kTm = Region("kTm")
          pT = [PA([128, 1024], BF16) for _ in range(2)]; r_pT = [Region("pT%d" % i) for i in range(2)]
          wkv = PA([128, 512], BF16); r_wkv = Region("wkv")
          wq = PA([128, 2, 384], BF16); r_wq = Region("wq")
          NS = 3
          kcat = [PA([128, 4, QKH]) for _ in range(NS)]; r_kcat = [Region("kcat%d" % i) for i in range(NS)]
          rt = [[PA([128, 4, 2, 8]) for _ in range(2)] for _ in range(NS)]; r_rt = [Region("rt%d" % i) for i in range(NS)]
          rs4 = [PA([128, 4]) for _ in range(NS)]; r_rs4 = [Region("rs4_%d" % i) for i in range(NS)]
          assert PA.cur - P_OFF <= 32 * KB, PA.cur - P_OFF
          PA.at(80)
          Vaug = PA([128, NKT, 4, 96], BF16); r_V = [Region("V%d" % t) for t in range(NKT)]; r_Vones = Region("Vones")
          qTb = [PA([128, 4, 512], BF16) for _ in range(2)]; r_qTb = [Region("qTb0"), Region("qTb1")]; r_qTm = [Region("qTm0"), Region("qTm1")]
          rec = [PA([128, 512]) for _ in range(2)]; r_rec = [Region("rec0"), Region("rec1")]
          ones_t = PA([128, 512]); r_ones = Region("ones_t")
          jq = PA([128, QKH], BF16)
          jq2 = PA([128, QKH], BF16)
          kfin = [PA([128, 4, QKH], BF16) for _ in range(NS)]; r_kfin = [Region("kfin%d" % i) for i in range(NS)]
          rt2 = [[PA([128, 4, 2, 8]) for _ in range(2)] for _ in range(NS)]; r_rt2 = [Region("rt2_%d" % i) for i in range(NS)]
          assert PA.cur - P_OFF <= 112 * KB, PA.cur - P_OFF
          S.dma("pool", kT[96:128, :, :], ka_d.unsqueeze(1).to_broadcast([32, 4, NK]), writes=[r_kTm])
          S.op("pool", lambda e: e.memset(Vaug[:, :, :, 64:96], 1.0), writes=[r_Vones])
          S.op("pool", lambda e: e.memset(ones_t[64:96, :], -1.0), writes=[r_ones])
          wqv = wq_d[l].rearrange("(c p) n -> p c n", p=128)
          cnt = {"i": 0}

          def st1(it):
              s_ = it["s"]
              b = 4
              if it["kind"] == "k":
                  kt = it["kt"]
                  mm(PB[b], ckvT[:, kt * 128:(kt + 1) * 128], wkv, True, True, [r_ckvT[kt], r_wkv], [RB[b]])
                  bv = PB[b].rearrange("p (h n) -> p h n", h=4)
                  cp("act", Vaug[:, kt, :, 0:64], bv[:, :, 64:128], [RB[b]], [r_V[kt]])
                  cp("act", kcat[s_][:, :, 0:NOPE], bv[:, :, 0:NOPE], [RB[b]], [r_kcat[s_]])
                  cp("pool", kcat[s_][:, :, NOPE:QKH], kpe_sb[:, kt, :].unsqueeze(1).to_broadcast([128, 4, ROPE]), [r_kpe[kt]], [r_kcat[s_]])
              else:
                  t = it["t"]
                  for c in range(2):
                      mm(PB[b][:, 0:384], qlnT[:, c, t * 128:(t + 1) * 128], wq[:, c, :], c == 0, c == 1,
                         [r_qlnT[t], r_wq], [RB[b]], sig=(c == 1))
                  cp("dve", kcat[s_], PB[b][:, 0:384].rearrange("p (h n) -> p h n", h=4), [RB[b]], [r_kcat[s_]])

          def st2a(it):
              s_ = it["s"]
              src = kcat[s_]; r_src = r_kcat[s_]; dst = kfin[s_]; r_dst = r_kfin[s_]
              if it["kind"] == "k":
                  gain = khgb; cos_t = cos_sb[:, it["kt"], :]; sin_t = sin_sb[:, it["kt"], :]
              else:
                  gain = qhgb; cos_t = cos_sb[:, it["t"], :]; sin_t = sin_sb[:, it["t"], :]
              for hh in range(4):
                  if it["kind"] == "k":
                      act(jq, src[:, hh, :], AF.Square, [r_src], [r_rs4[s_]], accum_out=rs4[s_][:, hh:hh + 1])
                  else:
                      S.op("dve", lambda e, hh=hh: e.scalar_tensor_tensor(out=jq2, in0=src[:, hh, :], scalar=1.0, in1=src[:, hh, :],
                                                                          op0=ALU.mult, op1=ALU.mult, accum_out=rs4[s_][:, hh:hh + 1]),
                         [r_src], [r_rs4[s_]])
              tt("pool", src, src, gain.unsqueeze(1).to_broadcast([128, 4, QKH]), ALU.mult, [r_src, r_gb], [r_src])

          def st2b(it):
              s_ = it["s"]
              src = kcat[s_]; r_src = r_kcat[s_]; dst = kfin[s_]; r_dst = r_kfin[s_]
              if it["kind"] == "k":
                  cos_t = cos_sb[:, it["kt"], :]; sin_t = sin_sb[:, it["kt"], :]
              else:
                  cos_t = cos_sb[:, it["t"], :]; sin_t = sin_sb[:, it["t"], :]
              rstd_of(rs4[s_], QKH, r_rs4[s_], epsc)
              tt("dve", dst[:, :, 0:NOPE], src[:, :, 0:NOPE], rs4[s_].unsqueeze(2).to_broadcast([128, 4, NOPE]), ALU.mult,
                 [r_src, r_rs4[s_]], [r_dst])
              tt("dve", src[:, :, NOPE:QKH], src[:, :, NOPE:QKH], rs4[s_].unsqueeze(2).to_broadcast([128, 4, ROPE]), ALU.mult,
                 [r_src, r_rs4[s_]], [r_src])
              rv = src[:, :, NOPE:QKH].rearrange("p h (a b i) -> p h a b i", a=2, b=2)
              dv = dst[:, :, NOPE:QKH].rearrange("p h (a b i) -> p h a b i", a=2, b=2)
              x1 = rv[:, :, :, 0, :]; x2 = rv[:, :, :, 1, :]
              cb_ = cos_t.rearrange("p (a i) -> p a i", a=2).unsqueeze(1).to_broadcast([128, 4, 2, 8])
              sb_ = sin_t.rearrange("p (a i) -> p a i", a=2).unsqueeze(1).to_broadcast([128, 4, 2, 8])
              t0, t1 = rt[s_]
              t2, t3 = rt2[s_]
              tt("pool", t0, x1, cb_, ALU.mult, [r_src, r_trig], [r_rt[s_]])
              tt("pool", t1, x2, sb_, ALU.mult, [r_src, r_trig], [r_rt[s_]])
              tt("pool", dv[:, :, :, 0, :], t0, t1, ALU.subtract, [r_rt[s_]], [r_dst])
              tt("dve", t2, x1, sb_, ALU.mult, [r_src, r_trig], [r_rt2[s_]])
              tt("dve", t3, x2, cb_, ALU.mult, [r_src, r_trig], [r_rt2[s_]])
              tt("dve", dv[:, :, :, 1, :], t2, t3, ALU.add, [r_rt2[s_]], [r_dst])

          def st3(it):
              s_ = it["s"]
              bt = 5
              for hh in range(4):
                  S.op("pe", lambda e, hh=hh: e.transpose(PBb[bt][0:QKH, hh * 128:(hh + 1) * 128], kfin[s_][:, hh, :], ident),
                       [r_kfin[s_], r_ident], [RB[bt]], sig=(hh == 3))
              srcp = PBb[bt][0:QKH, 0:512].rearrange("p (h n) -> p h n", h=4)
              if it["kind"] == "k":
                  kt = it["kt"]
                  cp("dve", kT[0:QKH, :, kt * 128:(kt + 1) * 128], srcp, [RB[bt]], [r_kT[kt]])
              else:
                  tl = it["t"] % 4
                  cp("dve", qTb[it["buf"]][0:QKH, :, tl * 128:(tl + 1) * 128], srcp, [RB[bt]], [r_qTb[it["buf"]]])

          def mk(kind, **kw):
              it = dict(kind=kind, s=cnt["i"] % NS, n=cnt["i"], **kw)
              cnt["i"] += 1
              return it

          def run_skewed(items):
              n = len(items)
              for i in range(n + 2):
                  if i < n:
                      st1(items[i])
                  if 0 <= i - 1 < n:
                      st2a(items[i - 1]); st2b(items[i - 1])
                  if 0 <= i - 2 < n:
                      st3(items[i - 2])

          def qa_rows(qb, buf):
              S.dma("pool", qTb[buf][96:128, :, :], qa_d[:, qb * 512:(qb + 1) * 512].unsqueeze(1).to_broadcast([32, 4, 512]), writes=[r_qTm[buf]])

          hcount = 0
          for g in range(2):
              S.dma("pool", wkv, wkv_d[l][:, g * 512:(g + 1) * 512], writes=[r_wkv])
              S.dma("pool", wq, wqv[:, :, g * 384:(g + 1) * 384], writes=[r_wq])
              qa_rows(0, 0)
              run_skewed([mk("k", kt=kt) for kt in range(NKT)])
              if "kT" in dbg and l == 0 and g == 0:
                  dump("kT", kT, r_kT + [r_kTm], BF16); dump("Vaug", Vaug, r_V + [r_Vones], BF16)
              run_skewed([mk("q", t=t, buf=0) for t in range(4)])
              if "qT" in dbg and l == 0 and g == 0:
                  dump("qT", qTb[0], [r_qTb[0], r_qTm[0]], BF16)
              NP = NKT // 2
              jobs = [(qb, hh, p_) for qb in range(4) for hh in range(4) for p_ in range(NP)]
              nxt_items = {}
              for qb in range(3):
                  nxt_items[qb] = None

              def score(job, gp):
                  qb, hh, p_ = job
                  buf = qb % 2
                  for q_ in range(2):
                      kt = 2 * p_ + q_
                      b = 2 * (gp % 2) + q_
                      mm(PB[b], kT[:, hh, kt * 128:(kt + 1) * 128], qTb[buf][:, hh, :], True, True,
                         [r_kT[kt], r_kTm, r_qTb[buf], r_qTm[buf]], [RB[b]])

              score(jobs[0], 0); score(jobs[1], 1)
              ob = 6
              pend_fin = []
              for gp, job in enumerate(jobs):
                  qb, hh, p_ = job
                  buf = qb % 2
                  if p_ == 0:
                      ob = 6 + hcount % 2
                      hcount += 1
                      if qb + 1 < 4:
                          if hh == 0:
                              qa_rows(qb + 1, 1 - buf)
                              nxt_items[qb] = [mk("q", t=(qb + 1) * 4 + tl, buf=1 - buf) for tl in range(4)]
                          st1(nxt_items[qb][hh]); st2a(nxt_items[qb][hh])
                  if p_ == 3 and qb + 1 < 4:
                      st2b(nxt_items[qb][hh])
                  if p_ == 6 and qb + 1 < 4:
                      st3(nxt_items[qb][hh])
                  if p_ == 2 and pend_fin:
                      pend_fin.pop(0)()
                  pb = gp % 2
                  b0 = 2 * pb
                  act(pT[pb], ps_all[:, b0 * 512:(b0 + 2) * 512], AF.Exp, [RB[b0], RB[b0 + 1]], [r_pT[pb]], scale=float(QKH) ** -0.5)
                  if gp + 2 < len(jobs):
                      score(jobs[gp + 2], gp + 2)
                  for q_ in range(2):
                      kt = 2 * p_ + q_
                      mm(PB[ob][0:96, :], Vaug[:, kt, hh, :], pT[pb][:, q_ * 512:(q_ + 1) * 512], kt == 0, kt == NKT - 1,
                         [r_V[kt], r_Vones, r_pT[pb]], [RB[ob]])
                  if p_ == NP - 1:
                      h_ = 4 * g + hh
                      ri = (hcount - 1) % 2
                      S.op("dve", lambda e: e.reciprocal(out=rec[ri][64:96, :], in_=PB[ob][64:96, :]), [RB[ob]], [r_rec[ri]])

                      def fin(h_=h_, ob=ob, ri=ri, qb=qb):
                          for hv in range(2):
                              r0 = (h_ % 2) * 64 + hv * 32
                              tt("dve", mixT[r0:r0 + 32, h_ // 2, qb * 512:(qb + 1) * 512], PB[ob][hv * 32:hv * 32 + 32, :], rec[ri][64:96, :],
                                 ALU.mult, [RB[ob], r_rec[ri]], [r_attnT[h_ // 2][qb]])
                      pend_fin.append(fin)
              while pend_fin:
                  pend_fin.pop(0)()
          dump("attnT%d" % l, mixT[:, 0:4, :], [r for c in range(4) for r in r_attnT[c]], BF16)
          chk("attn%d" % l)

          S.barrier()
          PA.at(0)
          wo = PA([128, 8, D], BF16); r_wo = [Region("wo0"), Region("wo1")]
          tmpx = [PA([128, 512]) for _ in range(2)]; r_tmpx = [Region("tmpx0"), Region("tmpx1")]
          wov = wo_d[l].rearrange("(c p) n -> p c n", p=128)
          for hf in range(2):
              S.dma("pool", wo[:, :, hf * 512:(hf + 1) * 512], wov[:, :, hf * 512:(hf + 1) * 512], writes=[r_wo[hf]])
          bi = 0
          for hf in range(2):
              for t in range(NT):
                  mreads = [r_attnT[c][t // 4] for c in range(4)] + r_convT + [r_retT[0][t], r_retT[1][t]]
                  bk = bi % 8; i2 = bi % 2; bi += 1
                  for c in range(8):
                      mm(PB[bk], mixT[:, c, t * 128:(t + 1) * 128], wo[:, c, hf * 512:(hf + 1) * 512], c == 0, c == 7,
                         mreads + [r_wo[hf]], [RB[bk]], sig=(c == 7))
                  tt("dve", tmpx[i2], PB[bk], gates[:, 0, hf * 512:(hf + 1) * 512], ALU.mult, [RB[bk], r_gates[0]], [r_tmpx[i2]])
                  tt("pool", X[:, t, hf * 512:(hf + 1) * 512], X[:, t, hf * 512:(hf + 1) * 512], tmpx[i2], ALU.add, [RX[t], r_tmpx[i2]], [RX[t]])
          dump("xmid%d" % l, X, RX)
          chk("wo%d" % l)

          S.barrier()
          PA.at(0)
          h2T = PA([128, 8, 1024], BF16); r_h2T = [Region("h2T0"), Region("h2T1")]
          actT = PA([128, NHC, 1024], BF16); r_actT = [[Region("actT%d_%d" % (hc, b)) for b in range(2)] for hc in range(NHC)]
          wgu = [PA([128, 2, 8, 256], BF16) for _ in range(3)]; r_wgu = [Region("wgu%d" % i) for i in range(3)]
          wdp = [PA([128, 2, D], BF16) for _ in range(3)]; r_wdp = [Region("wdp%d" % i) for i in range(3)]
          xn2 = [PA([128, D], BF16) for _ in range(4)]; r_xn2 = [Region("xn2_%d" % i) for i in range(4)]
          junk2 = PA([128, D], BF16); r_junk2 = Region("junk2")
          ssb2 = smallscr[:, 16:24]; r_ssb2 = Region("ssb2")
          sg = [PA([128, 512], BF16) for _ in range(2)]; r_sg = [Region("sg0"), Region("sg1")]
          tmpy = [PA([128, 512]) for _ in range(2)]; r_tmpy = [Region("tmpy0"), Region("tmpy1")]
          assert PA.cur - P_OFF <= 112 * KB, PA.cur - P_OFF
          wgv = wg_d[l].rearrange("(c p) n -> p c n", p=128)
          wuv = wu_d[l].rearrange("(c p) n -> p c n", p=128)
          wdv = wd_d[l].rearrange("(a p) n -> p a n", p=128)
          gu_n = [0]; dn_n = [0]

          def load_gu(pc):
              i = gu_n[0] % 3; gu_n[0] += 1
              S.dma("pool", wgu[i][:, 0], wgv[:, :, pc * 256:(pc + 1) * 256], writes=[r_wgu[i]])
              S.dma("pool", wgu[i][:, 1], wuv[:, :, pc * 256:(pc + 1) * 256], writes=[r_wgu[i]])
              return i

          def load_dn(pc):
              i = dn_n[0] % 3; dn_n[0] += 1
              S.dma("pool", wdp[i], wdv[:, 2 * pc:2 * pc + 2, :], writes=[r_wdp[i]])
              return i

          bi = 0
          for half in range(2):
              tiles = list(range(half * 8, half * 8 + 8))
              gq = [load_gu(0), load_gu(1)]
              norm_phase(2, tiles, h2T, r_h2T, xn2, r_xn2, junk2, r_junk2, ssb2, r_ssb2)
              if half == 0:
                  dump("h2T%d" % l, h2T, r_h2T, BF16)
              dq = []
              for pc in range(11):
                  wi = gq.pop(0)
                  if pc + 2 < 11:
                      gq.append(load_gu(pc + 2))
                  elif pc + 2 == 11:
                      dq.append(load_dn(0))
                  else:
                      dq.append(load_dn(1))
                  for hh in range(2):
                      hc = 2 * pc + hh
                      for tb in range(2):
                          bG = (2 * bi) % 8; bU = (2 * bi + 1) % 8; i2 = bi % 2; bi += 1
                          for k in range(8):
                              mm(PB[bG], wgu[wi][:, 0, k, hh * 128:(hh + 1) * 128], h2T[:, k, tb * 512:(tb + 1) * 512], k == 0, k == 7,
                                 [r_wgu[wi], r_h2T[tb]], [RB[bG]], sig=(k == 7))
                          for k in range(8):
                              mm(PB[bU], wgu[wi][:, 1, k, hh * 128:(hh + 1) * 128], h2T[:, k, tb * 512:(tb + 1) * 512], k == 0, k == 7,
                                 [r_wgu[wi], r_h2T[tb]], [RB[bU]], sig=(k == 7))
                          act(sg[i2], PB[bG], AF.Silu, [RB[bG]], [r_sg[i2]])
                          tt("dve", actT[:, hc, tb * 512:(tb + 1) * 512], PB[bU], sg[i2], ALU.mult, [RB[bU], r_sg[i2]], [r_actT[hc][tb]])
              for ps_ in range(2):
                  for pc in range(11):
                      wi = dq.pop(0)
                      nxt_pc = pc + 2
                      if nxt_pc < 11:
                          dq.append(load_dn(nxt_pc))
                      elif ps_ == 0:
                          dq.append(load_dn(nxt_pc - 11))
                      for hh in range(2):
                          hc = 2 * pc + hh
                          for tl4 in range(4):
                              tl = ps_ * 4 + tl4
                              for hf in range(2):
                                  bk = tl4 * 2 + hf
                                  last = (hh == 1 and tl4 == 3 and hf == 1)
                                  mm(PB[bk], actT[:, hc, tl * 128:(tl + 1) * 128], wdp[wi][:, hh, hf * 512:(hf + 1) * 512],
                                     hc == 0, hc == NHC - 1, [r_actT[hc][tl // 4], r_wdp[wi]], [RB[bk]], sig=last)
                  for tl4 in range(4):
                      tl = ps_ * 4 + tl4
                      t = tiles[tl]
                      for hf in range(2):
                          bk = tl4 * 2 + hf; i2 = hf
                          tt("dve", tmpy[i2], PB[bk], gates[:, 1, hf * 512:(hf + 1) * 512], ALU.mult, [RB[bk], r_gates[1]], [r_tmpy[i2]])
                          tt("pool", X[:, t, hf * 512:(hf + 1) * 512], X[:, t, hf * 512:(hf + 1) * 512], tmpy[i2], ALU.add,
                             [RX[t], r_tmpy[i2]], [RX[t]])
                      if l == L - 1:
                          S.dma("sp", y_d.rearrange("(t p) d -> p t d", p=128)[:, t, :], X[:, t, :], reads=[RX[t]])
          dump("xout%d" % l, X, RX)
          chk("ffn%d" % l)
    except _Stop:
        pass

    S.finish("sp")
    print("instructions", S.ninstr, "waits", S.nwaits, "cnt", S.cnt)
    return nc, dbg_outs


def _tables(is_sample):
    cos = np.ones((NK, 16), np.float32); sin = np.zeros((NK, 16), np.float32)
    if is_sample:
        t = np.arange(T)
        row = (t // GRID_W).astype(np.float32); col = (t % GRID_W).astype(np.float32)
        half = ROPE // 2
        freqs = np.power(np.float32(10000.0), -np.arange(0, half, 2, dtype=np.float32) / half).astype(np.float32)
        ar = row[:, None] * freqs[None]; ac = col[:, None] * freqs[None]
        cos[:T, 0:8] = np.cos(ar); cos[:T, 8:16] = np.cos(ac)
        sin[:T, 0:8] = np.sin(ar); sin[:T, 8:16] = np.sin(ac)
    qa = np.zeros((32, T), np.float32); ka = np.zeros((32, NK), np.float32)
    if is_sample:
        ka[0, :] = 1.0
    else:
        gq = np.arange(T) // 256
        for j in range(8):
            ka[j, :T] = (gq == j)
            qa[j, :] = np.where(gq == j, 0.0, -BIG)
        ka[8, T:] = 1.0
        qa[8, :] = -BIG
    seqlen = T if is_sample else 256
    t = np.arange(T)
    mL = (t % seqlen != 0).astype(np.float32); mR = (t % seqlen != seqlen - 1).astype(np.float32)
    keepf = np.ones(NT, np.float32); keepb = np.ones(NT, np.float32)
    if not is_sample:
        keepf[0::2] = 0.0
        keepb[1::2] = 0.0
    keepf[0] = 1.0; keepb[NT - 1] = 1.0
    nbL = (mL[0::256] - 1.0).astype(np.float32)
    nbR = (mR[255::256] - 1.0).astype(np.float32)
    return dict(cos=cos, sin=sin, qa=qa, ka=ka, nbL=nbL, nbR=nbR, keepf=keepf, keepb=keepb)


def _cst():
    c = np.zeros((128, 3, 128), np.float32)
    k = np.arange(128, dtype=np.float32)[:, None]; q = np.arange(128, dtype=np.float32)[None, :]
    c[:, 0, :] = q - k
    c[:, 1, :] = q + 1.0
    c[:, 2, :] = 128.0 - q
    return c


_WNAMES = ["ada_w", "ada_b", "w_in", "q_norm_g", "kv_norm_g", "w_q_up", "w_kv_up",
           "q_head_norm_g", "k_head_norm_g", "ret_decay_fwd", "ret_decay_bwd", "w_o",
           "w_ffn_gate", "w_ffn_up", "w_ffn_down"]

_PROGRAM = {}


def _run(inputs, dbg=None, stop=None):
    f32 = lambda a: np.ascontiguousarray(np.asarray(a, dtype=np.float32))
    key = (tuple(sorted(dbg)) if dbg else (), stop)
    if key not in _PROGRAM:
        _PROGRAM[key] = build_program(dbg, stop)
    nc, dbg_outs = _PROGRAM[key]
    W = {k: f32(inputs[k]) for k in _WNAMES}
    W["ada_b_pc"] = f32(inputs["ada_b"]).reshape(L, 48, 128).transpose(0, 2, 1)
    W["n1g_pc"] = f32(inputs["norm1_g"]).reshape(L, 8, 128).transpose(0, 2, 1)
    W["n2g_pc"] = f32(inputs["norm2_g"]).reshape(L, 8, 128).transpose(0, 2, 1)
    W["cw_pc"] = f32(inputs["conv_w"]).reshape(L, 3, 2, 128).transpose(0, 3, 2, 1)
    xs = f32(inputs["x_sample"]); xp = f32(inputs["x_prompt"])
    c = f32(inputs["c"]); c_ctx = f32(inputs["c_ctx"])
    cckv = f32(inputs["cache_ckv"]); ckpe = f32(inputs["cache_kpe"]); st = f32(inputs["state_ret"])
    ts_, tp_ = _tables(True), _tables(False)
    cst = _cst()
    in_maps = []
    for i in range(NCORES):
        if i < 4:
            m = dict(x=xs[i], cond_pc=c[i].reshape(8, 128).T, cckv=cckv[i], ckpe=ckpe[i], s0=st[i], **ts_)
        else:
            j = i - 4
            m = dict(x=xp[8 * j:8 * j + 8].reshape(T, D), cond_pc=c_ctx.reshape(8, 128).T,
                     cckv=np.zeros((L, NCTX, KVL), np.float32), ckpe=np.zeros((L, NCTX, ROPE), np.float32),
                     s0=np.zeros((L, 2, RH, 64, 64), np.float32), **tp_)
        m["cst"] = cst
        m.update(W)
        in_maps.append({k: np.ascontiguousarray(v) for k, v in m.items()})
    res = run_bass_kernel_spmd(nc, in_maps, core_ids=list(range(NCORES)))
    return res.results


def kernel(**inputs):
    r = _run(inputs)
    y_sample = np.stack([r[i]["y"] for i in range(4)], 0)
    y_prompt = np.concatenate([r[4 + j]["y"].reshape(8, 256, D) for j in range(4)], 0)
    new_ckv = np.concatenate([r[4 + j]["ockv"].reshape(L, 8, 256, KVL).transpose(1, 0, 2, 3) for j in range(4)], 0)
    new_kpe = np.concatenate([r[4 + j]["okpe"].reshape(L, 8, 256, ROPE).transpose(1, 0, 2, 3) for j in range(4)], 0)
    new_st = np.concatenate([r[4 + j]["ost"].transpose(2, 0, 1, 3, 4, 5) for j in range(4)], 0)
    return (y_prompt.astype(np.float32), y_sample.astype(np.float32), np.ascontiguousarray(new_ckv, dtype=np.float32),
            np.ascontiguousarray(new_kpe, dtype=np.float32), np.ascontiguousarray(new_st, dtype=np.float32))
```

```python
import os
import numpy as np
import concourse.bass as bass
import concourse.mybir as mybir
from concourse.bass_utils import run_bass_kernel_spmd

F32 = mybir.dt.float32
BF16 = mybir.dt.bfloat16
AF = mybir.ActivationFunctionType
ALU = mybir.AluOpType
AX = mybir.AxisListType

L = 2; D = 1024; T = 2048; NT = 16; NCTX = 256; NK = T + NCTX; NKT = 18
QL = 256; KVL = 128; ROPE = 32; NOPE = 64; QKH = 96; VH = 64; H = 8
CW = 256; RH = 4; CH = 128
FFN = 2816; NHC = 22; EPS = 1e-6; BIG = 30000.0
GRID_W = 64
NCORES = 8
C_QLAT = 0; C_KV = 256; C_KPE = 384; C_GB = 416; C_GC = 672; C_XIN = 928
C_RQ = 1184; C_RK = 1440; C_RV = 1696; C_RG = 1952


class _Stop(Exception):
    pass


class Region:
    __slots__ = ("name", "w", "r", "excl")

    def __init__(self, name, excl=False):
        self.name = name
        self.w = []
        self.r = []
        self.excl = excl


ATTACH_ENGINES = ("pe", "act", "dve", "pool")


class Sched:
    ENG = ("pe", "act", "dve", "pool", "sp")

    def __init__(self, nc, n_dma_sems=32):
        self.nc = nc
        self.e = {"pe": nc.tensor, "act": nc.scalar, "dve": nc.vector,
                  "pool": nc.gpsimd, "sp": nc.sync}
        self.sem = {k: nc.alloc_semaphore("sem_" + k) for k in self.ENG}
        self.cnt = {k: 0 for k in self.ENG}
        self.seen = {k: {} for k in self.ENG}
        self.hist = {}
        self.dma_sems = [nc.alloc_semaphore("dsem%d" % i) for i in range(n_dma_sems)]
        self.dma_cnt = [0] * n_dma_sems
        self.dma_last = [None] * n_dma_sems
        self.dma_rr = 0
        self.nwaits = 0
        self.ninstr = 0
        self.defer = None
        self.limit = int(os.environ.get("KSTOPN", "0"))

    def _need(self, eng, tok):
        key, sem, val = tok
        if key == eng and eng in ("pe", "sp", "pool"):
            return False
        return self.seen[eng].get(key, 0) < val

    def _wait(self, eng, tok):
        key, sem, val = tok
        if not self._need(eng, tok):
            return
        if self.defer is not None:
            self.defer.append((sem, val))
        else:
            self.e[eng].wait_ge(sem, val)
        self.nwaits += 1
        s = self.seen[eng]
        s[key] = val
        h = self.hist.get((key, val))
        if h:
            for k2, v2 in h.items():
                if s.get(k2, 0) < v2:
                    s[k2] = v2

    def _deps(self, eng, reads, writes):
        best = {}
        self_raw = None

        def upd(t):
            if t[0] not in best or best[t[0]][2] < t[2]:
                best[t[0]] = t

        for r in reads:
            for t in r.w:
                if t[0] == eng:
                    if self_raw is None or self_raw[2] < t[2]:
                        self_raw = t
                else:
                    upd(t)
            if r.excl:
                for t in r.r:
                    if t[0] != eng:
                        upd(t)
        for w in writes:
            for t in w.w + w.r:
                if t[0] != eng:
                    upd(t)
        if self_raw is not None:
            best[eng] = self_raw
        for t in best.values():
            self._wait(eng, t)

    def _record(self, tok, reads, writes):
        for r in reads:
            if r.excl:
                r.w = [tok]
                r.r = []
            else:
                r.r = [t for t in r.r if t[0] != tok[0]] + [tok]
        for w in writes:
            w.w = [tok]
            w.r = []

    def op(self, eng, fn, reads=(), writes=(), sig=True):
        reads = list(reads)
        writes = list(writes)
        if self.limit and self.ninstr >= self.limit:
            raise _Stop()
        self.defer = []
        self._deps(eng, reads, writes)
        pend = self.defer
        self.defer = None
        attach = None
        if pend and eng in ATTACH_ENGINES:
            attach = pend.pop()
        for (sem_, val_) in pend:
            self.e[eng].wait_ge(sem_, val_)
        ins = fn(self.e[eng])
        if attach is not None:
            ins._wait_ge(attach[0], attach[1])
        self.ninstr += 1
        if sig:
            self.cnt[eng] += 1
            ins.then_inc(self.sem[eng], 1)
            tok = (eng, self.sem[eng], self.cnt[eng])
            self.hist[(eng, self.cnt[eng])] = dict(self.seen[eng])
        else:
            tok = (eng, self.sem[eng], self.cnt[eng] + 1)
        self._record(tok, reads, writes)
        return ins

    def dma(self, eng, out, in_, reads=(), writes=(), **kw):
        reads = list(reads)
        writes = list(writes)
        if self.limit and self.ninstr >= self.limit:
            raise _Stop()
        slot = self.dma_rr
        self.dma_rr = (self.dma_rr + 1) % len(self.dma_sems)
        prev = self.dma_last[slot]
        self._deps(eng, reads, writes)
        if prev is not None:
            self._wait(eng, prev)
        ins = self.e[eng].dma_start(out=out, in_=in_, **kw)
        self.ninstr += 1
        self.dma_cnt[slot] += 16
        ins.then_inc(self.dma_sems[slot], 16)
        key = "d%d" % slot
        tok = (key, self.dma_sems[slot], self.dma_cnt[slot])
        self.hist[(key, self.dma_cnt[slot])] = dict(self.seen[eng])
        self.dma_last[slot] = tok
        self._record(tok, reads, writes)
        return tok

    def all_tokens(self):
        toks = []
        for k in self.ENG:
            if self.cnt[k] > 0:
                toks.append((k, self.sem[k], self.cnt[k]))
        for t in self.dma_last:
            if t is not None:
                toks.append(t)
        return toks

    def barrier(self, engines=None):
        toks = self.all_tokens()
        for e in (engines or self.ENG):
            for t in toks:
                if t[0] != e:
                    self._wait(e, t)

    def finish(self, eng="sp"):
        for t in self.all_tokens():
            if t[0] != eng:
                self._wait(eng, t)


def _prod(s):
    n = 1
    for v in s:
        n *= v
    return n


def build_program(dbg=None, stop=None):
    nc = bass.Bass("TRN2", target_bir_lowering=False)
    S = Sched(nc)
    dbg = dbg or set()
    dbg_outs = {}

    def din(name, shape):
        return nc.dram_tensor(name, list(shape), F32, kind="ExternalInput").ap()

    def dout(name, shape, dt=F32):
        return nc.dram_tensor(name, list(shape), dt, kind="ExternalOutput").ap()

    x_d = din("x", [T, D]); cond_d = din("cond_pc", [128, 8])
    cckv_d = din("cckv", [L, NCTX, KVL]); ckpe_d = din("ckpe", [L, NCTX, ROPE])
    s0_d = din("s0", [L, 2, RH, 64, 64])
    cos_d = din("cos", [NK, 16]); sin_d = din("sin", [NK, 16])
    qa_d = din("qa", [32, T]); ka_d = din("ka", [32, NK])
    nbL_d = din("nbL", [8]); nbR_d = din("nbR", [8])
    keepf_d = din("keepf", [NT]); keepb_d = din("keepb", [NT])
    cst_d = din("cst", [128, 3, 128])
    ada_w_d = din("ada_w", [L, D, 6 * D]); ada_b_d = din("ada_b", [L, 6 * D]); adab_pc_d = din("ada_b_pc", [L, 128, 48])
    n1g_d = din("n1g_pc", [L, 128, 8]); n2g_d = din("n2g_pc", [L, 128, 8])
    w_in_d = din("w_in", [L, D, 2208])
    qng_d = din("q_norm_g", [L, QL]); kvng_d = din("kv_norm_g", [L, KVL])
    wq_d = din("w_q_up", [L, QL, H * QKH]); wkv_d = din("w_kv_up", [L, KVL, H * 128])
    qhg_d = din("q_head_norm_g", [L, QKH]); khg_d = din("k_head_norm_g", [L, QKH])
    cw_d = din("cw_pc", [L, 128, 2, 3])
    rdf_d = din("ret_decay_fwd", [L, RH]); rdb_d = din("ret_decay_bwd", [L, RH])
    wo_d = din("w_o", [L, D, D])
    wg_d = din("w_ffn_gate", [L, D, FFN]); wu_d = din("w_ffn_up", [L, D, FFN])
    wd_d = din("w_ffn_down", [L, FFN, D])

    y_d = dout("y", [T, D]); ockv_d = dout("ockv", [L, T, KVL]); okpe_d = dout("okpe", [L, T, ROPE])
    ost_d = dout("ost", [L, 2, 8, RH, 64, 64])

    BIGW = 51200
    big = nc.alloc_sbuf_tensor("big", [128, BIGW], F32)
    ps_all = nc.alloc_psum_tensor("ps_all", [128, 4096], F32)
    PB = [ps_all[:, i * 512:(i + 1) * 512] for i in range(8)]
    PBb = [ps_all[:, i * 512:(i + 1) * 512].bitcast(BF16) for i in range(8)]
    RB = [Region("bank%d" % i, excl=True) for i in range(8)]
    bank_rr = [0]

    def nb():
        b = bank_rr[0]
        bank_rr[0] = (b + 1) % 8
        return b

    def view(off, shape, dt=F32):
        assert off % 4 == 0 and shape[0] == 128
        n = _prod(shape[1:])
        esz = 4 if dt == F32 else 2
        words = (n * esz + 3) // 4
        assert off // 4 + words <= BIGW, (off, shape)
        ap = big[:, off // 4: off // 4 + words]
        if dt != F32:
            ap = ap.bitcast(dt)
            if ap.shape[1] != n:
                ap = ap[:, 0:n]
        if len(shape) == 3:
            ap = ap.rearrange("p (a b) -> p a b", a=shape[1])
        elif len(shape) == 4:
            ap = ap.rearrange("p (a b c) -> p a b c", a=shape[1], b=shape[2])
        elif len(shape) == 5:
            ap = ap.rearrange("p (a b c d) -> p a b c d", a=shape[1], b=shape[2], c=shape[3])
        return ap

    KB = 1024
    X_OFF = 0
    C_OFF = 64 * KB
    P_OFF = 88 * KB

    class Alloc:
        def __init__(self, base, limit):
            self.base = base; self.cur = base; self.limit = limit

        def __call__(self, shape, dt=F32):
            n = _prod(shape[1:]) * (4 if dt == F32 else 2)
            n = (n + 31) // 32 * 32
            off = self.cur
            self.cur += n
            assert self.cur <= self.limit, ("arena overflow", self.cur - self.limit)
            return view(off, shape, dt)

        def at(self, off_kb):
            self.cur = self.base + int(off_kb * KB)

    CA = Alloc(C_OFF, P_OFF)
    PA = Alloc(P_OFF, BIGW * 4)

    def act(out, in_, func, reads, writes, **kw):
        return S.op("act", lambda e: e.activation(out=out, in_=in_, func=func, **kw), reads, writes)

    def tt(eng, out, in0, in1, op, reads, writes):
        return S.op(eng, lambda e: e.tensor_tensor(out=out, in0=in0, in1=in1, op=op), reads, writes)

    def ts(eng, out, in0, s1, op0, reads, writes, s2=None, op1=None):
        if op1 is None:
            return S.op(eng, lambda e: e.tensor_scalar(out=out, in0=in0, scalar1=s1, scalar2=None, op0=op0), reads, writes)
        return S.op(eng, lambda e: e.tensor_scalar(out=out, in0=in0, scalar1=s1, scalar2=s2, op0=op0, op1=op1), reads, writes)

    def stt(eng, out, in0, scalar, in1, op0, op1, reads, writes):
        return S.op(eng, lambda e: e.scalar_tensor_tensor(out=out, in0=in0, scalar=scalar, in1=in1, op0=op0, op1=op1), reads, writes)

    def cp(eng, out, in_, reads, writes):
        if eng == "act":
            return S.op("act", lambda e: e.copy(out=out, in_=in_), reads, writes)
        return S.op(eng, lambda e: e.tensor_copy(out=out, in_=in_), reads, writes)

    def mm(out, lhsT, rhs, start, stop, reads, writes, sig=True):
        return S.op("pe", lambda e: e.matmul(out, lhsT=lhsT, rhs=rhs, start=start, stop=stop), reads, writes, sig=sig)

    def chk(tag):
        if stop == tag:
            raise _Stop()

    def dump(name, ap, reads, dt=F32):
        if name not in dbg:
            return
        d = dout("dbg_" + name, list(ap.shape), dt)
        dbg_outs[name] = d
        S.dma("sp", d, ap, reads=reads)

    def rstd_of(ss_ap, n, r_ss, eps_ap):
        act(ss_ap, ss_ap, AF.Ln, [r_ss], [r_ss], scale=1.0 / n, bias=eps_ap)
        act(ss_ap, ss_ap, AF.Exp, [r_ss], [r_ss], scale=-0.5)

    X = view(X_OFF, [128, NT, D]); RX = [Region("x%d" % t) for t in range(NT)]
    ident = CA([128, 128], BF16); r_ident = Region("ident")
    cst = CA([128, 3, 128]); r_cst = Region("cst")
    epsc = CA([128, 1]); r_eps = Region("eps")
    gates = CA([128, 2, D]); r_gates = [Region("g1"), Region("g2")]
    colmod = CA([128, 48]); r_colmod = Region("colmod")
    adab_col = CA([128, 48]); r_adabc = Region("adabc")
    ngc = CA([128, 2, 8]); r_ngc = Region("ngc")
    GS = CA([128, 4, 8]); r_GS = Region("GS")
    cos_sb = CA([128, NKT, 16]); sin_sb = CA([128, NKT, 16]); r_trig = Region("trig")
    qgb = CA([128, QL]); kvgb = CA([128, KVL]); qhgb = CA([128, QKH]); khgb = CA([128, QKH]); r_gb = Region("gb")
    cwc = CA([128, 2, 3]); r_cwc = Region("cwc")
    condc = CA([128, 8]); r_cond = Region("cond")
    sc_col = CA([128, 8], BF16); r_sccol = Region("sccol")
    lgall = CA([128, 8]); r_lg = Region("lg")
    lgcol = CA([128, 2, 2]); r_lgcol = Region("lgcol")
    Mtab = CA([128, RH, 128]); r_M = Region("M")
    qdtab = CA([128, 2, 2, 128]); r_qd = Region("qdtab")
    kdtab = CA([128, 2, RH]); r_kd = Region("kdtab")
    cdcol = CA([128, 2, 2]); r_cd = Region("cdcol")
    keepfb = CA([128, 2, NT]); r_keep = Region("keep")
    aktab = CA([128, 2, 2, NT]); r_ak = Region("ak")
    smallscr = CA([128, 64]); r_scr = Region("scr")
    tri = CA([128, 2, 128]); r_tri = Region("tri")
    nbt = CA([128, 2, 8]); r_nbt = Region("nbt")
    print("const arena used", CA.cur - C_OFF, "of", P_OFF - C_OFF)

    xv = x_d.rearrange("(t p) d -> p t d", p=128)
    S.op("pool", lambda e: e.memset(ident, 0.0), writes=[r_ident])
    S.op("pool", lambda e: e.affine_select(out=ident, in_=ident, pattern=[[-1, 128]], compare_op=ALU.not_equal,
                                           fill=1.0, base=0, channel_multiplier=1), reads=[r_ident], writes=[r_ident])
    S.op("dve", lambda e: e.memset(epsc, EPS), writes=[r_eps])
    S.dma("act", cst, cst_d, writes=[r_cst])
    S.dma("act", condc, cond_d, writes=[r_cond])
    S.dma("act", cos_sb, cos_d.rearrange("(t p) f -> p t f", p=128), writes=[r_trig])
    S.dma("act", sin_sb, sin_d.rearrange("(t p) f -> p t f", p=128), writes=[r_trig])
    S.dma("act", nbt[:, 0, :], nbL_d.partition_broadcast(128), writes=[r_nbt])
    S.dma("act", nbt[:, 1, :], nbR_d.partition_broadcast(128), writes=[r_nbt])
    S.dma("act", keepfb[:, 0, :], keepf_d.partition_broadcast(128), writes=[r_keep])
    S.dma("act", keepfb[:, 1, :], keepb_d.partition_broadcast(128), writes=[r_keep])
    act(sc_col, condc, AF.Silu, [r_cond], [r_sccol])
    ts("dve", tri[:, 0, :], cst[:, 0, :], 0.0, ALU.is_ge, [r_cst], [r_tri])
    ts("dve", tri[:, 1, :], cst[:, 0, :], 0.0, ALU.is_le, [r_cst], [r_tri])

    try:
      for l in range(L):
          S.barrier()
          PA.at(0)
          sc_b = PA([128, 8, 128], BF16); r_scb = Region("scb")
          NAP = 12
          apiece = [PA([128, 8, 512], BF16) for _ in range(NAP)]; r_ap = [Region("ap%d" % i) for i in range(NAP)]
          abp = [PA([128, 512]) for _ in range(2)]; r_abp = [Region("abp0"), Region("abp1")]
          cp("dve", sc_b, sc_col.unsqueeze(2).to_broadcast([128, 8, 128]), [r_sccol], [r_scb])
          S.dma("act", adab_col, adab_pc_d[l], writes=[r_adabc])
          S.dma("act", ngc[:, 0, :], n1g_d[l], writes=[r_ngc])
          S.dma("act", ngc[:, 1, :], n2g_d[l], writes=[r_ngc])
          S.dma("act", qgb, qng_d[l].partition_broadcast(128), writes=[r_gb])
          S.dma("act", kvgb, kvng_d[l].partition_broadcast(128), writes=[r_gb])
          S.dma("act", qhgb, qhg_d[l].partition_broadcast(128), writes=[r_gb])
          S.dma("act", khgb, khg_d[l].partition_broadcast(128), writes=[r_gb])
          S.dma("act", cwc, cw_d[l], writes=[r_cwc])
          S.dma("act", lgall[:, 0:4], rdf_d[l].partition_broadcast(128), writes=[r_lg])
          S.dma("act", lgall[:, 4:8], rdb_d[l].partition_broadcast(128), writes=[r_lg])
          adav = ada_w_d[l].rearrange("(c p) n -> p c n", p=128)
          colbank = nb()
          gi = 0
          for pc in range(NAP):
              S.dma("pool", apiece[pc], adav[:, :, pc * 512:(pc + 1) * 512], writes=[r_ap[pc]])
          if l == 0:
              for t in range(NT):
                  S.dma("sp", X[:, t, :], xv[:, t, :], reads=[r_ap[8]], writes=[RX[t]])
          for pc in range(12):
              bi = pc % NAP
              if pc >= NAP:
                  S.dma("pool", apiece[bi], adav[:, :, pc * 512:(pc + 1) * 512], writes=[r_ap[bi]])
              if pc in (4, 5, 10, 11):
                  g = 0 if pc < 6 else 1
                  half = pc % 2 if pc < 6 else (pc - 10)
                  S.dma("act", abp[gi % 2], ada_b_d[l, pc * 512:(pc + 1) * 512].partition_broadcast(128), writes=[r_abp[gi % 2]])
                  b = nb()
                  if b == colbank:
                      b = nb()
                  for k in range(8):
                      mm(PB[b], sc_b[:, k, :], apiece[bi][:, k, :], k == 0, k == 7, [r_scb, r_ap[bi]], [RB[b]], sig=(k == 7))
                  tt("dve", gates[:, g, half * 512:(half + 1) * 512], PB[b], abp[gi % 2], ALU.add,
                     [RB[b], r_abp[gi % 2]], [r_gates[g]])
                  gi += 1
              else:
                  for j in range(4):
                      col = pc * 4 + j
                      for k in range(8):
                          mm(PB[colbank][:, col:col + 1], apiece[bi][:, k, j * 128:(j + 1) * 128], sc_col[:, k:k + 1],
                             k == 0, k == 7, [r_sccol, r_ap[bi]], [RB[colbank]], sig=(k == 7))
          tt("dve", colmod, PB[colbank][:, 0:48], adab_col, ALU.add, [RB[colbank], r_adabc], [r_colmod])
          stt("dve", GS[:, 0, :], colmod[:, 8:16], 1.0, ngc[:, 0, :], ALU.add, ALU.mult, [r_colmod, r_ngc], [r_GS])
          cp("dve", GS[:, 1, :], colmod[:, 0:8], [r_colmod], [r_GS])
          stt("dve", GS[:, 2, :], colmod[:, 32:40], 1.0, ngc[:, 1, :], ALU.add, ALU.mult, [r_colmod, r_ngc], [r_GS])
          cp("dve", GS[:, 3, :], colmod[:, 24:32], [r_colmod], [r_GS])
          dump("GS%d" % l, GS, [r_GS]); dump("gates%d" % l, gates, r_gates)

          act(lgall, lgall, AF.Exp, [r_lg], [r_lg], scale=-1.0)
          act(lgall, lgall, AF.Ln, [r_lg], [r_lg], bias=1.0)
          ts("dve", lgall, lgall, -1.0, ALU.mult, [r_lg], [r_lg])
          for d_ in range(2):
              for pr in range(2):
                  for hh in range(2):
                      h_ = 2 * pr + hh
                      cp("dve", lgcol[hh * 64:(hh + 1) * 64, d_, pr:pr + 1], lgall[hh * 64:(hh + 1) * 64, d_ * 4 + h_:d_ * 4 + h_ + 1],
                         [r_lg], [r_lgcol])
          dpos = smallscr
          tmpA = PA([128, 128]); tmpB = PA([128, 128]); tmpC = PA([128, 128]); r_tmp = Region("tmpM")
          ts("dve", tmpA, cst[:, 0, :], 0.0, ALU.max, [r_cst], [r_tmp])
          ts("dve", tmpB, cst[:, 0, :], -1.0, ALU.mult, [r_cst], [r_tmp], s2=0.0, op1=ALU.max)
          for h_ in range(RH):
              act(tmpC, tmpA, AF.Exp, [r_tmp, r_lg], [r_tmp], scale=lgall[:, h_:h_ + 1])
              tt("dve", Mtab[:, h_, :], tmpC, tri[:, 0, :], ALU.mult, [r_tmp, r_tri], [r_M])
              act(tmpC, tmpB, AF.Exp, [r_tmp, r_lg], [r_tmp], scale=lgall[:, 4 + h_:5 + h_])
              tt("dve", tmpC, tmpC, tri[:, 1, :], ALU.mult, [r_tmp, r_tri], [r_tmp])
              tt("dve", Mtab[:, h_, :], Mtab[:, h_, :], tmpC, ALU.add, [r_tmp, r_M], [r_M])
          for pr in range(2):
              act(qdtab[:, 0, pr, :], cst[:, 1, :], AF.Exp, [r_cst, r_lgcol], [r_qd], scale=lgcol[:, 0, pr:pr + 1])
              act(qdtab[:, 1, pr, :], cst[:, 2, :], AF.Exp, [r_cst, r_lgcol], [r_qd], scale=lgcol[:, 1, pr:pr + 1])
          ts("dve", smallscr[:, 0:1], cst[:, 0, 0:1], -1.0, ALU.mult, [r_cst], [r_scr])
          act(kdtab[:, 0, :], lgall[:, 0:4], AF.Exp, [r_lg, r_cst], [r_kd], scale=cst[:, 0, 127:128])
          act(kdtab[:, 1, :], lgall[:, 4:8], AF.Exp, [r_lg, r_scr], [r_kd], scale=smallscr[:, 0:1])
          act(cdcol.rearrange("p a b -> p (a b)"), lgcol.rearrange("p a b -> p (a b)"), AF.Exp, [r_lgcol], [r_cd], scale=128.0)
          for d_ in range(2):
              for pr in range(2):
                  ts("dve", aktab[:, d_, pr, :], keepfb[:, d_, :], cdcol[:, d_, pr:pr + 1], ALU.mult, [r_keep, r_cd], [r_ak])
          dump("Mtab%d" % l, Mtab, [r_M]); dump("qdtab%d" % l, qdtab, [r_qd]); dump("kdtab%d" % l, kdtab, [r_kd])
          chk("mod%d" % l)

          norm_ctr = [0]

          def norm_phase(gidx, tiles, hT, r_hT, xn, r_xn, junk, r_junk, ssb, r_ss):
              ng = len(tiles) // 4
              for g in range(ng):
                  tl = tiles[g * 4:(g + 1) * 4]
                  for i, t in enumerate(tl):
                      act(junk, X[:, t, :], AF.Square, [RX[t]], [r_junk, r_ss], accum_out=ssb[:, g * 4 + i:g * 4 + i + 1])
                  rstd_of(ssb[:, g * 4:(g + 1) * 4], D, r_ss, epsc)
                  b0 = 4 * (norm_ctr[0] % 2)
                  norm_ctr[0] += 1
                  for i, t in enumerate(tl):
                      ts("dve", xn[i], X[:, t, :], ssb[:, g * 4 + i:g * 4 + i + 1], ALU.mult, [RX[t], r_ss], [r_xn[i]])
                      for c in range(8):
                          bk = b0 + c // 2
                          o = PBb[bk][:, (c % 2) * 512 + i * 128:(c % 2) * 512 + (i + 1) * 128]
                          S.op("pe", lambda e, o=o, c=c, i=i: e.transpose(o, xn[i][:, c * 128:(c + 1) * 128], ident),
                               [r_xn[i], r_ident], [RB[bk]], sig=(c % 2 == 1))
                  for c in range(8):
                      bk = b0 + c // 2
                      src = PBb[bk][:, (c % 2) * 512:(c % 2 + 1) * 512]
                      dst = hT[:, c, g * 512:(g + 1) * 512]
                      if c % 2 == 0:
                          act(dst, src, AF.Identity, [RB[bk], r_GS], [r_hT[g]], scale=GS[:, gidx, c:c + 1], bias=GS[:, gidx + 1, c:c + 1])
                      else:
                          ts("dve", dst, src, GS[:, gidx, c:c + 1], ALU.mult, [RB[bk], r_GS], [r_hT[g]],
                             s2=GS[:, gidx + 1, c:c + 1], op1=ALU.add)

          S.barrier()
          PA.at(0)
          hT = PA([128, 8, T], BF16); r_hT = [Region("hT%d" % g) for g in range(4)]
          qlnT = PA([128, 2, T], BF16); r_qlnT = [Region("qlnT%d" % t) for t in range(NT)]
          ckvT = PA([128, NK], BF16); r_ckvT = [Region("ckvT%d" % t) for t in range(NKT)]
          kpe_sb = PA([128, NKT, ROPE]); r_kpe = [Region("kpe%d" % t) for t in range(NKT)]
          assert PA.cur - P_OFF <= 48 * KB, PA.cur - P_OFF
          PA.at(48)
          mixT = PA([128, 8, T], BF16)
          r_attnT = [[Region("attnT%d_%d" % (c, b)) for b in range(4)] for c in range(4)]
          r_convT = [Region("convT0"), Region("convT1")]
          r_retT = [[Region("retT%d_%d" % (c, t)) for t in range(NT)] for c in range(2)]
          PA.at(80)
          xn = [PA([128, D], BF16) for _ in range(4)]; r_xn = [Region("xn%d" % i) for i in range(4)]
          junk = PA([128, D], BF16); r_junk = Region("junk")
          ssb = PA([128, NT]); r_ss = Region("ss")
          wA = PA([128, 8, 416], BF16); r_wA = Region("wA")
          w_inv = w_in_d[l].rearrange("(c p) n -> p c n", p=128)
          S.dma("pool", wA, w_inv[:, :, 0:416], writes=[r_wA])
          norm_phase(0, list(range(NT)), hT, r_hT, xn, r_xn, junk, r_junk, ssb, r_ss)
          dump("hT%d" % l, hT, r_hT, BF16)
          chk("norm1%d" % l)

          qn = [PA([128, QL], BF16) for _ in range(2)]; r_qn = [Region("qn0"), Region("qn1")]
          ckvf = [PA([128, KVL]) for _ in range(2)]; r_ckvf = [Region("ckvf0"), Region("ckvf1")]
          ckvb = [PA([128, KVL], BF16) for _ in range(2)]; r_ckvb = [Region("ckvb0"), Region("ckvb1")]
          ss2 = PA([128, NT, 2]); r_ss2 = Region("ss2")
          cctx = PA([128, 2, KVL]); r_cctx = Region("cctx")
          S.dma("act", cctx, cckv_d[l].rearrange("(t p) f -> p t f", p=128), writes=[r_cctx])
          S.dma("act", kpe_sb[:, NT:NKT, :], ckpe_d[l].rearrange("(t p) f -> p t f", p=128), writes=[r_kpe[16], r_kpe[17]])
          ockv_v = ockv_d[l].rearrange("(t p) f -> p t f", p=128)
          for t in range(NT):
              b = nb()
              for k in range(8):
                  mm(PB[b][:, 0:416], hT[:, k, t * 128:(t + 1) * 128], wA[:, k, :], k == 0, k == 7,
                     [r_hT[t // 4], r_wA], [RB[b]], sig=(k == 7))
              i2 = t % 2
              act(junk[:, 0:QL], PB[b][:, 0:QL], AF.Square, [RB[b]], [r_junk, r_ss2], accum_out=ss2[:, t, 0:1])
              act(junk[:, 0:KVL], PB[b][:, C_KV:C_KV + KVL], AF.Square, [RB[b]], [r_junk, r_ss2], accum_out=ss2[:, t, 1:2])
              act(ss2[:, t, 0:1], ss2[:, t, 0:1], AF.Ln, [r_ss2], [r_ss2], scale=1.0 / QL, bias=epsc)
              act(ss2[:, t, 1:2], ss2[:, t, 1:2], AF.Ln, [r_ss2], [r_ss2], scale=1.0 / KVL, bias=epsc)
              act(ss2[:, t, :], ss2[:, t, :], AF.Exp, [r_ss2], [r_ss2], scale=-0.5)
              stt("dve", qn[i2], PB[b][:, 0:QL], ss2[:, t, 0:1], qgb, ALU.mult, ALU.mult, [RB[b], r_ss2, r_gb], [r_qn[i2]])
              stt("dve", ckvf[i2], PB[b][:, C_KV:C_KV + KVL], ss2[:, t, 1:2], kvgb, ALU.mult, ALU.mult, [RB[b], r_ss2, r_gb], [r_ckvf[i2]])
              cp("act", kpe_sb[:, t, :], PB[b][:, C_KPE:C_KPE + ROPE], [RB[b]], [r_kpe[t]])
              cp("dve", ckvb[i2], ckvf[i2], [r_ckvf[i2]], [r_ckvb[i2]])
              S.dma("sp", ockv_v[:, t, :], ckvf[i2], reads=[r_ckvf[i2]])
              b2 = nb()
              for c in range(2):
                  S.op("pe", lambda e, c=c: e.transpose(PBb[b2][:, c * 128:(c + 1) * 128], qn[i2][:, c * 128:(c + 1) * 128], ident),
                       [r_qn[i2], r_ident], [RB[b2]], sig=False)
              S.op("pe", lambda e: e.transpose(PBb[b2][:, 256:384], ckvb[i2], ident), [r_ckvb[i2], r_ident], [RB[b2]])
              cp("act", qlnT[:, :, t * 128:(t + 1) * 128], PBb[b2][:, 0:256].rearrange("p (c n) -> p c n", c=2), [RB[b2]], [r_qlnT[t]])
              cp("dve", ckvT[:, t * 128:(t + 1) * 128], PBb[b2][:, 256:384], [RB[b2]], [r_ckvT[t]])
          S.dma("sp", okpe_d[l].rearrange("(t p) f -> p t f", p=128), kpe_sb[:, 0:NT, :], reads=r_kpe[0:NT])
          for j in range(2):
              i2 = j
              cp("dve", ckvb[i2], cctx[:, j, :], [r_cctx], [r_ckvb[i2]])
              b2 = nb()
              S.op("pe", lambda e: e.transpose(PBb[b2][:, 0:128], ckvb[i2], ident), [r_ckvb[i2], r_ident], [RB[b2]])
              cp("dve", ckvT[:, (NT + j) * 128:(NT + j + 1) * 128], PBb[b2][:, 0:128], [RB[b2]], [r_ckvT[NT + j]])
          dump("qlnT%d" % l, qlnT, r_qlnT, BF16); dump("ckvT%d" % l, ckvT, r_ckvT, BF16)
          chk("projA%d" % l)

          wC_pre = view(P_OFF + 80 * KB, [128, 8, 3, 128], BF16); r_wC = Region("wC")
          for a in range(3):
              S.dma("pool", wC_pre[:, :, a, :], w_inv[:, :, C_GB + a * 256:C_GB + a * 256 + 128],
                    writes=[r_wC, r_xn[0], r_xn[1], r_xn[2]])
          S.barrier()
          PA.at(48)
          gbs = PA([128, T]); r_gbs = Region("gbs")
          acc = PA([128, T]); r_acc = Region("acc")
          PA.at(80)
          wC = PA([128, 8, 3, 128], BF16)
          u_h = [PA([128, T + 2]) for _ in range(2)]; r_u = [Region("u0"), Region("u1")]
          gcs = [PA([128, 512])] * 2; r_gcs = [Region("gcs0")] * 2
          ctmp = PA([128, 2, 8]); r_ctmp = Region("ctmp")
          wC2 = PA([128, 8, 3, 128], BF16); r_wC2 = Region("wC2")
          assert PA.cur - P_OFF <= 112 * KB, PA.cur - P_OFF
          for a in range(3):
              S.dma("pool", wC2[:, :, a, :], w_inv[:, :, C_GB + a * 256 + 128:C_GB + a * 256 + 256], writes=[r_wC2])
          wCs = [wC, wC2]; r_wCs = [r_wC, r_wC2]
          for c in range(2):
              S.op("pool", lambda e, c=c: e.memset(u_h[c][:, 0:1], 0.0), writes=[r_u[c]])
              S.op("pool", lambda e, c=c: e.memset(u_h[c][:, T + 1:T + 2], 0.0), writes=[r_u[c]])
          cb = 0
          for c in range(2):
              uh = u_h[c]
              for tb in range(4):
                  bks = [(cb * 3 + a) % 6 for a in range(3)]
                  cb += 1
                  for a in (1, 2, 0):
                      bk = bks[a]
                      for k in range(8):
                          mm(PB[bk], wCs[c][:, k, a, :], hT[:, k, tb * 512:(tb + 1) * 512], k == 0, k == 7,
                             [r_wCs[c], r_hT[tb]], [RB[bk]], sig=(k == 7))
                  bg, bc, bx = bks
                  cp("act", gcs[tb % 2], PB[bc], [RB[bc]], [r_gcs[tb % 2]])
                  tt("dve", uh[:, 1 + tb * 512:1 + (tb + 1) * 512], PB[bx], gcs[tb % 2], ALU.mult, [RB[bx], r_gcs[tb % 2]], [r_u[c]])
                  cp("act", gbs[:, tb * 512:(tb + 1) * 512], PB[bg], [RB[bg]], [r_gbs])
              ts("dve", acc, uh[:, 0:T], cwc[:, c, 0:1], ALU.mult, [r_u[c], r_cwc], [r_acc])
              stt("dve", acc, uh[:, 1:T + 1], cwc[:, c, 1:2], acc, ALU.mult, ALU.add, [r_u[c], r_cwc, r_acc], [r_acc])
              stt("dve", acc, uh[:, 2:T + 2], cwc[:, c, 2:3], acc, ALU.mult, ALU.add, [r_u[c], r_cwc, r_acc], [r_acc])
              accv = acc.rearrange("p (k c) -> p k c", c=256)
              uL = uh[:, 0:T].rearrange("p (k c) -> p k c", c=256)[:, :, 0]
              uR = uh[:, 2:T + 2].rearrange("p (k c) -> p k c", c=256)[:, :, 255]
              tt("dve", ctmp[:, 0, :], uL, nbt[:, 0, :], ALU.mult, [r_u[c], r_nbt], [r_ctmp])
              tt("dve", ctmp[:, 1, :], uR, nbt[:, 1, :], ALU.mult, [r_u[c], r_nbt], [r_ctmp])
              stt("dve", accv[:, :, 0], ctmp[:, 0, :], cwc[:, c, 0:1], accv[:, :, 0], ALU.mult, ALU.add, [r_ctmp, r_cwc, r_acc], [r_acc])
              stt("dve", accv[:, :, 255], ctmp[:, 1, :], cwc[:, c, 2:3], accv[:, :, 255], ALU.mult, ALU.add, [r_ctmp, r_cwc, r_acc], [r_acc])
              tt("dve", mixT[:, 4 + c, :], acc, gbs, ALU.mult, [r_acc, r_gbs], [r_convT[c]])
          dump("convT%d" % l, mixT[:, 4:6, :], r_convT, BF16)
          chk("conv%d" % l)

          S.barrier()
          PA.at(48)
          rqT = PA([128, 2, T], BF16); rkT = PA([128, 2, T], BF16)
          r_rqT = [[Region("rqT%d_%d" % (c, b)) for b in range(4)] for c in range(2)]
          r_rkT = [[Region("rkT%d_%d" % (c, b)) for b in range(4)] for c in range(2)]
          PA.at(80)
          rk_tok = PA([128, NT, 256], BF16); rv_tok = PA([128, NT, 256], BF16)
          r_rktok = [Region("rktok%d" % t) for t in range(NT)]; r_rvtok = [Region("rvtok%d" % t) for t in range(NT)]
          wR = [PA([128, 8, 512], BF16) for _ in range(2)]; r_wR = [Region("wR0"), Region("wR1")]
          S.dma("pool", wR[0], w_inv[:, :, C_RQ:C_RQ + 512], writes=[r_wR[0]])
          S.dma("pool", wR[1], w_inv[:, :, C_RK:C_RK + 512], writes=[r_wR[1]])
          bi = 0
          for cc in range(4):
              for tb in range(4):
                  bk = bi % 8; bi += 1
                  for k in range(8):
                      mm(PB[bk], wR[0][:, k, cc * 128:(cc + 1) * 128], hT[:, k, tb * 512:(tb + 1) * 512], k == 0, k == 7,
                         [r_wR[0], r_hT[tb]], [RB[bk]], sig=(k == 7))
                  if cc < 2:
                      cp("act" if tb % 2 == 0 else "dve", rqT[:, cc, tb * 512:(tb + 1) * 512], PB[bk], [RB[bk]], [r_rqT[cc][tb]])
                  else:
                      if tb % 2 == 0:
                          act(rkT[:, cc - 2, tb * 512:(tb + 1) * 512], PB[bk], AF.Copy, [RB[bk]], [r_rkT[cc - 2][tb]], scale=0.125)
                      else:
                          ts("dve", rkT[:, cc - 2, tb * 512:(tb + 1) * 512], PB[bk], 0.125, ALU.mult, [RB[bk]], [r_rkT[cc - 2][tb]])
          for t in range(NT):
              bk = bi % 8; bi += 1
              for k in range(8):
                  mm(PB[bk], hT[:, k, t * 128:(t + 1) * 128], wR[1][:, k, :], k == 0, k == 7,
                     [r_wR[1], r_hT[t // 4]], [RB[bk]], sig=(k == 7))
              act(rk_tok[:, t, :], PB[bk][:, 0:256], AF.Copy, [RB[bk]], [r_rktok[t]], scale=0.125)
              cp("dve", rv_tok[:, t, :], PB[bk][:, 256:512], [RB[bk]], [r_rvtok[t]])
          S.dma("pool", wR[0][:, :, 0:256], w_inv[:, :, C_RG:C_RG + 256], writes=[r_wR[0]])
          for cc in range(2):
              for tb in range(4):
                  bk = bi % 8; bi += 1
                  for k in range(8):
                      mm(PB[bk], wR[0][:, k, cc * 128:(cc + 1) * 128], hT[:, k, tb * 512:(tb + 1) * 512], k == 0, k == 7,
                         [r_wR[0], r_hT[tb]], [RB[bk]], sig=(k == 7))
                  act(mixT[:, 6 + cc, tb * 512:(tb + 1) * 512], PB[bk], AF.Silu, [RB[bk]], r_retT[cc][tb * 4:(tb + 1) * 4])
          dump("rqT%d" % l, rqT, r_rqT[0] + r_rqT[1], BF16); dump("rkT%d" % l, rkT, r_rkT[0] + r_rkT[1], BF16)
          dump("rv_tok%d" % l, rv_tok, r_rvtok, BF16); dump("rk_tok%d" % l, rk_tok, r_rktok, BF16)
          chk("retproj%d" % l)

          S.barrier()
          PA.at(0)
          U_sb = PA([128, NT, 2, 2, 64]); r_U = [[Region("U%d_%d" % (d_, j)) for j in range(NT)] for d_ in range(2)]
          Sin = PA([128, 2, NT, 2, 64], BF16); r_Sin = [[Region("Sin%d_%d" % (d_, j)) for j in range(NT)] for d_ in range(2)]
          stout = PA([128, 2, 8, 2, 64]); r_stout = [Region("stout0"), Region("stout1")]
          PA.at(96)
          ND = 3
          kd = [PA([128, 2, 256], BF16) for _ in range(2)]; r_kdb = [Region("kd0"), Region("kd1")]
          Pm = [PA([128, 4, 128], BF16) for _ in range(ND)]; r_Pm = [Region("Pm%d" % i) for i in range(ND)]
          qd = [PA([128, 2, 2, 128], BF16) for _ in range(ND)]; r_qdb = [Region("qd%d" % i) for i in range(ND)]
          Sst = PA([128, 2, 2, 64]); r_Sst = [Region("Sst0"), Region("Sst1")]
          o_sb = [PA([128, 256]) for _ in range(ND)]; r_osb = [Region("osb%d" % i) for i in range(ND)]
          sqb = [PA([128, 256]) for _ in range(2)]; r_sqb = [Region("sqb0"), Region("sqb1")]
          rss = PA([128, NT, 4]); r_rss = [Region("rss%d" % j) for j in range(NT)]
          onb = [PA([128, 256], BF16) for _ in range(ND)]; r_onb = [Region("on%d" % i) for i in range(ND)]
          assert PA.cur - P_OFF <= 112 * KB, PA.cur - P_OFF
          for hh in range(2):
              for a in range(2):
                  S.dma("act", Sst[hh * 64:(hh + 1) * 64, a], s0_d[l, a].rearrange("(r hh) d v -> hh d r v", hh=2)[hh], writes=r_Sst)
          def u_step(d_, j, n):
              i2 = n % 2
              tt("pool" if d_ == 0 else "dve", kd[i2][:, d_, :].rearrange("p (h n) -> p h n", h=4), rk_tok[:, j, :].rearrange("p (h n) -> p h n", h=4),
                 kdtab[:, d_, :].unsqueeze(2).to_broadcast([128, 4, 64]), ALU.mult, [r_rktok[j], r_kd], [r_kdb[i2]])
              bk = n % 2
              for pr in range(2):
                  mm(PB[bk][:, pr * 128:(pr + 1) * 128], kd[i2][:, d_, pr * 128:(pr + 1) * 128], rv_tok[:, j, pr * 128:(pr + 1) * 128],
                     True, True, [r_kdb[i2], r_rvtok[j]], [RB[bk]], sig=(pr == 1))
              bv = PB[bk][:, 0:256].rearrange("p (a n) -> p a n", a=2)
              cp("act", U_sb[0:64, j, d_], bv[0:64, :, 0:64], [RB[bk]], [r_U[d_][j]])
              cp("act", U_sb[64:128, j, d_], bv[64:128, :, 64:128], [RB[bk]], [r_U[d_][j]])

          def chain_step(d_, j, eng):
              tt(eng, Sin[:, d_, j], Sst[:, d_], keepfb[:, d_, j:j + 1].unsqueeze(2).to_broadcast([128, 2, 64]), ALU.mult,
                 [r_Sst[d_], r_keep], [r_Sin[d_][j]])
              tt(eng, Sst[:, d_], Sst[:, d_], aktab[:, d_, :, j:j + 1].to_broadcast([128, 2, 64]), ALU.mult,
                 [r_Sst[d_], r_ak], [r_Sst[d_]])
              tt(eng, Sst[:, d_], Sst[:, d_], U_sb[:, j, d_], ALU.add, [r_Sst[d_], r_U[d_][j]], [r_Sst[d_]])
              if (d_ == 0 and j % 2 == 1) or (d_ == 1 and j % 2 == 0):
                  cp(eng, stout[:, d_, j // 2], Sst[:, d_], [r_Sst[d_]], [r_stout[d_]])

          n_ = 0
          for i in range(NT + 1):
              if i < NT:
                  u_step(0, i, n_); n_ += 1
                  u_step(1, NT - 1 - i, n_); n_ += 1
              if i >= 1:
                  chain_step(0, i - 1, "pool")
                  chain_step(1, NT - i, "dve")
          for d_ in range(2):
              for hh in range(2):
                  for pr in range(2):
                      S.dma("sp", ost_d[l, d_, :, 2 * pr + hh].rearrange("s d v -> d s v"), stout[hh * 64:(hh + 1) * 64, d_, :, pr, :],
                            reads=[r_stout[d_]])
          def stA(j):
              i3 = j % ND; tb = j // 4
              for par in range(2):
                  bs = 2 * (j % 2) + par
                  for h_ in (par, par + 2):
                      rows = slice((h_ % 2) * 64, (h_ % 2 + 1) * 64)
                      mm(PB[bs][:, (h_ // 2) * 128:(h_ // 2 + 1) * 128], rkT[rows, h_ // 2, j * 128:(j + 1) * 128],
                         rqT[rows, h_ // 2, j * 128:(j + 1) * 128],
                         True, True, [r_rkT[h_ // 2][tb], r_rqT[h_ // 2][tb]], [RB[bs]], sig=(h_ == par + 2))
              for par in range(2):
                  bs = 2 * (j % 2) + par
                  tt("dve", Pm[i3][:, par::2, :], PB[bs][:, 0:256].rearrange("p (h n) -> p h n", h=2), Mtab[:, par::2, :], ALU.mult,
                     [RB[bs], r_M], [r_Pm[i3]])
              for d_ in range(2):
                  tt("pool", qd[i3][:, d_], rqT[:, :, j * 128:(j + 1) * 128], qdtab[:, d_], ALU.mult,
                     [r_rqT[0][tb], r_rqT[1][tb], r_qd], [r_qdb[i3]])

          def stB(j):
              i3 = j % ND
              bo = 4 + j % 2
              for h_ in range(RH):
                  rows = slice((h_ % 2) * 64, (h_ % 2 + 1) * 64)
                  oap = PB[bo][:, h_ * 64:(h_ + 1) * 64]
                  mm(oap, Pm[i3][:, h_, :], rv_tok[:, j, h_ * 64:(h_ + 1) * 64], True, False, [r_Pm[i3], r_rvtok[j]], [RB[bo]], sig=False)
                  mm(oap, qd[i3][rows, 0, h_ // 2, :], Sin[rows, 0, j, h_ // 2, :], False, False, [r_qdb[i3], r_Sin[0][j]], [RB[bo]], sig=False)
                  mm(oap, qd[i3][rows, 1, h_ // 2, :], Sin[rows, 1, j, h_ // 2, :], False, True, [r_qdb[i3], r_Sin[1][j]], [RB[bo]], sig=(h_ == 3))
              cp("act", o_sb[i3], PB[bo][:, 0:256], [RB[bo]], [r_osb[i3]])
              tt("dve", sqb[j % 2], o_sb[i3], o_sb[i3], ALU.mult, [r_osb[i3]], [r_sqb[j % 2]])
              S.op("dve", lambda e: e.tensor_reduce(out=rss[:, j, :], in_=sqb[j % 2].rearrange("p (h n) -> p h n", h=4), axis=AX.X, op=ALU.add),
                   [r_sqb[j % 2]], [r_rss[j]])
              rstd_of(rss[:, j, :], 64, r_rss[j], epsc)

          def stC(j):
              i3 = j % ND
              tt("dve", onb[i3].rearrange("p (h n) -> p h n", h=4), o_sb[i3].rearrange("p (h n) -> p h n", h=4),
                 rss[:, j, :].unsqueeze(2).to_broadcast([128, 4, 64]), ALU.mult, [r_osb[i3], r_rss[j]], [r_onb[i3]])
              bt = 6 + j % 2
              for c in range(2):
                  S.op("pe", lambda e, c=c: e.transpose(PBb[bt][:, c * 128:(c + 1) * 128], onb[i3][:, c * 128:(c + 1) * 128], ident),
                       [r_onb[i3], r_ident], [RB[bt]], sig=(c == 1))
              tt("dve", mixT[:, 6:8, j * 128:(j + 1) * 128], PBb[bt][:, 0:256].rearrange("p (c n) -> p c n", c=2),
                 mixT[:, 6:8, j * 128:(j + 1) * 128], ALU.mult, [RB[bt], r_retT[0][j], r_retT[1][j]], [r_retT[0][j], r_retT[1][j]])

          for i in range(NT + 2):
              if i < NT:
                  stA(i)
              if 0 <= i - 1 < NT:
                  stB(i - 1)
              if 0 <= i - 2 < NT:
                  stC(i - 2)
          dump("retT%d" % l, mixT[:, 6:8, :], r_retT[0] + r_retT[1], BF16)
          dump("U%d" % l, U_sb, r_U[0] + r_U[1]); dump("Sin%d" % l, Sin, r_Sin[0] + r_Sin[1], BF16)
          chk("ret%d" % l)

          S.barrier()
          PA.at(0)
          kT = PA([128, 4, NK], BF16); r_kT = [Region("kT%d" % t) for t in range(NKT)]; r_kTm = Region("kTm")
          pT = [PA([128, 1024], BF16) for _ in range(2)]; r_pT = [Region("pT%d" % i) for i in range(2)]
          wkv = PA([128, 512], BF16); r_wkv = Region("wkv")
          wq = PA([128, 2, 384], BF16); r_wq = Region("wq")
          NS = 3
          kcat = [PA([128, 4, QKH]) for _ in range(NS)]; r_kcat = [Region("kcat%d" % i) for i in range(NS)]
          rt = [[PA([128, 4, 2, 8]) for _ in range(2)] for _ in range(NS)]; r_rt = [Region("rt%d" % i) for i in range(NS)]
          rs4 = [PA([128, 4]) for _ in range(NS)]; r_rs4 = [Region("rs4_%d" % i) for i in range(NS)]
          assert PA.cur - P_OFF <= 32 * KB, PA.cur - P_OFF
          PA.at(80)
          Vaug = PA([128, NKT, 4, 96], BF16); r_V = [Region("V%d" % t) for t in range(NKT)]; r_Vones = Region("Vones")
          qTb = [PA([128, 4, 512], BF16) for _ in range(2)]; r_qTb = [Region("qTb0"), Region("qTb1")]; r_qTm = [Region("qTm0"), Region("qTm1")]
          rec = [PA([128, 512]) for _ in range(2)]; r_rec = [Region("rec0"), Region("rec1")]
          ones_t = PA([128, 512]); r_ones = Region("ones_t")
          jq = PA([128, QKH], BF16)
          jq2 = PA([128, QKH], BF16)
          kfin = [PA([128, 4, QKH], BF16) for _ in range(NS)]; r_kfin = [Region("kfin%d" % i) for i in range(NS)]
          rt2 = [[PA([128, 4, 2, 8]) for _ in range(2)] for _ in range(NS)]; r_rt2 = [Region("rt2_%d" % i) for i in range(NS)]
          assert PA.cur - P_OFF <= 112 * KB, PA.cur - P_OFF
          S.dma("pool", kT[96:128, :, :], ka_d.unsqueeze(1).to_broadcast([32, 4, NK]), writes=[r_kTm])
          S.op("pool", lambda e: e.memset(Vaug[:, :, :, 64:96], 1.0), writes=[r_Vones])
          S.op("pool", lambda e: e.memset(ones_t[64:96, :], -1.0), writes=[r_ones])
          wqv = wq_d[l].rearrange("(c p) n -> p c n", p=128)
          cnt = {"i": 0}

          def st1(it):
              s_ = it["s"]
              b = 4
              if it["kind"] == "k":
                  kt = it["kt"]
                  mm(PB[b], ckvT[:, kt * 128:(kt + 1) * 128], wkv, True, True, [r_ckvT[kt], r_wkv], [RB[b]])
                  bv = PB[b].rearrange("p (h n) -> p h n", h=4)
                  cp("act", Vaug[:, kt, :, 0:64], bv[:, :, 64:128], [RB[b]], [r_V[kt]])
                  cp("act", kcat[s_][:, :, 0:NOPE], bv[:, :, 0:NOPE], [RB[b]], [r_kcat[s_]])
                  cp("pool", kcat[s_][:, :, NOPE:QKH], kpe_sb[:, kt, :].unsqueeze(1).to_broadcast([128, 4, ROPE]), [r_kpe[kt]], [r_kcat[s_]])
              else:
                  t = it["t"]
                  for c in range(2):
                      mm(PB[b][:, 0:384], qlnT[:, c, t * 128:(t + 1) * 128], wq[:, c, :], c == 0, c == 1,
                         [r_qlnT[t], r_wq], [RB[b]], sig=(c == 1))
                  cp("dve", kcat[s_], PB[b][:, 0:384].rearrange("p (h n) -> p h n", h=4), [RB[b]], [r_kcat[s_]])

          def st2a(it):
              s_ = it["s"]
              src = kcat[s_]; r_src = r_kcat[s_]; dst = kfin[s_]; r_dst = r_kfin[s_]
              if it["kind"] == "k":
                  gain = khgb; cos_t = cos_sb[:, it["kt"], :]; sin_t = sin_sb[:, it["kt"], :]
              else:
                  gain = qhgb; cos_t = cos_sb[:, it["t"], :]; sin_t = sin_sb[:, it["t"], :]
              for hh in range(4):
                  if it["kind"] == "k":
                      act(jq, src[:, hh, :], AF.Square, [r_src], [r_rs4[s_]], accum_out=rs4[s_][:, hh:hh + 1])
                  else:
                      S.op("dve", lambda e, hh=hh: e.scalar_tensor_tensor(out=jq2, in0=src[:, hh, :], scalar=1.0, in1=src[:, hh, :],
                                                                          op0=ALU.mult, op1=ALU.mult, accum_out=rs4[s_][:, hh:hh + 1]),
                         [r_src], [r_rs4[s_]])
              tt("pool", src, src, gain.unsqueeze(1).to_broadcast([128, 4, QKH]), ALU.mult, [r_src, r_gb], [r_src])

          def st2b(it):
              s_ = it["s"]
              src = kcat[s_]; r_src = r_kcat[s_]; dst = kfin[s_]; r_dst = r_kfin[s_]
              if it["kind"] == "k":
                  cos_t = cos_sb[:, it["kt"], :]; sin_t = sin_sb[:, it["kt"], :]
              else:
                  cos_t = cos_sb[:, it["t"], :]; sin_t = sin_sb[:, it["t"], :]
              rstd_of(rs4[s_], QKH, r_rs4[s_], epsc)
              tt("dve", dst[:, :, 0:NOPE], src[:, :, 0:NOPE], rs4[s_].unsqueeze(2).to_broadcast([128, 4, NOPE]), ALU.mult,
                 [r_src, r_rs4[s_]], [r_dst])
              tt("dve", src[:, :, NOPE:QKH], src[:, :, NOPE:QKH], rs4[s_].unsqueeze(2).to_broadcast([128, 4, ROPE]), ALU.mult,
                 [r_src, r_rs4[s_]], [r_src])
              rv = src[:, :, NOPE:QKH].rearrange("p h (a b i) -> p h a b i", a=2, b=2)
              dv = dst[:, :, NOPE:QKH].rearrange("p h (a b i) -> p h a b i", a=2, b=2)
              x1 = rv[:, :, :, 0, :]; x2 = rv[:, :, :, 1, :]
              cb_ = cos_t.rearrange("p (a i) -> p a i", a=2).unsqueeze(1).to_broadcast([128, 4, 2, 8])
              sb_ = sin_t.rearrange("p (a i) -> p a i", a=2).unsqueeze(1).to_broadcast([128, 4, 2, 8])
              t0, t1 = rt[s_]
              t2, t3 = rt2[s_]
              tt("pool", t0, x1, cb_, ALU.mult, [r_src, r_trig], [r_rt[s_]])
              tt("pool", t1, x2, sb_, ALU.mult, [r_src, r_trig], [r_rt[s_]])
              tt("pool", dv[:, :, :, 0, :], t0, t1, ALU.subtract, [r_rt[s_]], [r_dst])
              tt("dve", t2, x1, sb_, ALU.mult, [r_src, r_trig], [r_rt2[s_]])
              tt("dve", t3, x2, cb_, ALU.mult, [r_src, r_trig], [r_rt2[s_]])
              tt("dve", dv[:, :, :, 1, :], t2, t3, ALU.add, [r_rt2[s_]], [r_dst])

          def st3(it):
              s_ = it["s"]
              bt = 5
              for hh in range(4):
                  S.op("pe", lambda e, hh=hh: e.transpose(PBb[bt][0:QKH, hh * 128:(hh + 1) * 128], kfin[s_][:, hh, :], ident),
                       [r_kfin[s_], r_ident], [RB[bt]], sig=(hh == 3))
              srcp = PBb[bt][0:QKH, 0:512].rearrange("p (h n) -> p h n", h=4)
              if it["kind"] == "k":
                  kt = it["kt"]
                  cp("dve", kT[0:QKH, :, kt * 128:(kt + 1) * 128], srcp, [RB[bt]], [r_kT[kt]])
              else:
                  tl = it["t"] % 4
                  cp("dve", qTb[it["buf"]][0:QKH, :, tl * 128:(tl + 1) * 128], srcp, [RB[bt]], [r_qTb[it["buf"]]])

          def mk(kind, **kw):
              it = dict(kind=kind, s=cnt["i"] % NS, n=cnt["i"], **kw)
              cnt["i"] += 1
              return it

          def run_skewed(items):
              n = len(items)
              for i in range(n + 2):
                  if i < n:
                      st1(items[i])
                  if 0 <= i - 1 < n:
                      st2a(items[i - 1]); st2b(items[i - 1])
                  if 0 <= i - 2 < n:
                      st3(items[i - 2])

          def qa_rows(qb, buf):
              S.dma("pool", qTb[buf][96:128, :, :], qa_d[:, qb * 512:(qb + 1) * 512].unsqueeze(1).to_broadcast([32, 4, 512]), writes=[r_qTm[buf]])

          hcount = 0
          for g in range(2):
              S.dma("pool", wkv, wkv_d[l][:, g * 512:(g + 1) * 512], writes=[r_wkv])
              S.dma("pool", wq, wqv[:, :, g * 384:(g + 1) * 384], writes=[r_wq])
              qa_rows(0, 0)
              run_skewed([mk("k", kt=kt) for kt in range(NKT)])
              if "kT" in dbg and l == 0 and g == 0:
                  dump("kT", kT, r_kT + [r_kTm], BF16); dump("Vaug", Vaug, r_V + [r_Vones], BF16)
              run_skewed([mk("q", t=t, buf=0) for t in range(4)])
              if "qT" in dbg and l == 0 and g == 0:
                  dump("qT", qTb[0], [r_qTb[0], r_qTm[0]], BF16)
              NP = NKT // 2
              jobs = [(qb, hh, p_) for qb in range(4) for hh in range(4) for p_ in range(NP)]
              nxt_items = {}
              for qb in range(3):
                  nxt_items[qb] = None

              def score(job, gp):
                  qb, hh, p_ = job
                  buf = qb % 2
                  for q_ in range(2):
                      kt = 2 * p_ + q_
                      b = 2 * (gp % 2) + q_
                      mm(PB[b], kT[:, hh, kt * 128:(kt + 1) * 128], qTb[buf][:, hh, :], True, True,
                         [r_kT[kt], r_kTm, r_qTb[buf], r_qTm[buf]], [RB[b]])

              score(jobs[0], 0); score(jobs[1], 1)
              ob = 6
              pend_fin = []
              for gp, job in enumerate(jobs):
                  qb, hh, p_ = job
                  buf = qb % 2
                  if p_ == 0:
                      ob = 6 + hcount % 2
                      hcount += 1
                      if qb + 1 < 4:
                          if hh == 0:
                              qa_rows(qb + 1, 1 - buf)
                              nxt_items[qb] = [mk("q", t=(qb + 1) * 4 + tl, buf=1 - buf) for tl in range(4)]
                          st1(nxt_items[qb][hh]); st2a(nxt_items[qb][hh])
                  if p_ == 3 and qb + 1 < 4:
                      st2b(nxt_items[qb][hh])
                  if p_ == 6 and qb + 1 < 4:
                      st3(nxt_items[qb][hh])
                  if p_ == 2 and pend_fin:
                      pend_fin.pop(0)()
                  pb = gp % 2
                  b0 = 2 * pb
                  act(pT[pb], ps_all[:, b0 * 512:(b0 + 2) * 512], AF.Exp, [RB[b0], RB[b0 + 1]], [r_pT[pb]], scale=float(QKH) ** -0.5)
                  if gp + 2 < len(jobs):
                      score(jobs[gp + 2], gp + 2)
                  for q_ in range(2):
                      kt = 2 * p_ + q_
                      mm(PB[ob][0:96, :], Vaug[:, kt, hh, :], pT[pb][:, q_ * 512:(q_ + 1) * 512], kt == 0, kt == NKT - 1,
                         [r_V[kt], r_Vones, r_pT[pb]], [RB[ob]])
                  if p_ == NP - 1:
                      h_ = 4 * g + hh
                      ri = (hcount - 1) % 2
                      S.op("dve", lambda e: e.reciprocal(out=rec[ri][64:96, :], in_=PB[ob][64:96, :]), [RB[ob]], [r_rec[ri]])

                      def fin(h_=h_, ob=ob, ri=ri, qb=qb):
                          for hv in range(2):
                              r0 = (h_ % 2) * 64 + hv * 32
                              tt("dve", mixT[r0:r0 + 32, h_ // 2, qb * 512:(qb + 1) * 512], PB[ob][hv * 32:hv * 32 + 32, :], rec[ri][64:96, :],
                                 ALU.mult, [RB[ob], r_rec[ri]], [r_attnT[h_ // 2][qb]])
                      pend_fin.append(fin)
              while pend_fin:
                  pend_fin.pop(0)()
          dump("attnT%d" % l, mixT[:, 0:4, :], [r for c in range(4) for r in r_attnT[c]], BF16)
          chk("attn%d" % l)

          S.barrier()
          PA.at(0)
          wo = PA([128, 8, D], BF16); r_wo = [Region("wo0"), Region("wo1")]
          tmpx = [PA([128, 512]) for _ in range(2)]; r_tmpx = [Region("tmpx0"), Region("tmpx1")]
          wov = wo_d[l].rearrange("(c p) n -> p c n", p=128)
          for hf in range(2):
              S.dma("pool", wo[:, :, hf * 512:(hf + 1) * 512], wov[:, :, hf * 512:(hf + 1) * 512], writes=[r_wo[hf]])
          bi = 0
          for hf in range(2):
              for t in range(NT):
                  mreads = [r_attnT[c][t // 4] for c in range(4)] + r_convT + [r_retT[0][t], r_retT[1][t]]
                  bk = bi % 8; i2 = bi % 2; bi += 1
                  for c in range(8):
                      mm(PB[bk], mixT[:, c, t * 128:(t + 1) * 128], wo[:, c, hf * 512:(hf + 1) * 512], c == 0, c == 7,
                         mreads + [r_wo[hf]], [RB[bk]], sig=(c == 7))
                  tt("dve", tmpx[i2], PB[bk], gates[:, 0, hf * 512:(hf + 1) * 512], ALU.mult, [RB[bk], r_gates[0]], [r_tmpx[i2]])
                  tt("pool", X[:, t, hf * 512:(hf + 1) * 512], X[:, t, hf * 512:(hf + 1) * 512], tmpx[i2], ALU.add, [RX[t], r_tmpx[i2]], [RX[t]])
          dump("xmid%d" % l, X, RX)
          chk("wo%d" % l)

          S.barrier()
          PA.at(0)
          h2T = PA([128, 8, 1024], BF16); r_h2T = [Region("h2T0"), Region("h2T1")]
          actT = PA([128, NHC, 1024], BF16); r_actT = [[Region("actT%d_%d" % (hc, b)) for b in range(2)] for hc in range(NHC)]
          wgu = [PA([128, 2, 8, 256], BF16) for _ in range(3)]; r_wgu = [Region("wgu%d" % i) for i in range(3)]
          wdp = [PA([128, 2, D], BF16) for _ in range(3)]; r_wdp = [Region("wdp%d" % i) for i in range(3)]
          xn2 = [PA([128, D], BF16) for _ in range(4)]; r_xn2 = [Region("xn2_%d" % i) for i in range(4)]
          junk2 = PA([128, D], BF16); r_junk2 = Region("junk2")
          ssb2 = smallscr[:, 16:24]; r_ssb2 = Region("ssb2")
          sg = [PA([128, 512], BF16) for _ in range(2)]; r_sg = [Region("sg0"), Region("sg1")]
          tmpy = [PA([128, 512]) for _ in range(2)]; r_tmpy = [Region("tmpy0"), Region("tmpy1")]
          assert PA.cur - P_OFF <= 112 * KB, PA.cur - P_OFF
          wgv = wg_d[l].rearrange("(c p) n -> p c n", p=128)
          wuv = wu_d[l].rearrange("(c p) n -> p c n", p=128)
          wdv = wd_d[l].rearrange("(a p) n -> p a n", p=128)
          gu_n = [0]; dn_n = [0]

          def load_gu(pc):
              i = gu_n[0] % 3; gu_n[0] += 1
              S.dma("pool", wgu[i][:, 0], wgv[:, :, pc * 256:(pc + 1) * 256], writes=[r_wgu[i]])
              S.dma("pool", wgu[i][:, 1], wuv[:, :, pc * 256:(pc + 1) * 256], writes=[r_wgu[i]])
              return i

          def load_dn(pc):
              i = dn_n[0] % 3; dn_n[0] += 1
              S.dma("pool", wdp[i], wdv[:, 2 * pc:2 * pc + 2, :], writes=[r_wdp[i]])
              return i

          bi = 0
          for half in range(2):
              tiles = list(range(half * 8, half * 8 + 8))
              gq = [load_gu(0), load_gu(1)]
              norm_phase(2, tiles, h2T, r_h2T, xn2, r_xn2, junk2, r_junk2, ssb2, r_ssb2)
              if half == 0:
                  dump("h2T%d" % l, h2T, r_h2T, BF16)
              dq = []
              for pc in range(11):
                  wi = gq.pop(0)
                  if pc + 2 < 11:
                      gq.append(load_gu(pc + 2))
                  elif pc + 2 == 11:
                      dq.append(load_dn(0))
                  else:
                      dq.append(load_dn(1))
                  for hh in range(2):
                      hc = 2 * pc + hh
                      for tb in range(2):
                          bG = (2 * bi) % 8; bU = (2 * bi + 1) % 8; i2 = bi % 2; bi += 1
                          for k in range(8):
                              mm(PB[bG], wgu[wi][:, 0, k, hh * 128:(hh + 1) * 128], h2T[:, k, tb * 512:(tb + 1) * 512], k == 0, k == 7,
                                 [r_wgu[wi], r_h2T[tb]], [RB[bG]], sig=(k == 7))
                          for k in range(8):
                              mm(PB[bU], wgu[wi][:, 1, k, hh * 128:(hh + 1) * 128], h2T[:, k, tb * 512:(tb + 1) * 512], k == 0, k == 7,
                                 [r_wgu[wi], r_h2T[tb]], [RB[bU]], sig=(k == 7))
                          act(sg[i2], PB[bG], AF.Silu, [RB[bG]], [r_sg[i2]])
                          tt("dve", actT[:, hc, tb * 512:(tb + 1) * 512], PB[bU], sg[i2], ALU.mult, [RB[bU], r_sg[i2]], [r_actT[hc][tb]])
              for ps_ in range(2):
                  for pc in range(11):
                      wi = dq.pop(0)
                      nxt_pc = pc + 2
                      if nxt_pc < 11:
                          dq.append(load_dn(nxt_pc))
                      elif ps_ == 0:
                          dq.append(load_dn(nxt_pc - 11))
                      for hh in range(2):
                          hc = 2 * pc + hh
                          for tl4 in range(4):
                              tl = ps_ * 4 + tl4
                              for hf in range(2):
                                  bk = tl4 * 2 + hf
                                  last = (hh == 1 and tl4 == 3 and hf == 1)
                                  mm(PB[bk], actT[:, hc, tl * 128:(tl + 1) * 128], wdp[wi][:, hh, hf * 512:(hf + 1) * 512],
                                     hc == 0, hc == NHC - 1, [r_actT[hc][tl // 4], r_wdp[wi]], [RB[bk]], sig=last)
                  for tl4 in range(4):
                      tl = ps_ * 4 + tl4
                      t = tiles[tl]
                      for hf in range(2):
                          bk = tl4 * 2 + hf; i2 = hf
                          tt("dve", tmpy[i2], PB[bk], gates[:, 1, hf * 512:(hf + 1) * 512], ALU.mult, [RB[bk], r_gates[1]], [r_tmpy[i2]])
                          tt("pool", X[:, t, hf * 512:(hf + 1) * 512], X[:, t, hf * 512:(hf + 1) * 512], tmpy[i2], ALU.add,
                             [RX[t], r_tmpy[i2]], [RX[t]])
                      if l == L - 1:
                          S.dma("sp", y_d.rearrange("(t p) d -> p t d", p=128)[:, t, :], X[:, t, :], reads=[RX[t]])
          dump("xout%d" % l, X, RX)
          chk("ffn%d" % l)
    except _Stop:
        pass

    S.finish("sp")
    print("instructions", S.ninstr, "waits", S.nwaits, "cnt", S.cnt)
    return nc, dbg_outs


def _tables(is_sample):
    cos = np.ones((NK, 16), np.float32); sin = np.zeros((NK, 16), np.float32)
    if is_sample:
        t = np.arange(T)
        row = (t // GRID_W).astype(np.float32); col = (t % GRID_W).astype(np.float32)
        half = ROPE // 2
        freqs = np.power(np.float32(10000.0), -np.arange(0, half, 2, dtype=np.float32) / half).astype(np.float32)
        ar = row[:, None] * freqs[None]; ac = col[:, None] * freqs[None]
        cos[:T, 0:8] = np.cos(ar); cos[:T, 8:16] = np.cos(ac)
        sin[:T, 0:8] = np.sin(ar); sin[:T, 8:16] = np.sin(ac)
    qa = np.zeros((32, T), np.float32); ka = np.zeros((32, NK), np.float32)
    if is_sample:
        ka[0, :] = 1.0
    else:
        gq = np.arange(T) // 256
        for j in range(8):
            ka[j, :T] = (gq == j)
            qa[j, :] = np.where(gq == j, 0.0, -BIG)
        ka[8, T:] = 1.0
        qa[8, :] = -BIG
    seqlen = T if is_sample else 256
    t = np.arange(T)
    mL = (t % seqlen != 0).astype(np.float32); mR = (t % seqlen != seqlen - 1).astype(np.float32)
    keepf = np.ones(NT, np.float32); keepb = np.ones(NT, np.float32)
    if not is_sample:
        keepf[0::2] = 0.0
        keepb[1::2] = 0.0
    keepf[0] = 1.0; keepb[NT - 1] = 1.0
    nbL = (mL[0::256] - 1.0).astype(np.float32)
    nbR = (mR[255::256] - 1.0).astype(np.float32)
    return dict(cos=cos, sin=sin, qa=qa, ka=ka, nbL=nbL, nbR=nbR, keepf=keepf, keepb=keepb)


def _cst():
    c = np.zeros((128, 3, 128), np.float32)
    k = np.arange(128, dtype=np.float32)[:, None]; q = np.arange(128, dtype=np.float32)[None, :]
    c[:, 0, :] = q - k
    c[:, 1, :] = q + 1.0
    c[:, 2, :] = 128.0 - q
    return c


_WNAMES = ["ada_w", "ada_b", "w_in", "q_norm_g", "kv_norm_g", "w_q_up", "w_kv_up",
           "q_head_norm_g", "k_head_norm_g", "ret_decay_fwd", "ret_decay_bwd", "w_o",
           "w_ffn_gate", "w_ffn_up", "w_ffn_down"]

_PROGRAM = {}


def _run(inputs, dbg=None, stop=None):
    f32 = lambda a: np.ascontiguousarray(np.asarray(a, dtype=np.float32))
    key = (tuple(sorted(dbg)) if dbg else (), stop)
    if key not in _PROGRAM:
        _PROGRAM[key] = build_program(dbg, stop)
    nc, dbg_outs = _PROGRAM[key]
    W = {k: f32(inputs[k]) for k in _WNAMES}
    W["ada_b_pc"] = f32(inputs["ada_b"]).reshape(L, 48, 128).transpose(0, 2, 1)
    W["n1g_pc"] = f32(inputs["norm1_g"]).reshape(L, 8, 128).transpose(0, 2, 1)
    W["n2g_pc"] = f32(inputs["norm2_g"]).reshape(L, 8, 128).transpose(0, 2, 1)
    W["cw_pc"] = f32(inputs["conv_w"]).reshape(L, 3, 2, 128).transpose(0, 3, 2, 1)
    xs = f32(inputs["x_sample"]); xp = f32(inputs["x_prompt"])
    c = f32(inputs["c"]); c_ctx = f32(inputs["c_ctx"])
    cckv = f32(inputs["cache_ckv"]); ckpe = f32(inputs["cache_kpe"]); st = f32(inputs["state_ret"])
    ts_, tp_ = _tables(True), _tables(False)
    cst = _cst()
    in_maps = []
    for i in range(NCORES):
        if i < 4:
            m = dict(x=xs[i], cond_pc=c[i].reshape(8, 128).T, cckv=cckv[i], ckpe=ckpe[i], s0=st[i], **ts_)
        else:
            j = i - 4
            m = dict(x=xp[8 * j:8 * j + 8].reshape(T, D), cond_pc=c_ctx.reshape(8, 128).T,
                     cckv=np.zeros((L, NCTX, KVL), np.float32), ckpe=np.zeros((L, NCTX, ROPE), np.float32),
                     s0=np.zeros((L, 2, RH, 64, 64), np.float32), **tp_)
        m["cst"] = cst
        m.update(W)
        in_maps.append({k: np.ascontiguousarray(v) for k, v in m.items()})
    res = run_bass_kernel_spmd(nc, in_maps, core_ids=list(range(NCORES)))
    return res.results


def kernel(**inputs):
    r = _run(inputs)
    y_sample = np.stack([r[i]["y"] for i in range(4)], 0)
    y_prompt = np.concatenate([r[4 + j]["y"].reshape(8, 256, D) for j in range(4)], 0)
    new_ckv = np.concatenate([r[4 + j]["ockv"].reshape(L, 8, 256, KVL).transpose(1, 0, 2, 3) for j in range(4)], 0)
    new_kpe = np.concatenate([r[4 + j]["okpe"].reshape(L, 8, 256, ROPE).transpose(1, 0, 2, 3) for j in range(4)], 0)
    new_st = np.concatenate([r[4 + j]["ost"].transpose(2, 0, 1, 3, 4, 5) for j in range(4)], 0)
    return (y_prompt.astype(np.float32), y_sample.astype(np.float32), np.ascontiguousarray(new_ckv, dtype=np.float32),
            np.ascontiguousarray(new_kpe, dtype=np.float32), np.ascontiguousarray(new_st, dtype=np.float32))
```
